# Optimizing a Trainium2 kernel written in Bass

```python
import math
import jax, jax.numpy as jnp
from jax import lax
import numpy as np

D_MODEL = 2048
BATCH = 16
SEQ = 2048
DEPTH = 1
DEC_BATCH = 32
DEC_SEQ = 8
PAST_LEN = 16384
PAGE_SIZE = 128

MIX_WIDTH = D_MODEL
RET_HEADS = 4
RET_DK = MIX_WIDTH // (4 * RET_HEADS)
RET_DV = 2 * RET_DK
RET_CHUNK = 128
NSA_HEADS = 8
NSA_KV_HEADS = 2
NSA_HD = MIX_WIDTH // (2 * NSA_HEADS)
NSA_GROUP = NSA_HEADS // NSA_KV_HEADS
CMP_BLOCK = 32
CMP_STRIDE = 16
CMP_HIDDEN = 2 * NSA_HD
SLC_BLOCK = 64
N_SELECT = 16
WINDOW = 512
NSA_QBLOCK = 32
N_BRANCH = 3
D_FF = ((8 * D_MODEL // 3 + 255) // 256) * 256
ROPE_THETA = 10000.0
LN_EPS = 1e-5
ALPHA = (2.0 * DEPTH) ** 0.25
BETA = (8.0 * DEPTH) ** -0.25

RET_Q = RET_HEADS * RET_DK
RET_V = RET_HEADS * RET_DV
NSA_Q = NSA_HEADS * NSA_HD
NSA_KV = NSA_KV_HEADS * NSA_HD

kernel_name = 'hymba_retnet_nsa_macaron_decoder'

F32 = jnp.float32


def split_sizes():
    return [RET_Q, RET_Q, RET_V, RET_V, NSA_Q, 6 * NSA_KV, N_BRANCH * NSA_HEADS]


def layer_norm(x, g, b):
    xf = x.astype(F32)
    mu = xf.mean(-1, keepdims=True)
    var = jnp.square(xf - mu).mean(-1, keepdims=True)
    return ((xf - mu) * lax.rsqrt(var + LN_EPS) * g + b).astype(x.dtype)


def swiglu(x, w_up, w_down):
    a, b = jnp.split(x @ w_up, 2, axis=-1)
    return (jax.nn.silu(a) * b) @ w_down


def rope(x, pos):
    half = x.shape[-1] // 2
    inv = ROPE_THETA ** (-jnp.arange(half, dtype=F32) / half)
    ang = pos.astype(F32)[:, None] * inv[None, :]
    cos = jnp.cos(ang)[None, :, None, :]
    sin = jnp.sin(ang)[None, :, None, :]
    xf = x.astype(F32)
    x1, x2 = xf[..., :half], xf[..., half:]
    return jnp.concatenate([x1 * cos - x2 * sin, x1 * sin + x2 * cos], -1).astype(x.dtype)


def masked_softmax(s, mask):
    s = jnp.where(mask, s, -jnp.inf)
    m = jnp.max(s, -1, keepdims=True)
    m = jnp.where(jnp.isfinite(m), m, 0.0)
    e = jnp.exp(s - m)
    d = e.sum(-1, keepdims=True)
    return e / jnp.where(d > 0, d, 1.0)


def retention(q, k, v, s0):
    B, L, H, DK = q.shape
    DV = v.shape[-1]
    C = RET_CHUNK if L % RET_CHUNK == 0 else L
    nc = L // C
    lg = jnp.log1p(-jnp.exp2(-5.0 - jnp.arange(H, dtype=F32)))
    i = jnp.arange(C, dtype=F32)
    diff = i[:, None] - i[None, :]
    dmask = jnp.where(diff >= 0, jnp.exp(lg[:, None, None] * jnp.maximum(diff, 0.0)), 0.0)
    in_decay = jnp.exp(lg[:, None] * (i + 1.0))
    st_decay = jnp.exp(lg[:, None] * (C - 1.0 - i))
    chunk_decay = jnp.exp(lg * C)

    def to_chunks(t):
        return t.astype(F32).reshape(B, nc, C, H, -1).transpose(1, 0, 3, 2, 4)

    def step(S, inp):
        qc, kc, vc = inp
        a = jnp.einsum('bhid,bhjd->bhij', qc, kc) * dmask
        o = (jnp.einsum('bhij,bhje->bhie', a, vc)
             + jnp.einsum('bhid,bhde->bhie', qc * in_decay[:, :, None], S))
        S = S * chunk_decay[:, None, None] + jnp.einsum('bhjd,bhje->bhde', kc * st_decay[:, :, None], vc)
        return S, o

    S, o = lax.scan(step, s0.astype(F32), (to_chunks(q), to_chunks(k), to_chunks(v)))
    return o.transpose(1, 0, 3, 2, 4).reshape(B, L, H, DV), S


def compress(x, pos_emb, w1, w2):
    B, Lk = x.shape[:2]
    R = CMP_BLOCK // CMP_STRIDE
    n_chunks = Lk // CMP_STRIDE
    n_cmp = n_chunks - R + 1
    chunks = x[:, :n_chunks * CMP_STRIDE].astype(F32).reshape(B, n_chunks, CMP_STRIDE, NSA_KV_HEADS, NSA_HD)
    w1f = w1.astype(F32)
    w1r = w1f.reshape(R, CMP_STRIDE, NSA_HD, CMP_HIDDEN)
    h = jnp.einsum('pd,pdf->f', pos_emb.astype(F32), w1f)
    for r in range(R):
        h = h + jnp.einsum('bnskd,sdf->bnkf', chunks, w1r[r])[:, r:r + n_cmp]
    return jnp.einsum('bnkf,fd->bnkd', jax.nn.gelu(h), w2.astype(F32))


def nsa(q, k_cmp, v_cmp, k_slc, v_slc, k_win_src, v_win_src, gates, q_off, p):
    B, Lq, H, HD = q.shape
    Lk = k_slc.shape[1]
    scale = HD ** -0.5
    kc = compress(k_cmp, p['cmp_pos_k'], p['cmp_w1_k'], p['cmp_w2_k'])
    vc = compress(v_cmp, p['cmp_pos_v'], p['cmp_w1_v'], p['cmp_w2_v'])
    n_cmp = kc.shape[1]
    cmp_end = jnp.arange(n_cmp) * CMP_STRIDE + CMP_BLOCK - 1
    n_slc = -(-Lk // SLC_BLOCK)
    pad = n_slc * SLC_BLOCK - Lk

    def blocks(t):
        t = jnp.pad(t, ((0, 0), (0, pad), (0, 0), (0, 0)))
        return t.reshape(B, n_slc, SLC_BLOCK, NSA_KV_HEADS, HD).transpose(0, 3, 1, 2, 4)

    kb, vb = blocks(k_slc), blocks(v_slc)
    c_i = jnp.arange(n_cmp)[:, None]
    n_i = jnp.arange(n_slc)[None, :]
    cover = jnp.clip(jnp.minimum(c_i * CMP_STRIDE + CMP_BLOCK, (n_i + 1) * SLC_BLOCK)
                     - jnp.maximum(c_i * CMP_STRIDE, n_i * SLC_BLOCK), 0, None).astype(F32) / CMP_BLOCK
    n_sel = min(N_SELECT, n_slc)
    QB = NSA_QBLOCK if Lq % NSA_QBLOCK == 0 else Lq
    WL = WINDOW + QB - 1
    bi = jnp.arange(B)[:, None, None, None]
    hi = jnp.arange(NSA_KV_HEADS)[None, None, :, None]
    blk = jnp.arange(n_slc)

    def one_block(qs):
        qq = lax.dynamic_slice_in_dim(q, qs, QB, 1).astype(F32).reshape(B, QB, NSA_KV_HEADS, NSA_GROUP, HD)
        t = q_off + qs + jnp.arange(QB)
        s = jnp.einsum('bqkgd,bckd->bqkgc', qq, kc) * scale
        p_cmp = masked_softmax(s, (cmp_end[None, :] <= t[:, None])[None, :, None, None, :])
        o_cmp = jnp.einsum('bqkgc,bckd->bqkgd', p_cmp, vc)
        imp = jnp.einsum('bqkgc,cn->bqkn', p_cmp, cover)
        valid = blk[None, :] * SLC_BLOCK <= t[:, None]
        cur = t[:, None] // SLC_BLOCK
        forced = (blk[None, :] == 0) | (blk[None, :] == cur) | (blk[None, :] == cur - 1)
        prio = jnp.where(forced[None, :, None, :], jnp.inf,
                         jnp.where(valid[None, :, None, :], imp, -jnp.inf))
        _, idx = lax.top_k(prio, n_sel)
        sel_ok = jnp.take_along_axis(jnp.broadcast_to(valid[None, :, None, :], prio.shape), idx, -1)
        ks = kb[bi, hi, idx].astype(F32)
        vs = vb[bi, hi, idx].astype(F32)
        pos = idx[..., None] * SLC_BLOCK + jnp.arange(SLC_BLOCK)
        msel = (sel_ok[..., None] & (pos <= t[None, :, None, None, None])).reshape(B, QB, NSA_KV_HEADS, 1, n_sel * SLC_BLOCK)
        s = jnp.einsum('bqkgd,bqknrd->bqkgnr', qq, ks).reshape(B, QB, NSA_KV_HEADS, NSA_GROUP, n_sel * SLC_BLOCK) * scale
        p_slc = masked_softmax(s, msel)
        o_slc = jnp.einsum('bqkgm,bqkmd->bqkgd', p_slc, vs.reshape(B, QB, NSA_KV_HEADS, n_sel * SLC_BLOCK, HD))
        kw = lax.dynamic_slice_in_dim(k_win_src, qs + 1, WL, 1).astype(F32)
        vw = lax.dynamic_slice_in_dim(v_win_src, qs + 1, WL, 1).astype(F32)
        sp = q_off - WINDOW + qs + 1 + jnp.arange(WL)
        dlt = t[:, None] - sp[None, :]
        mw = ((sp[None, :] >= 0) & (dlt >= 0) & (dlt < WINDOW))[None, :, None, None, :]
        s = jnp.einsum('bqkgd,bskd->bqkgs', qq, kw) * scale
        o_win = jnp.einsum('bqkgs,bskd->bqkgd', masked_softmax(s, mw), vw)
        g = lax.dynamic_slice_in_dim(gates, qs, QB, 1).reshape(B, QB, NSA_KV_HEADS, NSA_GROUP, N_BRANCH)
        o = g[..., 0:1] * o_cmp + g[..., 1:2] * o_slc + g[..., 2:3] * o_win
        return o.reshape(B, QB, H, HD)

    out = lax.map(one_block, jnp.arange(Lq // QB) * QB)
    return out.transpose(1, 0, 2, 3, 4).reshape(B, Lq, H, HD).astype(q.dtype)


def mixer(h, q_off, ret_s0, past_kv, win_past, p):
    B, L, _ = h.shape
    cuts = [int(c) for c in np.cumsum(split_sizes())[:-1]]
    rq, rk, rv, rg, nq, nkv, ng = jnp.split(h @ p['w_in'], cuts, axis=-1)
    pos = q_off + jnp.arange(L, dtype=jnp.int32)
    rq = rope(rq.reshape(B, L, RET_HEADS, RET_DK), pos)
    rk = rope(rk.reshape(B, L, RET_HEADS, RET_DK), pos) * (RET_DK ** -0.5)
    rv = rv.reshape(B, L, RET_HEADS, RET_DV)
    ro, ret_s = retention(rq, rk, rv, ret_s0)
    mu = ro.mean(-1, keepdims=True)
    var = jnp.square(ro - mu).mean(-1, keepdims=True)
    ro = ((ro - mu) * lax.rsqrt(var + LN_EPS)).reshape(B, L, RET_V) * p['ret_gn_g'] + p['ret_gn_b']
    ro = jax.nn.silu(rg.astype(F32)) * ro
    nq = rope(nq.reshape(B, L, NSA_HEADS, NSA_HD), pos)
    nkv = nkv.reshape(B, L, 6, NSA_KV_HEADS, NSA_HD)
    kv_rows = jnp.stack([rope(nkv[:, :, 0], pos), nkv[:, :, 1], rope(nkv[:, :, 2], pos), nkv[:, :, 3]], axis=2)
    win_rows = jnp.stack([rope(nkv[:, :, 4], pos), nkv[:, :, 5]], axis=2)
    full = kv_rows if past_kv is None else jnp.concatenate([past_kv, kv_rows], axis=1)
    wsrc = jnp.concatenate([win_past, win_rows], axis=1)
    gates = jax.nn.sigmoid(ng.astype(F32)).reshape(B, L, NSA_HEADS, N_BRANCH)
    no = nsa(nq, full[:, :, 0], full[:, :, 1], full[:, :, 2], full[:, :, 3],
             wsrc[:, :, 0], wsrc[:, :, 1], gates, q_off, p)
    mixed = jnp.concatenate([ro.astype(h.dtype), no.reshape(B, L, NSA_Q)], axis=-1)
    return mixed @ p['w_out'], ret_s, kv_rows, win_rows


def decoder_layer(x, q_off, ret_s0, past_kv, win_past, p):
    x = layer_norm(ALPHA * x + 0.5 * swiglu(x, p['ffn1_w_up'], p['ffn1_w_down']), p['ln1_g'], p['ln1_b'])
    m, ret_s, kv_rows, win_rows = mixer(x, q_off, ret_s0, past_kv, win_past, p)
    x = layer_norm(ALPHA * x + m, p['ln2_g'], p['ln2_b'])
    x = layer_norm(ALPHA * x + 0.5 * swiglu(x, p['ffn2_w_up'], p['ffn2_w_down']), p['ln3_g'], p['ln3_b'])
    return x, ret_s, kv_rows, win_rows


def setup_inputs(seed: int = 0) -> dict:
    key = jax.random.key(seed)
    ks = iter(jax.random.split(key, 40))

    def nrm(shape, scale):
        return jax.random.normal(next(ks), shape, F32) * scale

    n_pages = PAST_LEN // PAGE_SIZE
    n_pool = (5 * DEC_BATCH * n_pages + 3) // 4
    w_buf = min(WINDOW, PAST_LEN)
    n_in = sum(split_sizes())
    page_table = jax.random.permutation(next(ks), n_pool)[:DEC_BATCH * n_pages].reshape(DEC_BATCH, n_pages).astype(jnp.int32)
    return {
        'x_prompt': nrm((BATCH, SEQ, D_MODEL), 1.0),
        'x_sample': nrm((DEC_BATCH, DEC_SEQ, D_MODEL), 1.0),
        'state_ret': nrm((DEPTH, DEC_BATCH, RET_HEADS, RET_DK, RET_DV), 0.5),
        'cache_nsa_kv': nrm((DEPTH, n_pool, PAGE_SIZE, 4, NSA_KV_HEADS, NSA_HD), 1.0),
        'cache_win': nrm((DEPTH, DEC_BATCH, w_buf, 2, NSA_KV_HEADS, NSA_HD), 1.0),
        'page_table': page_table,
        'ffn1_w_up': nrm((DEPTH, D_MODEL, 2 * D_FF), D_MODEL ** -0.5),
        'ffn1_w_down': nrm((DEPTH, D_FF, D_MODEL), BETA * D_FF ** -0.5),
        'ln1_g': 1.0 + nrm((DEPTH, D_MODEL), 0.02),
        'ln1_b': nrm((DEPTH, D_MODEL), 0.02),
        'w_in': nrm((DEPTH, D_MODEL, n_in), D_MODEL ** -0.5),
        'w_out': nrm((DEPTH, MIX_WIDTH, D_MODEL), BETA * MIX_WIDTH ** -0.5),
        'ret_gn_g': 1.0 + nrm((DEPTH, RET_V), 0.02),
        'ret_gn_b': nrm((DEPTH, RET_V), 0.02),
        'cmp_pos_k': nrm((DEPTH, CMP_BLOCK, NSA_HD), 0.1),
        'cmp_w1_k': nrm((DEPTH, CMP_BLOCK, NSA_HD, CMP_HIDDEN), (CMP_BLOCK * NSA_HD) ** -0.5),
        'cmp_w2_k': nrm((DEPTH, CMP_HIDDEN, NSA_HD), CMP_HIDDEN ** -0.5),
        'cmp_pos_v': nrm((DEPTH, CMP_BLOCK, NSA_HD), 0.1),
        'cmp_w1_v': nrm((DEPTH, CMP_BLOCK, NSA_HD, CMP_HIDDEN), (CMP_BLOCK * NSA_HD) ** -0.5),
        'cmp_w2_v': nrm((DEPTH, CMP_HIDDEN, NSA_HD), CMP_HIDDEN ** -0.5),
        'ln2_g': 1.0 + nrm((DEPTH, D_MODEL), 0.02),
        'ln2_b': nrm((DEPTH, D_MODEL), 0.02),
        'ffn2_w_up': nrm((DEPTH, D_MODEL, 2 * D_FF), D_MODEL ** -0.5),
        'ffn2_w_down': nrm((DEPTH, D_FF, D_MODEL), BETA * D_FF ** -0.5),
        'ln3_g': 1.0 + nrm((DEPTH, D_MODEL), 0.02),
        'ln3_b': nrm((DEPTH, D_MODEL), 0.02),
    }


def reference(x_prompt, x_sample, state_ret, cache_nsa_kv, cache_win, page_table,
              ffn1_w_up, ffn1_w_down, ln1_g, ln1_b, w_in, w_out, ret_gn_g, ret_gn_b,
              cmp_pos_k, cmp_w1_k, cmp_w2_k, cmp_pos_v, cmp_w1_v, cmp_w2_v,
              ln2_g, ln2_b, ffn2_w_up, ffn2_w_down, ln3_g, ln3_b):
    B, L_p, _ = x_prompt.shape
    DB = x_sample.shape[0]
    n_pages = page_table.shape[1]
    past_len = n_pages * cache_nsa_kv.shape[2]
    w_buf = cache_win.shape[2]
    yp, ys = x_prompt, x_sample
    rs_p_l, rs_s_l, kv_p_l, kv_s_l, win_p_l, win_s_l = [], [], [], [], [], []
    for l in range(DEPTH):
        p = {
            'ffn1_w_up': ffn1_w_up[l], 'ffn1_w_down': ffn1_w_down[l], 'ln1_g': ln1_g[l], 'ln1_b': ln1_b[l],
            'w_in': w_in[l], 'w_out': w_out[l], 'ret_gn_g': ret_gn_g[l], 'ret_gn_b': ret_gn_b[l],
            'cmp_pos_k': cmp_pos_k[l], 'cmp_w1_k': cmp_w1_k[l], 'cmp_w2_k': cmp_w2_k[l],
            'cmp_pos_v': cmp_pos_v[l], 'cmp_w1_v': cmp_w1_v[l], 'cmp_w2_v': cmp_w2_v[l],
            'ln2_g': ln2_g[l], 'ln2_b': ln2_b[l], 'ffn2_w_up': ffn2_w_up[l], 'ffn2_w_down': ffn2_w_down[l],
            'ln3_g': ln3_g[l], 'ln3_b': ln3_b[l],
        }
        s0 = jnp.zeros((B, RET_HEADS, RET_DK, RET_DV), F32)
        win0 = jnp.zeros((B, WINDOW, 2, NSA_KV_HEADS, NSA_HD), x_prompt.dtype)
        yp, rs_p, kv_p, win_p = decoder_layer(yp, 0, s0, None, win0, p)
        past = cache_nsa_kv[l][page_table].reshape(DB, past_len, 4, NSA_KV_HEADS, NSA_HD)
        win_past = jnp.pad(cache_win[l], ((0, 0), (WINDOW - w_buf, 0), (0, 0), (0, 0), (0, 0)))
        ys, rs_s, kv_s, win_s = decoder_layer(ys, past_len, state_ret[l], past, win_past, p)
        rs_p_l.append(rs_p)
        rs_s_l.append(rs_s)
        kv_p_l.append(kv_p)
        kv_s_l.append(kv_s)
        win_p_l.append(win_p[:, L_p - min(WINDOW, L_p):])
        win_s_l.append(jnp.concatenate([cache_win[l], win_s], axis=1)[:, -w_buf:])
    ret_state_prompt = jnp.stack(rs_p_l)
    ret_state_sample = jnp.stack(rs_s_l)
    nsa_kv_prompt = jnp.stack(kv_p_l)
    nsa_kv_sample = jnp.stack(kv_s_l)
    win_kv_prompt = jnp.stack(win_p_l)
    win_kv_sample = jnp.stack(win_s_l)
    return (yp, ys, ret_state_prompt, ret_state_sample, nsa_kv_prompt, nsa_kv_sample, win_kv_prompt, win_kv_sample)
```

```python
import math
import numpy as np
import concourse.bass as bass
import concourse.mybir as mybir
from concourse.bass_utils import run_bass_kernel_spmd

F32 = mybir.dt.float32
BF16 = mybir.dt.bfloat16
I32 = mybir.dt.int32
AF = mybir.ActivationFunctionType
ALU = mybir.AluOpType
AX = mybir.AxisListType

D = 2048
KC = 16
DFF = 5632
FC = 44
SEQ = 2048
NB_P = 16
NB_S = 32
DEC = 8
PAST = 16384
PAGE = 128
RH, RDK, RDV = 4, 128, 256
NH, NKV, HD = 8, 2, 128
NIN = 5656
ALPHA = 2.0 ** 0.25
EPS = 1e-5
WINDOW = 512
N_CORES = 8

_ENG_ATTR = {'pe': 'tensor', 'dve': 'vector', 'act': 'scalar', 'pool': 'gpsimd', 'sp': 'sync'}
_DMA_METHODS = ('dma_start', 'indirect_dma_start', 'dma_start_transpose')
SEM_ROTATE = 30000


def _keys_of(x):
    if x is None or isinstance(x, (int, float, str, bool)):
        return []
    nm = getattr(x, 'name', None)
    if isinstance(nm, str) and hasattr(x, 'ap'):
        return [nm]
    return []


class _Op:
    __slots__ = ('eng', 'meth', 'args', 'kw', 'R', 'W', 'is_dma', 'deps', 'signal', 'sem', 'val', 'idx', 'fn')


class _Proxy:
    def __init__(self, sched, eng):
        self._s = sched
        self._e = eng

    def __getattr__(self, meth):
        def rec(*args, R=None, W=None, **kw):
            return self._s._record(self._e, meth, args, kw, R, W)
        return rec


class Sched:
    def __init__(self, nc, same_engine_sync=True, n_dma_sems=24):
        self.nc = nc
        self.ops = []
        self.same_engine_sync = same_engine_sync
        self.n_dma_sems = n_dma_sems
        self.dram_names = set()
        for e in _ENG_ATTR:
            setattr(self, e, _Proxy(self, e))
        self.eng_obj = {e: getattr(nc, a) for e, a in _ENG_ATTR.items()}
        self.eng_sem = {}
        self.eng_cnt = {}
        self.sem_id = 0
        self.dma_sems = {}
        self.dma_rr = {}
        self.known = {e: {} for e in _ENG_ATTR}
        self.sems = {}
        self.pending_barrier = {}
        self.total_ops = 0

    def _new_sem(self, tag):
        self.sem_id += 1
        s = self.nc.alloc_semaphore(f"s_{tag}_{self.sem_id}")
        self.sems[id(s)] = s
        return s

    def _record(self, eng, meth, args, kw, R, W):
        op = _Op()
        op.eng, op.meth, op.args, op.kw = eng, meth, args, kw
        op.is_dma = meth in _DMA_METHODS
        if W is None:
            W = []
            if 'out' in kw:
                W += _keys_of(kw['out'])
            elif args:
                W += _keys_of(args[0])
            if kw.get('accum_out') is not None:
                W += _keys_of(kw['accum_out'])
        if R is None:
            R = []
            first_is_out = 'out' not in kw
            for i, a in enumerate(args):
                if i == 0 and first_is_out:
                    continue
                R += _keys_of(a)
            for k, v in kw.items():
                if k in ('out', 'accum_out'):
                    continue
                R += _keys_of(v)
        op.R = [k for k in R if k not in self.dram_names]
        op.W = [k for k in W if k not in self.dram_names]
        op.idx = len(self.ops)
        self.ops.append(op)
        return op

    def dma(self, eng, out, in_, max_desc=512, nosplit=False, R=None, W=None):
        shp = tuple(out.shape)
        if (not nosplit) and len(shp) == 3 and shp[0] * shp[1] > max_desc and tuple(in_.shape) == shp:
            step = max(1, max_desc // shp[0])
            for a0 in range(0, shp[1], step):
                a1 = min(shp[1], a0 + step)
                getattr(self, eng).dma_start(out=out[:, a0:a1, :], in_=in_[:, a0:a1, :], R=R, W=W)
        else:
            getattr(self, eng).dma_start(out=out, in_=in_, R=R, W=W)

    def custom(self, eng, fn, R, W, is_dma=True):
        op = _Op()
        op.eng, op.meth, op.args, op.kw, op.fn = eng, None, (), {}, fn
        op.is_dma = is_dma
        op.R = [k for k in R if k not in self.dram_names]
        op.W = [k for k in W if k not in self.dram_names]
        op.idx = len(self.ops)
        self.ops.append(op)
        return op

    def _wait(self, eng, sem, val):
        kn = self.known[eng]
        key = id(sem)
        if kn.get(key, 0) >= val:
            return
        self.eng_obj[eng].wait_ge(sem, val)
        kn[key] = val

    def flush(self):
        ops = self.ops
        self.ops = []
        if not ops:
            return 0
        last_w = {}
        readers = {}
        for op in ops:
            deps = set()
            for k in op.R:
                if k in last_w:
                    deps.add(last_w[k])
            for k in op.W:
                if k in last_w:
                    deps.add(last_w[k])
                for r in readers.get(k, ()):
                    deps.add(r)
            deps.discard(op.idx)
            fd = []
            for d in deps:
                dop = ops[d]
                if dop.is_dma or op.is_dma or dop.eng != op.eng:
                    fd.append(d)
                elif self.same_engine_sync and op.eng != 'pe':
                    fd.append(d)
            op.deps = fd
            op.signal = op.is_dma
            for k in op.W:
                last_w[k] = op.idx
                readers[k] = []
            for k in op.R:
                if k not in op.W:
                    readers.setdefault(k, []).append(op.idx)
        for op in ops:
            for d in op.deps:
                ops[d].signal = True
        last_on = {}
        for op in ops:
            if not op.is_dma:
                last_on[op.eng] = op
        for op in last_on.values():
            op.signal = True
        first_done = set()
        for op in ops:
            e = op.eng
            if e not in first_done:
                first_done.add(e)
                for key, val in self.pending_barrier.items():
                    self._wait(e, self.sems[key], val)
            need = {}
            for d in op.deps:
                dop = ops[d]
                key = id(dop.sem)
                if need.get(key, (None, 0))[1] < dop.val:
                    need[key] = (dop.sem, dop.val)
            for sem, val in need.values():
                self._wait(e, sem, val)
            if op.is_dma:
                lst = self.dma_sems.setdefault(e, [])
                if len(lst) < self.n_dma_sems:
                    lst.append([self._new_sem('d' + e), 0])
                    slot = lst[-1]
                else:
                    i = self.dma_rr.get(e, 0)
                    slot = lst[i]
                    self.dma_rr[e] = (i + 1) % len(lst)
                    self._wait(e, slot[0], slot[1])
                    if slot[1] >= SEM_ROTATE:
                        slot[0] = self._new_sem('d' + e)
                        slot[1] = 0
                inst = (op.fn(self.eng_obj[e]) if op.meth is None else getattr(self.eng_obj[e], op.meth)(*op.args, **op.kw))
                slot[1] += 16
                inst.then_inc(slot[0], 16)
                op.sem, op.val = slot[0], slot[1]
            else:
                inst = (op.fn(self.eng_obj[e]) if op.meth is None else getattr(self.eng_obj[e], op.meth)(*op.args, **op.kw))
                if op.signal:
                    if e not in self.eng_sem or self.eng_cnt[e] >= SEM_ROTATE:
                        self.eng_sem[e] = self._new_sem(e)
                        self.eng_cnt[e] = 0
                    self.eng_cnt[e] += 1
                    inst.then_inc(self.eng_sem[e], 1)
                    op.sem, op.val = self.eng_sem[e], self.eng_cnt[e]
        pb = {}
        for e, s in self.eng_sem.items():
            pb[id(s)] = self.eng_cnt[e]
        for e, lst in self.dma_sems.items():
            for slot in lst:
                if slot[1] > 0:
                    pb[id(slot[0])] = slot[1]
        self.pending_barrier = pb
        self.total_ops += len(ops)
        return len(ops)

    def finish(self):
        self.flush()
        for key, val in self.pending_barrier.items():
            self._wait('sp', self.sems[key], val)
        return self.total_ops


from contextlib import ExitStack

SCALE = HD ** -0.5
LG = [math.log1p(-2.0 ** (-5.0 - h)) for h in range(RH)]


class Cfg:
    def __init__(self, n_pseq=2, seq=SEQ, n_sseq=4, do_mixer=True, do_sample=True, npool=5120):
        self.n_pseq = n_pseq
        self.seq = seq
        self.n_sseq = n_sseq
        self.ntp = n_pseq * seq
        self.nts = n_sseq * DEC
        self.nt = self.ntp + self.nts
        self.do_mixer = do_mixer
        self.do_sample = do_sample
        self.tiles = [(i * 512, 512) for i in range(self.ntp // 512)]
        if self.nts:
            self.tiles.append((self.ntp, self.nts))
        self.nqt = seq // 128
        self.ncmp = seq // 16 - 1
        self.nslc = seq // 64
        self.npool = npool


def build_program(cfg):
    nc = bass.Bass("TRN2", target_bir_lowering=False)
    NT = cfg.nt
    import os as _osx
    S = Sched(nc, same_engine_sync=bool(int(_osx.environ.get("MK_SES", "1"))))
    BGCAST = bool(int(_osx.environ.get("MK_BGCAST", "1")))
    seq, nqt, NCMP, NSLC = cfg.seq, cfg.nqt, cfg.ncmp, cfg.nslc

    def din(name, shape, dt=F32):
        t = nc.dram_tensor(name, list(shape), dt, kind="ExternalInput")
        S.dram_names.add(name)
        return t.ap()

    def dout(name, shape, dt=F32):
        t = nc.dram_tensor(name, list(shape), dt, kind="ExternalOutput")
        S.dram_names.add(name)
        return t.ap()

    def dscr(name, shape, dt):
        t = nc.dram_tensor(name, list(shape), dt)
        S.dram_names.add(name)
        return t.ap()

    xin = din("xin", [NT, D])
    w_up = [din("ffn1_w_up", [D, 2 * DFF]), din("ffn2_w_up", [D, 2 * DFF])]
    w_dn = [din("ffn1_w_down", [DFF, D]), din("ffn2_w_down", [DFF, D])]
    w_in = din("w_in", [D, NIN])
    w_out = din("w_out", [D, D])
    lnp_d = din("lnp", [128, 6, KC])
    ident_d = din("ident", [128, 128])
    rope_d = din("rope", [NT, 128])
    gn_d = din("gn", [128, 2, RH * RDV])
    rc128_d = din("rc128", [128, 3, RH, 128])
    rc8_d = din("rc8", [128, 3, RH, 128])
    w1_d = [din("cmp_w1_k", [32, HD, 256]), din("cmp_w1_v", [32, HD, 256])]
    w2_d = [din("cmp_w2_k", [256, HD]), din("cmp_w2_v", [256, HD])]
    posT_d = din("cmp_posT", [128, 2, 32])
    mcmp_d = din("mcmp", [128, seq])
    cover_d = din("cover", [128, 32])
    mtri_d = din("mtri", [128, 2, 128])
    sel_d = din("selc", [128, nqt, 3, 32])
    E_d = din("Eexp", [32, nqt, 128])
    state_d = din("state", [max(cfg.n_sseq, 1), RH, RDK, RDV])
    cache_d = din("cache", [cfg.npool, PAGE, 4, NKV, HD])
    ptab_d = din("ptab", [max(cfg.n_sseq, 1), PAST // PAGE], I32)
    covs_d = din("cover_s", [128, 8, 257])
    selcs_d = din("selc_s", [16, 3, 257])
    Eall_d = din("E_all", [128, 64, 128])
    msm_d = din("msmall", [128, 16])
    r8_d = din("r8", [8, 32])
    gsel_d = din("gsel", [32, 2, 24])
    s16_d = din("sel16", [32, 2, 16])
    cwin_d = din("cwin", [max(cfg.n_sseq, 1), WINDOW, 512])
    y_out = dout("y", [NT, D])
    kv_out = dout("kvrows", [NT, 1024])
    winp_out = dout("winp", [max(cfg.n_pseq, 1), WINDOW, 512])
    wins_out = dout("wins", [max(cfg.n_sseq, 1), WINDOW, 512])
    rs_out = dout("rs", [cfg.n_pseq + cfg.n_sseq, RH, RDK, RDV])
    wup_b = [dscr("wup1b", [2 * DFF // 256, 128, KC, 256], BF16), dscr("wup2b", [2 * DFF // 256, 128, KC, 256], BF16)]
    wdn_b = [dscr("wdn1b", [KC, 128, FC, 128], BF16), dscr("wdn2b", [KC, 128, FC, 128], BF16)]
    win_b = dscr("winb", [12, 128, KC, 512], BF16)
    wout_b = dscr("woutb", [D // 256, 128, KC, 256], BF16)
    x1T = dscr("x1T", [128, KC, NT], F32)
    import os as _os0
    mixed = (dout("mixed", [NT, D], BF16) if _os0.environ.get("MK_DBGOUT") else dscr("mixed", [NT, D], BF16))
    rqT = dscr("rqT", [128, RH, NT], BF16)
    rkT = dscr("rkT", [128, RH, NT], BF16)
    rk_tok = dscr("rk_tok", [NT, RH * RDK], BF16)
    rv_tok = dscr("rv_tok", [NT, RH * RDV], BF16)
    rg_tok = dscr("rg_tok", [NT, RH * RDV], BF16)
    nqT = dscr("nqT", [128, NH, NT], BF16)
    kcmpT = dscr("kcmpT", [128, NKV, NT], BF16)
    vcmpT = dscr("vcmpT", [128, NKV, NT], BF16)
    kslcT = dscr("kslcT", [128, NKV, NT], BF16)
    kwinT = dscr("kwinT", [128, NKV, NT], BF16)
    vslc_tok = dscr("vslc_tok", [NT, NKV * HD], BF16)
    vwin_tok = dscr("vwin_tok", [NT, NKV * HD], BF16)
    gates_d = dscr("gates", [NT, 24], F32)

    def sb(name, shape, dt=F32):
        return nc.alloc_sbuf_tensor(name, list(shape), dt).ap()

    uniq = [0]

    def sbx(stack, name, shape, dt=F32):
        uniq[0] += 1
        return stack.enter_context(nc.sbuf_tensor(f"{name}_{uniq[0]}", list(shape), dt)).ap()

    def psx(stack, name, shape, dt=F32):
        uniq[0] += 1
        return stack.enter_context(nc.psum_tensor(f"{name}_{uniq[0]}", list(shape), dt)).ap()

    ident = sb("ident_sb", [128, 128])
    identb = sb("identb_sb", [128, 128], BF16)
    ones_f = sb("ones_f", [128, 128])
    ones_b = sb("ones_b", [128, 128], BF16)
    lnp = sb("lnp_sb", [128, 6, KC])
    lnpa = sb("lnpa_sb", [128, 6, KC])

    S.dma('sp', out=ident, in_=ident_d)
    S.dma('sp', out=lnp, in_=lnp_d)
    S.dve.tensor_copy(out=identb, in_=ident)
    S.dve.memset(ones_f, 1.0)
    S.dve.memset(ones_b, 1.0)
    S.act.mul(out=lnpa, in_=lnp, mul=ALPHA)
    CW = 1024

    def cast_items():
        items = []
        for li in range(2):
            for kc in range(KC):
                for c0 in range(0, 2 * DFF, CW):
                    n = min(CW, 2 * DFF - c0)
                    nb = n // 256
                    dst = wup_b[li][c0 // 256:c0 // 256 + nb, :, kc, :].rearrange("j p c -> p j c")
                    items.append((w_up[li][kc * 128:(kc + 1) * 128, c0:c0 + n], n, [(dst, 0, n, 256)], li))
            for fc in range(FC):
                for c0 in range(0, D, CW):
                    dst = wdn_b[li][c0 // 128:(c0 + CW) // 128, :, fc, :].rearrange("d p c -> p d c")
                    items.append((w_dn[li][fc * 128:(fc + 1) * 128, c0:c0 + CW], CW, [(dst, 0, CW, 128)], li))
            if li == 0:
                for kc in range(KC):
                    for c0 in range(0, NIN, CW):
                        n = min(CW, NIN - c0)
                        nb = n // 512
                        outs = []
                        if nb:
                            outs.append((win_b[c0 // 512:c0 // 512 + nb, :, kc, :].rearrange("j p c -> p j c"), 0, nb * 512, 512))
                        if n % 512:
                            outs.append((win_b[c0 // 512 + nb, :, kc, 0:n % 512], nb * 512, n, 0))
                        items.append((w_in[kc * 128:(kc + 1) * 128, c0:c0 + n], n, outs, 0))
        for kc in range(KC):
            for c0 in range(0, D, CW):
                dst = wout_b[c0 // 256:(c0 + CW) // 256, :, kc, :].rearrange("j p c -> p j c")
                items.append((w_out[kc * 128:(kc + 1) * 128, c0:c0 + CW], CW, [(dst, 0, CW, 256)], 1))
        return items

    cast_ci = [0]

    def emit_cast(item, wld, wcv):
        src, n, outs, _ = item
        q = cast_ci[0] % len(wld)
        cast_ci[0] += 1
        S.dma('sp', out=wld[q][:, :n], in_=src)
        ce = (cast_ci[0] - 1) % 3
        if ce == 0:
            S.dve.tensor_copy(out=wcv[q][:, :n], in_=wld[q][:, :n])
        elif ce == 1:
            S.act.copy(out=wcv[q][:, :n], in_=wld[q][:, :n])
        else:
            S.pool.tensor_copy(out=wcv[q][:, :n], in_=wld[q][:, :n])
        for (dst, a0, a1, blk) in outs:
            if blk:
                S.dma('sp', out=dst, in_=wcv[q][:, a0:a1].rearrange("p (j c) -> p j c", c=blk))
            else:
                S.dma('sp', out=dst, in_=wcv[q][:, a0:a1])

    all_items = cast_items()
    early_items = [it for it in all_items if it[3] == 0]
    late_items = [it for it in all_items if it[3] == 1]
    with ExitStack() as stW:
        wld = [sbx(stW, "wld", [128, CW]) for _ in range(6)]
        wcv = [sbx(stW, "wcv", [128, CW], BF16) for _ in range(6)]
        for it in early_items:
            emit_cast(it, wld, wcv)
        if not BGCAST:
            for it in late_items:
                emit_cast(it, wld, wcv)
            late_items = []
        S.flush()
    if not cfg.do_mixer:
        with ExitStack() as st0:
            z = sbx(st0, "zt", [128, D], BF16)
            S.dve.memset(z, 0.0)
            for r0 in range(0, NT, 128):
                rows = min(128, NT - r0)
                S.dma('sp', out=mixed[r0:r0 + rows, :], in_=z[:rows, :])
            S.flush()
    S.flush()

    TWM = 512
    import os as _os
    STOP = int(_os.environ.get("MK_STOP", "9"))
    DBG = int(_os.environ.get("MK_DBG", "0"))
    if STOP == 0:
        return nc, S.finish()

    class NS:
        pass

    NWB = 3

    def bg_cast(b, n):
        if b.wld is None:
            return
        for _ in range(n):
            if late_items:
                emit_cast(late_items.pop(0), b.wld, b.wcv)

    def dense_bufs(stack):
        b = NS()
        b.xf = sbx(stack, "xf", [128, KC, TWM])
        b.xb = sbx(stack, "xb", [128, KC, TWM], BF16)
        b.gT = sbx(stack, "gT", [128, FC, TWM], BF16)
        b.wa = [sbx(stack, "wa", [128, KC, 256], BF16) for _ in range(NWB)]
        b.wb = [sbx(stack, "wb", [128, KC, 256], BF16) for _ in range(NWB)]
        b.wld = b.wcv = None
        b.wd = [sbx(stack, "wd", [128, FC // 2, 128], BF16) for _ in range(2)]
        b.scr = [sbx(stack, "scr", [128, TWM]) for _ in range(2)]
        b.mean = sbx(stack, "mean", [128, TWM])
        b.rstd = sbx(stack, "rstd", [128, TWM])
        b.xtok = [sbx(stack, "xtok", [128, 1024]) for _ in range(2)]
        b.psA = [psx(stack, "psA", [128, 512]) for _ in range(2)]
        b.psB = [psx(stack, "psB", [128, 512]) for _ in range(2)]
        b.psS = psx(stack, "psS", [128, 512])
        b.psQ = psx(stack, "psQ", [128, 512])
        b.psT = [psx(stack, "psT", [128, 1024], BF16) for _ in range(2)]
        return b

    def ffn(b, li, TW):
        for jc in range(DFF // 256):
            bi = jc % NWB
            S.dma('sp', out=b.wa[bi], in_=wup_b[li][jc], nosplit=True)
            S.dma('sp', out=b.wb[bi], in_=wup_b[li][DFF // 256 + jc], nosplit=True)
            bg_cast(b, 2)
            for jj in range(2):
                j = jc * 2 + jj
                pa, pb_ = b.psA[j % 2], b.psB[j % 2]
                for kc in range(KC):
                    S.pe.matmul(pa[:, :TW], lhsT=b.wa[bi][:, kc, jj * 128:(jj + 1) * 128], rhs=b.xb[:, kc, :TW],
                                start=(kc == 0), stop=(kc == KC - 1))
                for kc in range(KC):
                    S.pe.matmul(pb_[:, :TW], lhsT=b.wb[bi][:, kc, jj * 128:(jj + 1) * 128], rhs=b.xb[:, kc, :TW],
                                start=(kc == 0), stop=(kc == KC - 1))
                S.act.activation(out=b.scr[j % 2][:, :TW], in_=pa[:, :TW], func=AF.Silu)
                S.dve.tensor_tensor(out=b.gT[:, j, :TW], in0=b.scr[j % 2][:, :TW], in1=pb_[:, :TW], op=ALU.mult)
        HF = FC // 2
        for dc in range(KC):
            pa = b.psA[dc % 2]
            for hf in range(2):
                S.dma('sp', out=b.wd[hf], in_=wdn_b[li][dc, :, hf * HF:(hf + 1) * HF, :], nosplit=True)
                for jl in range(HF):
                    j = hf * HF + jl
                    S.pe.matmul(pa[:, :TW], lhsT=b.wd[hf][:, jl, :], rhs=b.gT[:, j, :TW], start=(j == 0), stop=(j == FC - 1))
            S.dve.scalar_tensor_tensor(out=b.xf[:, dc, :TW], in0=pa[:, :TW], scalar=0.5, in1=b.xf[:, dc, :TW],
                                       op0=ALU.mult, op1=ALU.add)

    def layer_norm(b, gi, TW, scale_out):
        xf, xb = b.xf, b.xb
        for c in range(KC):
            S.pe.matmul(b.psS[:, :TW], lhsT=ones_f, rhs=xf[:, c, :TW], start=(c == 0), stop=(c == KC - 1))
        for c in range(KC):
            S.act.activation(out=b.scr[c % 2][:, :TW], in_=xf[:, c, :TW], func=AF.Square)
            S.pe.matmul(b.psQ[:, :TW], lhsT=ones_f, rhs=b.scr[c % 2][:, :TW], start=(c == 0), stop=(c == KC - 1))
        mean, rstd = b.mean, b.rstd
        S.act.mul(out=mean[:, :TW], in_=b.psS[:, :TW], mul=1.0 / D)
        S.dve.tensor_tensor(out=rstd[:, :TW], in0=mean[:, :TW], in1=mean[:, :TW], op=ALU.mult)
        S.dve.scalar_tensor_tensor(out=rstd[:, :TW], in0=b.psQ[:, :TW], scalar=1.0 / D, in1=rstd[:, :TW],
                                   op0=ALU.mult, op1=ALU.subtract)
        S.dve.tensor_scalar(out=rstd[:, :TW], in0=rstd[:, :TW], scalar1=EPS, scalar2=None, op0=ALU.add)
        S.act.activation(out=rstd[:, :TW], in_=rstd[:, :TW], func=AF.Sqrt)
        S.dve.reciprocal(out=rstd[:, :TW], in_=rstd[:, :TW])
        gsrc = lnpa if scale_out else lnp
        for c in range(KC):
            t = b.scr[c % 2]
            S.dve.tensor_tensor(out=t[:, :TW], in0=xf[:, c, :TW], in1=mean[:, :TW], op=ALU.subtract)
            S.dve.tensor_tensor(out=t[:, :TW], in0=t[:, :TW], in1=rstd[:, :TW], op=ALU.mult)
            S.act.activation(out=xb[:, c, :TW], in_=t[:, :TW], func=AF.Identity,
                             scale=lnp[:, 2 * gi, c:c + 1], bias=lnp[:, 2 * gi + 1, c:c + 1])
            S.act.activation(out=xf[:, c, :TW], in_=t[:, :TW], func=AF.Identity,
                             scale=gsrc[:, 2 * gi, c:c + 1], bias=gsrc[:, 2 * gi + 1, c:c + 1])

    def load_x(b, t0, TW):
        nsub = (TW + 127) // 128
        for s in range(nsub):
            rows = min(128, TW - s * 128)
            for hf in range(2):
                xt = b.xtok[hf]
                S.dma('sp', out=xt[:rows, :], in_=xin[t0 + s * 128:t0 + s * 128 + rows, hf * 1024:(hf + 1) * 1024])
                if DBG == 1:
                    continue
                for c4 in range(2):
                    pa = b.psA[c4 % 2]
                    for k in range(4):
                        S.pe.transpose(pa[:, k * 128:k * 128 + rows], xt[:rows, (c4 * 4 + k) * 128:(c4 * 4 + k + 1) * 128],
                                       ident[:rows, :rows])
                    if DBG == 2:
                        continue
                    src = pa.rearrange("p (k n) -> p k n", k=4)[:, :, :rows]
                    cb = hf * 8 + c4 * 4
                    S.act.mul(out=b.xf[:, cb:cb + 4, s * 128:s * 128 + rows], in_=src, mul=ALPHA)
                    if DBG != 5:
                        S.act.copy(out=b.xb[:, cb:cb + 4, s * 128:s * 128 + rows], in_=src)
                    elif DBG == 4:
                        S.dve.tensor_copy(out=b.xb[:, cb:cb + 4, s * 128:s * 128 + rows], in_=b.xf[:, cb:cb + 4, s * 128:s * 128 + rows])
                    else:
                        S.dve.tensor_copy(out=b.xb[:, cb:cb + 4, s * 128:s * 128 + rows], in_=src)


    def w_in_proj(b, a, t0, TW):
        nsub = (TW + 127) // 128
        rts = a.ropet[(t0 // 512) % 2]
        for s in range(nsub):
            rows = min(128, TW - s * 128)
            S.dma('sp', out=rts[:rows, s, :], in_=rope_d[t0 + s * 128:t0 + s * 128 + rows, :])
        tcount = [0]

        def rope_tok(dst, src, nh, rt, rows):
            s3 = src[:rows, :nh * 128].rearrange("p (h d) -> p h d", h=nh)
            d3 = dst[:rows, :nh * 128].rearrange("p (h d) -> p h d", h=nh)
            cosb = rt[:rows, 0:64].unsqueeze(1).to_broadcast([rows, nh, 64])
            sinb = rt[:rows, 64:128].unsqueeze(1).to_broadcast([rows, nh, 64])
            ta = a.t1[:rows, :nh * 64].rearrange("p (h d) -> p h d", h=nh)
            tb_ = a.t2[:rows, :nh * 64].rearrange("p (h d) -> p h d", h=nh)
            x1, x2 = s3[:, :, 0:64], s3[:, :, 64:128]
            S.dve.tensor_tensor(out=ta, in0=x1, in1=cosb, op=ALU.mult)
            S.dve.tensor_tensor(out=tb_, in0=x2, in1=sinb, op=ALU.mult)
            S.dve.tensor_tensor(out=d3[:, :, 0:64], in0=ta, in1=tb_, op=ALU.subtract)
            S.dve.tensor_tensor(out=ta, in0=x1, in1=sinb, op=ALU.mult)
            S.dve.tensor_tensor(out=tb_, in0=x2, in1=cosb, op=ALU.mult)
            S.dve.tensor_tensor(out=d3[:, :, 64:128], in0=ta, in1=tb_, op=ALU.add)

        def transposes_to(src_b, s, rows, nh, blk_tr):
            pt = b.psT[tcount[0] % 2]
            tcount[0] += 1
            for h in range(nh):
                S.pe.transpose(pt[:, h * 128:h * 128 + rows], src_b[:rows, h * 128:(h + 1) * 128], identb[:rows, :rows])
            S.act.copy(out=blk_tr[:, :nh, s * 128:s * 128 + rows],
                       in_=pt[:, :nh * 128].rearrange("p (h n) -> p h n", h=nh)[:, :, :rows])

        for blk in range(12):
            c0 = blk * 512
            ncol = min(512, NIN - c0)
            bi = blk % 2
            wt = b.gT[:, bi * KC:(bi + 1) * KC, :]
            wkey = "gT_w%d" % bi
            S.dma('sp', out=wt[:, :, :ncol], in_=win_b[blk][:, :, :ncol], nosplit=(ncol == 512),
                  R=[], W=[wkey] + ([b.gT.name] if blk == 0 else []))
            tr = a.trT[blk % 2]
            for s in range(nsub):
                rows = min(128, TW - s * 128)
                tok0 = t0 + s * 128
                q = (blk * 4 + s) % 2
                pa = b.psA[q]
                for kc in range(KC):
                    S.pe.matmul(pa[:rows, :ncol], lhsT=b.xb[:, kc, s * 128:s * 128 + rows], rhs=wt[:, kc, :ncol],
                                start=(kc == 0), stop=(kc == KC - 1), R=[b.xb.name, wkey, b.gT.name], W=[pa.name])
                e, r, rb = a.ev[q], a.rp[q], a.rpb[q]
                rt = rts[:, s, :]
                if blk in (0, 1, 6, 7):
                    S.act.copy(out=e[:rows, :], in_=pa[:rows, :])
                    rope_tok(r, e, 4, rt, rows)
                    if blk == 1:
                        S.act.mul(out=rb[:rows, :], in_=r[:rows, :], mul=RDK ** -0.5)
                        S.dma('sp', out=rk_tok[tok0:tok0 + rows, :], in_=rb[:rows, :])
                    else:
                        S.act.copy(out=rb[:rows, :], in_=r[:rows, :])
                    transposes_to(rb, s, rows, 4, tr)
                elif blk in (2, 3):
                    S.act.copy(out=rb[:rows, :], in_=pa[:rows, :])
                    S.dma('sp', out=rv_tok[tok0:tok0 + rows, (blk - 2) * 512:(blk - 1) * 512], in_=rb[:rows, :])
                elif blk in (4, 5):
                    S.act.activation(out=rb[:rows, :], in_=pa[:rows, :], func=AF.Silu)
                    S.dma('sp', out=rg_tok[tok0:tok0 + rows, (blk - 4) * 512:(blk - 3) * 512], in_=rb[:rows, :])
                elif blk in (8, 9, 10):
                    S.act.copy(out=e[:rows, :], in_=pa[:rows, :])
                    S.act.copy(out=r[:rows, 256:512], in_=pa[:rows, 256:512])
                    rope_tok(r, e, 2, rt, rows)
                    S.dve.tensor_copy(out=rb[:rows, :], in_=r[:rows, :])
                    if blk in (8, 9):
                        S.dma('sp', out=kv_out[tok0:tok0 + rows, (blk - 8) * 512:(blk - 7) * 512], in_=r[:rows, :])
                    elif tok0 < cfg.ntp:
                        b_loc, pos0 = tok0 // seq, tok0 % seq
                        w0 = pos0 - (seq - WINDOW)
                        if w0 >= 0:
                            S.dma('sp', out=winp_out[b_loc, w0:w0 + rows, :], in_=r[:rows, :])
                    else:
                        for sq_ in range(cfg.n_sseq):
                            S.dma('sp', out=wins_out[sq_, WINDOW - DEC:WINDOW, :], in_=r[sq_ * DEC:(sq_ + 1) * DEC, :])
                    if blk == 8:
                        transposes_to(rb, s, rows, 4, tr)
                    else:
                        transposes_to(rb, s, rows, 2, tr)
                        dstv = vslc_tok if blk == 9 else vwin_tok
                        S.dma('sp', out=dstv[tok0:tok0 + rows, :], in_=rb[:rows, 256:512])
                else:
                    S.act.activation(out=a.gsb[:rows, s, :], in_=pa[:rows, :24], func=AF.Sigmoid)
                    S.dma('sp', out=gates_d[tok0:tok0 + rows, :], in_=a.gsb[:rows, s, :])
            if blk == 0:
                S.dma('sp', out=rqT[:, :, t0:t0 + TW], in_=tr[:, :, :TW])
            elif blk == 1:
                S.dma('sp', out=rkT[:, :, t0:t0 + TW], in_=tr[:, :, :TW])
            elif blk in (6, 7):
                S.dma('sp', out=nqT[:, (blk - 6) * 4:(blk - 5) * 4, t0:t0 + TW], in_=tr[:, :, :TW])
            elif blk == 8:
                S.dma('sp', out=kcmpT[:, :, t0:t0 + TW], in_=tr[:, 0:2, :TW])
                S.dma('sp', out=vcmpT[:, :, t0:t0 + TW], in_=tr[:, 2:4, :TW])
            elif blk == 9:
                S.dma('sp', out=kslcT[:, :, t0:t0 + TW], in_=tr[:, 0:2, :TW])
            elif blk == 10:
                S.dma('sp', out=kwinT[:, :, t0:t0 + TW], in_=tr[:, 0:2, :TW])

    with ExitStack() as stA:
        b = dense_bufs(stA)
        a = NS()
        a.ev = [sbx(stA, "ev", [128, 512]) for _ in range(2)]
        a.rp = [sbx(stA, "rp", [128, 512]) for _ in range(2)]
        a.rpb = [sbx(stA, "rpb", [128, 512], BF16) for _ in range(2)]
        a.t1 = sbx(stA, "t1", [128, 256])
        a.t2 = sbx(stA, "t2", [128, 256])
        a.ropet = [sbx(stA, "ropet", [128, 4, 128])] * 2
        a.trT = [sbx(stA, "trT", [128, 4, TWM], BF16) for _ in range(2)]
        a.gsb = sbx(stA, "gsb", [128, 4, 24])
        if late_items:
            b.wld = [sbx(stA, "wldA", [128, CW]) for _ in range(2)]
            b.wcv = [sbx(stA, "wcvA", [128, CW], BF16) for _ in range(2)]
        for (t0, TW) in cfg.tiles:
            load_x(b, t0, TW)
            if STOP == 10:
                if not _os.environ.get("MK_NOST"):
                    S.dma('sp', out=x1T[:, :, t0:t0 + TW], in_=b.xf[:, :, :TW])
                return nc, S.finish()
            ffn(b, 0, TW)
            if STOP == 11:
                S.dma('sp', out=x1T[:, :, t0:t0 + TW], in_=b.xf[:, :, :TW])
                return nc, S.finish()
            layer_norm(b, 0, TW, True)
            S.dma('sp', out=x1T[:, :, t0:t0 + TW], in_=b.xf[:, :, :TW])
            if STOP == 12:
                return nc, S.finish()
            w_in_proj(b, a, t0, TW)
            if STOP == 13:
                return nc, S.finish()
        while late_items:
            bg_cast(b, 1)
        for sq_ in range(cfg.n_sseq):
            for r0 in range(0, WINDOW - DEC, 126):
                q = (r0 // 126) % 2
                S.dma('sp', out=a.ev[q][:126, :], in_=cwin_d[sq_, DEC + r0:DEC + r0 + 126, :])
                S.dma('sp', out=wins_out[sq_, r0:r0 + 126, :], in_=a.ev[q][:126, :])
        S.flush()

    if STOP == 1:
        return nc, S.finish()

    def retention_phase():
        with ExitStack() as stB:
            rcs = {128: sbx(stB, "rc128", [128, 3, RH, 128]), 8: sbx(stB, "rc8", [128, 3, RH, 128])}
            S.dma('sp', out=rcs[128], in_=rc128_d)
            S.dma('sp', out=rcs[8], in_=rc8_d)
            zt = sbx(stB, "ztile", [128, NH * HD], BF16)
            S.dve.memset(zt, 0.0)
            if cfg.nts:
                S.dma('sp', out=mixed[cfg.ntp:cfg.ntp + cfg.nts, RH * RDV:D], in_=zt[:cfg.nts, :])
            gn = sbx(stB, "gn", [128, 2, RH * RDV])
            S.dma('sp', out=gn, in_=gn_d)
            LM = seq
            rq_sb = sbx(stB, "rq_sb", [128, RH, LM], BF16)
            rk_sb = sbx(stB, "rk_sb", [128, RH, LM], BF16)
            rkt_sb = sbx(stB, "rkt_sb", [128, LM // 128, RH * RDK], BF16)
            rv_sb = sbx(stB, "rv_sb", [128, LM // 128, RH * RDV], BF16)
            rg_sb = sbx(stB, "rg_sb", [128, LM // 128, RH * RDV], BF16)
            S_f = sbx(stB, "S_f", [128, RH, RDV])
            S_b = sbx(stB, "S_b", [128, RH, RDV], BF16)
            aTb = [sbx(stB, "aTb", [128, 128], BF16) for _ in range(2)]
            qd = [sbx(stB, "qd", [128, 128], BF16) for _ in range(2)]
            ksc = [sbx(stB, "ksc", [128, 128], BF16) for _ in range(2)]
            cen = [sbx(stB, "cen", [128, RDV]) for _ in range(2)]
            sqt = [sbx(stB, "sqt", [128, RDV]) for _ in range(2)]
            stt = [sbx(stB, "stt", [128, 4]) for _ in range(2)]
            mix = [sbx(stB, "mixr", [128, RH * RDV], BF16) for _ in range(2)]
            pa_ = [psx(stB, "rpa", [128, 512]) for _ in range(2)]
            po_ = [psx(stB, "rpo", [128, 512]) for _ in range(2)]
            pst_ = [psx(stB, "rps", [128, 512]) for _ in range(2)]

            def run_seq(tb, L, CS, s0_ap, rs_idx):
                nch = L // CS
                rc = rcs[CS]
                rcbb = rc
                S.dma('sp', out=rq_sb[:, :, :L], in_=rqT[:, :, tb:tb + L])
                S.dma('sp', out=rk_sb[:, :, :L], in_=rkT[:, :, tb:tb + L])
                if CS == 128:
                    S.dma('sp', out=rkt_sb[:, :nch, :], in_=rk_tok[tb:tb + L, :].rearrange("(c p) n -> p c n", p=128))
                    S.dma('sp', out=rv_sb[:, :nch, :], in_=rv_tok[tb:tb + L, :].rearrange("(c p) n -> p c n", p=128))
                    S.dma('sp', out=rg_sb[:, :nch, :], in_=rg_tok[tb:tb + L, :].rearrange("(c p) n -> p c n", p=128))
                else:
                    S.dma('sp', out=rkt_sb[:CS, 0, :], in_=rk_tok[tb:tb + L, :])
                    S.dma('sp', out=rv_sb[:CS, 0, :], in_=rv_tok[tb:tb + L, :])
                    S.dma('sp', out=rg_sb[:CS, 0, :], in_=rg_tok[tb:tb + L, :])
                have_state = s0_ap is not None
                if have_state:
                    S.dma('sp', out=S_f, in_=s0_ap.rearrange("h d e -> d h e"))
                    S.act.copy(out=S_b, in_=S_f)
                u = 0
                for c in range(nch):
                    mx = mix[c % 2]
                    for h in range(RH):
                        q = u % 2
                        u += 1
                        cs = slice(c * CS, (c + 1) * CS)
                        pa, po, pst = pa_[q], po_[q], pst_[q]
                        S.pe.matmul(pa[:CS, :CS], lhsT=rk_sb[:, h, cs], rhs=rq_sb[:, h, cs], start=True, stop=True)
                        S.dve.tensor_tensor(out=aTb[q][:CS, :CS], in0=pa[:CS, :CS], in1=rcbb[:CS, 0, h, :CS], op=ALU.mult)
                        S.pe.matmul(po[:CS, :RDV], lhsT=aTb[q][:CS, :CS], rhs=rv_sb[:CS, c, h * RDV:(h + 1) * RDV],
                                    start=True, stop=not have_state)
                        if have_state:
                            S.pool.tensor_tensor(out=qd[q][:, :CS], in0=rq_sb[:, h, cs], in1=rcbb[:, 1, h, :CS], op=ALU.mult)
                            S.pe.matmul(po[:CS, :RDV], lhsT=qd[q][:, :CS], rhs=S_b[:, h, :], start=False, stop=True)
                        st_ = stt[q]
                        S.dve.reduce_sum(out=st_[:CS, 0:1], in_=po[:CS, :RDV], axis=AX.X)
                        S.dve.tensor_scalar(out=st_[:CS, 1:2], in0=st_[:CS, 0:1], scalar1=-1.0 / RDV, scalar2=None, op0=ALU.mult)
                        S.act.activation(out=cen[q][:CS, :], in_=po[:CS, :RDV], func=AF.Identity, bias=st_[:CS, 1:2], scale=1.0)
                        S.dve.tensor_tensor(out=sqt[q][:CS, :], in0=cen[q][:CS, :], in1=cen[q][:CS, :], op=ALU.mult)
                        S.dve.reduce_sum(out=st_[:CS, 2:3], in_=sqt[q][:CS, :], axis=AX.X)
                        S.dve.tensor_scalar(out=st_[:CS, 3:4], in0=st_[:CS, 2:3], scalar1=1.0 / RDV, scalar2=EPS,
                                            op0=ALU.mult, op1=ALU.add)
                        S.act.activation(out=st_[:CS, 3:4], in_=st_[:CS, 3:4], func=AF.Sqrt)
                        S.dve.reciprocal(out=st_[:CS, 3:4], in_=st_[:CS, 3:4])
                        S.dve.scalar_tensor_tensor(out=cen[q][:CS, :], in0=cen[q][:CS, :], scalar=st_[:CS, 3:4],
                                                   in1=gn[:CS, 0, h * RDV:(h + 1) * RDV], op0=ALU.mult, op1=ALU.mult)
                        S.dve.tensor_tensor(out=cen[q][:CS, :], in0=cen[q][:CS, :], in1=gn[:CS, 1, h * RDV:(h + 1) * RDV], op=ALU.add)
                        S.dve.tensor_tensor(out=mx[:CS, h * RDV:(h + 1) * RDV], in0=cen[q][:CS, :],
                                            in1=rg_sb[:CS, c, h * RDV:(h + 1) * RDV], op=ALU.mult)
                        S.act.activation(out=ksc[q][:CS, :], in_=rkt_sb[:CS, c, h * RDK:(h + 1) * RDK], func=AF.Identity,
                                         scale=rc[:CS, 2, h, 0:1])
                        S.pe.matmul(pst[:, :RDV], lhsT=ksc[q][:CS, :], rhs=rv_sb[:CS, c, h * RDV:(h + 1) * RDV], start=True, stop=True)
                        cdk = math.exp(LG[h] * CS)
                        if have_state:
                            S.dve.scalar_tensor_tensor(out=S_f[:, h, :], in0=S_f[:, h, :], scalar=float(np.float32(cdk)),
                                                       in1=pst[:, :RDV], op0=ALU.mult, op1=ALU.add)
                        else:
                            S.dve.tensor_copy(out=S_f[:, h, :], in_=pst[:, :RDV])
                        S.act.copy(out=S_b[:, h, :], in_=S_f[:, h, :])
                    have_state = True
                    S.dma('sp', out=mixed[tb + c * CS:tb + (c + 1) * CS, 0:RH * RDV], in_=mx[:CS, :])
                S.dma('sp', out=rs_out[rs_idx].rearrange("h d e -> d h e"), in_=S_f)

            for bq in range(cfg.n_pseq):
                run_seq(bq * seq, seq, 128, None, bq)
            for sq_ in range(cfg.n_sseq):
                run_seq(cfg.ntp + sq_ * DEC, DEC, DEC, state_d[sq_], cfg.n_pseq + sq_)
            S.flush()

    def nsa_prompt_phase():
        with ExitStack() as stC:
            w1 = [sbx(stC, "w1", [128, 32, 256], BF16) for _ in range(2)]
            w2 = [sbx(stC, "w2", [128, 2, 128], BF16) for _ in range(2)]
            posf = sbx(stC, "posf", [128, 2, 32])
            posb = sbx(stC, "posb", [128, 2, 32], BF16)
            hpos = sbx(stC, "hpos", [128, 2, 2])
            ld32 = sbx(stC, "ld32", [128, seq])
            mcmp = sbx(stC, "mcmp", [128, seq], BF16)
            cover = sbx(stC, "cover", [128, 32], BF16)
            mtri = sbx(stC, "mtri", [128, 2, 128], BF16)
            selc = sbx(stC, "selc", [128, nqt, 3, 32])
            Eb = sbx(stC, "Eb", [32, nqt, 128], BF16)
            for kv in range(2):
                for p0 in range(0, 32, 8):
                    S.dma('pool', out=w1[kv][:, p0:p0 + 8, :], in_=w1_d[kv][p0:p0 + 8].rearrange("p d f -> d p f"))
                S.dma('pool', out=w2[kv], in_=w2_d[kv].rearrange("(c p) d -> p c d", p=128))
            S.dma('sp', out=posf, in_=posT_d)
            S.dve.tensor_copy(out=posb, in_=posf)
            S.dma('sp', out=ld32, in_=mcmp_d)
            S.dve.tensor_copy(out=mcmp, in_=ld32)
            S.dma('sp', out=selc, in_=sel_d)
            S.dma('pool', out=cover, in_=cover_d)
            S.dma('pool', out=mtri, in_=mtri_d)
            S.dma('pool', out=Eb, in_=E_d)
            nq_sb = sbx(stC, "nq_sb", [128, NH, seq], BF16)
            kT_sb = {k: sbx(stC, k, [128, NKV, seq], BF16) for k in ("kslc", "kwin", "kcmp", "vcmp")}
            v_sb = {k: sbx(stC, k, [128, nqt, NKV * HD], BF16) for k in ("vslc", "vwin")}
            g_sb = sbx(stC, "g_sb", [128, nqt, 24])
            kcT = sbx(stC, "kcT", [128, NKV, 128], BF16)
            vc = sbx(stC, "vc", [128, NKV, 128], BF16)
            hx = [sbx(stC, "hx", [128, 128]) for _ in range(2)]
            hu = [sbx(stC, "hu", [128, 128]) for _ in range(2)]
            hT = [sbx(stC, "hT", [128, 128], BF16) for _ in range(2)]
            qc = [sbx(stC, "qc", [128, 512], BF16) for _ in range(2)]
            es = [sbx(stC, "es", [128, 512], BF16) for _ in range(3)]
            pT = [sbx(stC, "pT", [128, 512], BF16) for _ in range(3)]
            msk = [sbx(stC, "msk", [128, 128], BF16) for _ in range(2)]
            dens = sbx(stC, "dens", [128, 12])
            coef = sbx(stC, "coef", [128, 12])
            imp = sbx(stC, "imp", [128, 32])
            prio = sbx(stC, "prio", [128, 32])
            cmpm = sbx(stC, "cmpm", [128, 32, 32])
            rank = sbx(stC, "rank", [128, 32])
            selN = sbx(stC, "selN", [32, 128], BF16)
            ob = [sbx(stC, "ob", [128, 128]) for _ in range(2)]
            mixn = [sbx(stC, "mixn", [128, NH * HD], BF16) for _ in range(2)]
            pS = [psx(stC, "pS", [128, 512]) for _ in range(2)]
            pO = [psx(stC, "pO", [128, 512]) for _ in range(3)]
            pDen = psx(stC, "pDen", [128, 512])
            pX = psx(stC, "pX", [128, 512])
            pM = psx(stC, "pM", [128, 512])
            for kv in range(2):
                for fcn in range(2):
                    for p_ in range(32):
                        S.pe.matmul(pX[:, 0:1], lhsT=w1[kv][:, p_, fcn * 128:(fcn + 1) * 128], rhs=posb[:, kv, p_:p_ + 1],
                                    start=(p_ == 0), stop=(p_ == 31))
                    S.act.copy(out=hpos[:, kv, fcn:fcn + 1], in_=pX[:, 0:1])

            for bq in range(cfg.n_pseq):
                tb = bq * seq
                S.dma('sp', out=nq_sb, in_=nqT[:, :, tb:tb + seq])
                for k, src in (("kslc", kslcT), ("kwin", kwinT), ("kcmp", kcmpT), ("vcmp", vcmpT)):
                    S.dma('sp', out=kT_sb[k], in_=src[:, :, tb:tb + seq])
                S.dma('sp', out=v_sb["vslc"], in_=vslc_tok[tb:tb + seq, :].rearrange("(c p) n -> p c n", p=128))
                S.dma('sp', out=v_sb["vwin"], in_=vwin_tok[tb:tb + seq, :].rearrange("(c p) n -> p c n", p=128))
                S.dma('sp', out=g_sb, in_=gates_d[tb:tb + seq, :].rearrange("(c p) n -> p c n", p=128))
                for kv in range(2):
                    srcT = kT_sb["kcmp" if kv == 0 else "vcmp"]
                    for h in range(NKV):
                        xv = srcT[:, h, :].rearrange("p (n s) -> p n s", s=16)
                        for fcn in range(2):
                            ps_ = pS[fcn]
                            i = 0
                            for r in range(2):
                                for s_ in range(16):
                                    S.pe.matmul(ps_[:, :NCMP], lhsT=w1[kv][:, r * 16 + s_, fcn * 128:(fcn + 1) * 128],
                                                rhs=xv[:, r:r + NCMP, s_], start=(i == 0), stop=(i == 31))
                                    i += 1
                            x_, u_ = hx[fcn], hu[fcn]
                            S.act.activation(out=x_[:, :NCMP], in_=ps_[:, :NCMP], func=AF.Identity, bias=hpos[:, kv, fcn:fcn + 1], scale=1.0)
                            S.dve.tensor_tensor(out=u_[:, :NCMP], in0=x_[:, :NCMP], in1=x_[:, :NCMP], op=ALU.mult)
                            S.dve.tensor_scalar(out=u_[:, :NCMP], in0=u_[:, :NCMP], scalar1=0.044715, scalar2=1.0, op0=ALU.mult, op1=ALU.add)
                            S.dve.tensor_tensor(out=u_[:, :NCMP], in0=u_[:, :NCMP], in1=x_[:, :NCMP], op=ALU.mult)
                            S.act.activation(out=u_[:, :NCMP], in_=u_[:, :NCMP], func=AF.Tanh, scale=0.7978845608028654)
                            S.dve.tensor_scalar(out=u_[:, :NCMP], in0=u_[:, :NCMP], scalar1=1.0, scalar2=0.5, op0=ALU.add, op1=ALU.mult)
                            S.dve.tensor_tensor(out=hT[fcn][:, :NCMP], in0=u_[:, :NCMP], in1=x_[:, :NCMP], op=ALU.mult)
                        if kv == 0:
                            for fcn in range(2):
                                S.pe.matmul(pX[:, :NCMP], lhsT=w2[0][:, fcn, :], rhs=hT[fcn][:, :NCMP], start=(fcn == 0), stop=(fcn == 1))
                            S.act.copy(out=kcT[:, h, :NCMP], in_=pX[:, :NCMP])
                        else:
                            for fcn in range(2):
                                S.pe.matmul(pX[:NCMP, :128], lhsT=hT[fcn][:, :NCMP], rhs=w2[1][:, fcn, :], start=(fcn == 0), stop=(fcn == 1))
                            S.act.copy(out=vc[:NCMP, h, :], in_=pX[:NCMP, :128])
                u = 0
                ei = 0
                for qt in range(nqt):
                    mx = mixn[qt % 2]
                    for h in range(NKV):
                        qcu = qc[u % 2]
                        u += 1
                        S.pool.tensor_copy(out=qcu.rearrange("p (g n) -> p g n", g=4), in_=nq_sb[:, 4 * h:4 * h + 4, qt * 128:(qt + 1) * 128])

                        def pv(e_t, rows, vrhs, po, br, first, last):
                            for g in range(4):
                                S.pe.matmul(po[:, g * 128:(g + 1) * 128], lhsT=e_t[:rows, g * 128:(g + 1) * 128], rhs=vrhs,
                                            start=(first and g == 0), stop=last)
                            for g in range(4):
                                S.pe.matmul(pDen[:, br * 4 + g:br * 4 + g + 1], lhsT=e_t[:rows, g * 128:(g + 1) * 128],
                                            rhs=ones_b[:rows, 0:1], start=(first and br == 0 and g == 0), stop=last)

                        def bc4(m):
                            return m.unsqueeze(1).to_broadcast([m.shape[0], 4, 128])

                        ps_ = pS[ei % 2]
                        e_, p_ = es[ei % 3], pT[ei % 3]
                        ei += 1
                        S.pe.matmul(ps_[:NCMP, :], lhsT=kcT[:, h, :NCMP], rhs=qcu, start=True, stop=True)
                        S.act.activation(out=e_[:NCMP, :], in_=ps_[:NCMP, :], func=AF.Exp, scale=SCALE)
                        S.dve.tensor_tensor(out=p_[:NCMP, :].rearrange("p (g n) -> p g n", g=4),
                                            in0=e_[:NCMP, :].rearrange("p (g n) -> p g n", g=4),
                                            in1=bc4(mcmp[:NCMP, qt * 128:(qt + 1) * 128]), op=ALU.mult)
                        pv(p_, NCMP, vc[:NCMP, h, :], pO[0], 0, True, True)
                        for g in range(4):
                            S.pe.matmul(pX[:, g * 32:(g + 1) * 32], lhsT=p_[:NCMP, g * 128:(g + 1) * 128], rhs=cover[:NCMP, :],
                                        start=(g == 0), stop=True)
                        S.dve.tensor_scalar(out=dens[:, 0:4], in0=pDen[:, 0:4], scalar1=1e-30, scalar2=None, op0=ALU.max)
                        S.dve.reciprocal(out=dens[:, 0:4], in_=dens[:, 0:4])
                        S.dve.tensor_scalar(out=imp, in0=pX[:, 0:32], scalar1=dens[:, 0:1], scalar2=None, op0=ALU.mult)
                        for g in range(1, 4):
                            S.dve.scalar_tensor_tensor(out=imp, in0=pX[:, g * 32:(g + 1) * 32], scalar=dens[:, g:g + 1], in1=imp,
                                                       op0=ALU.mult, op1=ALU.add)
                        S.dve.tensor_tensor(out=prio, in0=imp, in1=selc[:, qt, 0, :], op=ALU.mult)
                        S.dve.tensor_tensor(out=prio, in0=prio, in1=selc[:, qt, 1, :], op=ALU.add)
                        S.dve.tensor_tensor(out=cmpm, in0=prio.unsqueeze(1).to_broadcast([128, 32, 32]),
                                            in1=prio.unsqueeze(2).to_broadcast([128, 32, 32]), op=ALU.is_gt)
                        S.dve.reduce_sum(out=rank, in_=cmpm, axis=AX.X)
                        S.dve.tensor_scalar(out=rank, in0=rank, scalar1=15.5, scalar2=None, op0=ALU.is_lt)
                        S.dve.tensor_tensor(out=rank, in0=rank, in1=selc[:, qt, 2, :], op=ALU.mult)
                        k0 = max(0, qt - WINDOW // 128)
                        tasks = [("win", kt) for kt in range(k0, qt + 1)] + [("slc", kt) for kt in range(qt + 1)]
                        slots = []

                        def do_score(i):
                            kind, kt = tasks[i]
                            ps_ = pS[i % 2]
                            if kind == "win":
                                S.pe.matmul(ps_, lhsT=kT_sb["kwin"][:, h, kt * 128:(kt + 1) * 128], rhs=qcu, start=True, stop=True)
                            else:
                                if kt == 0:
                                    S.pe.transpose(pM[:32, 128:256], rank, ident)
                                    S.dve.tensor_copy(out=selN, in_=pM[:32, 128:256])
                                S.pe.matmul(ps_, lhsT=kT_sb["kslc"][:, h, kt * 128:(kt + 1) * 128], rhs=qcu, start=True, stop=True)
                                mreg = pM[:, 0:128] if i % 2 == 0 else pM[:, 384:512]
                                S.pe.matmul(mreg, lhsT=Eb[:, kt, :], rhs=selN, start=True, stop=True)

                        def do_post(i):
                            kind, kt = tasks[i]
                            ps_ = pS[i % 2]
                            e_, p_ = es[i % 3], pT[i % 3]
                            S.act.activation(out=e_, in_=ps_, func=AF.Exp, scale=SCALE)
                            if kind == "win":
                                if kt == qt or kt == qt - WINDOW // 128:
                                    mi = 0 if kt == qt else 1
                                    S.dve.tensor_tensor(out=p_.rearrange("p (g n) -> p g n", g=4), in0=e_.rearrange("p (g n) -> p g n", g=4),
                                                        in1=bc4(mtri[:, mi, :]), op=ALU.mult)
                                    return p_
                                return e_
                            mreg = pM[:, 0:128] if i % 2 == 0 else pM[:, 384:512]
                            mk = msk[i % 2]
                            if kt == qt:
                                S.dve.tensor_tensor(out=mk, in0=mreg, in1=mtri[:, 0, :], op=ALU.mult)
                            else:
                                S.dve.tensor_copy(out=mk, in_=mreg)
                            S.dve.tensor_tensor(out=p_.rearrange("p (g n) -> p g n", g=4), in0=e_.rearrange("p (g n) -> p g n", g=4),
                                                in1=bc4(mk), op=ALU.mult)
                            return p_

                        def do_pv(i, src_e):
                            kind, kt = tasks[i]
                            if kind == "win":
                                pv(src_e, 128, v_sb["vwin"][:, kt, h * 128:(h + 1) * 128], pO[2], 2, kt == k0, kt == qt)
                            else:
                                pv(src_e, 128, v_sb["vslc"][:, kt, h * 128:(h + 1) * 128], pO[1], 1, kt == 0, kt == qt)

                        do_score(0)
                        for i in range(len(tasks)):
                            if i + 1 < len(tasks):
                                do_score(i + 1)
                            src_e = do_post(i)
                            do_pv(i, src_e)
                        S.dve.tensor_scalar(out=dens, in0=pDen[:, 0:12], scalar1=1e-30, scalar2=None, op0=ALU.max)
                        S.dve.reciprocal(out=dens, in_=dens)
                        for br in range(3):
                            S.dve.tensor_tensor(out=coef[:, br * 4:(br + 1) * 4], in0=dens[:, br * 4:(br + 1) * 4],
                                                in1=g_sb[:, qt, br * 8 + 4 * h:br * 8 + 4 * h + 4], op=ALU.mult)
                        for g in range(4):
                            o_ = ob[g % 2]
                            S.dve.tensor_scalar(out=o_, in0=pO[0][:, g * 128:(g + 1) * 128], scalar1=coef[:, g:g + 1], scalar2=None, op0=ALU.mult)
                            S.dve.scalar_tensor_tensor(out=o_, in0=pO[1][:, g * 128:(g + 1) * 128], scalar=coef[:, 4 + g:5 + g], in1=o_,
                                                       op0=ALU.mult, op1=ALU.add)
                            S.dve.scalar_tensor_tensor(out=mx[:, (4 * h + g) * 128:(4 * h + g + 1) * 128], in0=pO[2][:, g * 128:(g + 1) * 128],
                                                       scalar=coef[:, 8 + g:9 + g], in1=o_, op0=ALU.mult, op1=ALU.add)
                    S.dma('sp', out=mixed[tb + qt * 128:tb + (qt + 1) * 128, RH * RDV:D], in_=mx)
            S.flush()


    def nsa_sample_phase():
        NPG = PAST // PAGE
        NG = 8
        with ExitStack() as stS:
            w1 = [sbx(stS, "w1s", [128, 32, 256], BF16) for _ in range(2)]
            w2 = [sbx(stS, "w2s", [128, 2, 128], BF16) for _ in range(2)]
            posf = sbx(stS, "posfs", [128, 2, 32])
            posb = sbx(stS, "posbs", [128, 2, 32], BF16)
            hpos = sbx(stS, "hposs", [128, 2, 2])
            for kv in range(2):
                for p0 in range(0, 32, 8):
                    S.dma('pool', out=w1[kv][:, p0:p0 + 8, :], in_=w1_d[kv][p0:p0 + 8].rearrange("p d f -> d p f"))
                S.dma('pool', out=w2[kv], in_=w2_d[kv].rearrange("(c p) d -> p c d", p=128))
            S.dma('sp', out=posf, in_=posT_d)
            S.dve.tensor_copy(out=posb, in_=posf)
            E_all = sbx(stS, "E_all", [128, 64, 128], BF16)
            for e0 in range(0, 64, 4):
                S.dma('pool', out=E_all[:, e0:e0 + 4, :], in_=Eall_d[:, e0:e0 + 4, :])
            cov = sbx(stS, "cov_s", [128, 8, 257], BF16)
            for g0 in range(0, 8, 4):
                S.dma('pool', out=cov[:, g0:g0 + 4, :], in_=covs_d[:, g0:g0 + 4, :])
            selcs = sbx(stS, "selcs", [16, 3, 257])
            S.dma('sp', out=selcs, in_=selcs_d)
            msm = sbx(stS, "msm", [128, 16])
            msmb = sbx(stS, "msmb", [128, 16], BF16)
            S.dma('sp', out=msm, in_=msm_d)
            S.dve.tensor_copy(out=msmb, in_=msm)
            r8 = sbx(stS, "r8", [8, 32])
            gsel = sbx(stS, "gsel", [32, 2, 24])
            s16 = sbx(stS, "s16", [32, 2, 16])
            S.dma('sp', out=r8, in_=r8_d)
            S.dma('sp', out=gsel, in_=gsel_d)
            S.dma('sp', out=s16, in_=s16_d)
            iof = sbx(stS, "iof", [128, 1])
            S.pool.iota(iof, pattern=[[0, 1]], base=0, channel_multiplier=2, allow_small_or_imprecise_dtypes=True)
            pt_i = sbx(stS, "pt_i", [128, NPG], I32)
            pt_f = sbx(stS, "pt_f", [128, NPG])
            idxc = sbx(stS, "idxc", [128, NPG], I32)
            idxs = sbx(stS, "idxs", [128, NPG], I32)
            xg = [sbx(stS, "xg", [128, 4, 16 + 2048], BF16) for _ in range(2)]
            NPB = 24
            pg = [sbx(stS, "pg", [128, 512], BF16) for _ in range(NPB)]
            ksT = [sbx(stS, "ksT", [128, 2, 128], BF16) for _ in range(8)]
            xs = [sbx(stS, "xs", [128, 16, 129], BF16) for _ in range(2)]
            hx2 = [sbx(stS, "hx2", [128, 256]) for _ in range(2)]
            hu2 = [sbx(stS, "hu2", [128, 256]) for _ in range(2)]
            hb2 = [sbx(stS, "hb2", [128, 256], BF16) for _ in range(2)]
            hTs = [sbx(stS, "hTs", [128, 2, 128], BF16) for _ in range(2)]
            posrep = sbx(stS, "posrep", [128, 32, 128], BF16)
            hposB = sbx(stS, "hposB", [128, 2, 256])
            kcT = sbx(stS, "kcTs", [128, NKV, NG * 128], BF16)
            vc = sbx(stS, "vcs", [128, NG, NKV, 128], BF16)
            qs = sbx(stS, "qs", [128, NH * DEC], BF16)
            knew = {k: sbx(stS, "knew" + k, [128, NKV, DEC], BF16) for k in ("slc", "win")}
            vnew = {k: sbx(stS, "vnew" + k, [DEC, NKV * HD], BF16) for k in ("slc", "win")}
            g8 = sbx(stS, "g8", [DEC, 24])
            es_c = sbx(stS, "es_c", [128, 512], BF16)
            es_s = [sbx(stS, "es_s", [128, 256], BF16) for _ in range(2)]
            pt_s = [sbx(stS, "pt_s", [128, 256], BF16) for _ in range(2)]
            mk = [sbx(stS, "mks", [128, 64], BF16) for _ in range(2)]
            es_n = sbx(stS, "es_n", [DEC, 64], BF16)
            pt_n = sbx(stS, "pt_n", [DEC, 64], BF16)
            impu = [sbx(stS, "impu", [32, 257]) for _ in range(2)]
            wsum = [sbx(stS, "wsum", [32, 16]) for _ in range(2)]
            prio = sbx(stS, "prios", [16, 257])
            rank = sbx(stS, "ranks", [16, 257])
            cmpm = sbx(stS, "cmpms", [16, 16, 257])
            selN = sbx(stS, "selNs", [128, 3, 16], BF16)
            cw_f = sbx(stS, "cw_f", [128, 4, 512])
            cw_b = sbx(stS, "cw_b", [128, 4, 512], BF16)
            dens = sbx(stS, "denss", [32, 6])
            gm = sbx(stS, "gm", [32, 24])
            gate = sbx(stS, "gate", [32, 2, 3])
            coef = sbx(stS, "coefs", [32, 2, 3])
            ob = sbx(stS, "obs", [32, 128])
            mixs = sbx(stS, "mixs", [32, NKV, 128], BF16)
            pT_ = [psx(stS, "sT", [128, 1024], BF16) for _ in range(2)]
            pH = [psx(stS, "sH", [128, 512]) for _ in range(2)]
            pX = psx(stS, "sX", [128, 512])
            pM = psx(stS, "sM", [128, 512])
            pOa = psx(stS, "sOa", [128, 512])
            pOb = psx(stS, "sOb", [128, 512])
            for kv in range(2):
                S.dve.tensor_copy(out=posrep, in_=posb[:, kv, :].unsqueeze(2).to_broadcast([128, 32, 128]))
                for p_ in range(32):
                    S.pe.matmul(pX[:, 0:256], lhsT=posrep[:, p_, :], rhs=w1[kv][:, p_, :], start=(p_ == 0), stop=(p_ == 31))
                S.act.copy(out=hposB[:, kv, :], in_=pX[:, 0:256])
            rows_v = cache_d.rearrange("n r (k2 k1) h d -> (n r k2) (k1 h d)", k2=2)
            tcnt = [0]

            for sq_ in range(cfg.n_sseq):
                tb = cfg.ntp + sq_ * DEC
                started = set()

                def st_flag(bank):
                    if bank in started:
                        return False
                    started.add(bank)
                    return True

                S.dma('sp', out=pt_i, in_=ptab_d[sq_:sq_ + 1, :].to_broadcast([128, NPG]))
                S.dve.tensor_copy(out=pt_f, in_=pt_i)
                S.dve.tensor_scalar(out=pt_f, in0=pt_f, scalar1=256.0, scalar2=iof[:, 0:1], op0=ALU.mult, op1=ALU.add)
                S.dve.tensor_copy(out=idxc, in_=pt_f)
                S.dve.tensor_scalar(out=pt_f, in0=pt_f, scalar1=1.0, scalar2=None, op0=ALU.add)
                S.dve.tensor_copy(out=idxs, in_=pt_f)
                S.dma('sp', out=qs.rearrange("p (h i) -> p h i", h=NH), in_=nqT[:, :, tb:tb + DEC])
                S.dma('sp', out=knew["slc"], in_=kslcT[:, :, tb:tb + DEC])
                S.dma('sp', out=knew["win"], in_=kwinT[:, :, tb:tb + DEC])
                S.dma('sp', out=vnew["slc"], in_=vslc_tok[tb:tb + DEC, :])
                S.dma('sp', out=vnew["win"], in_=vwin_tok[tb:tb + DEC, :])
                S.dma('sp', out=g8, in_=gates_d[tb:tb + DEC, :])
                S.dma('sp', out=cw_f, in_=cwin_d[sq_].rearrange("(t p) n -> p t n", p=128))
                S.dve.tensor_copy(out=cw_b, in_=cw_f)

                for grp in range(NG):
                    xgc = xg[grp % 2]
                    if grp == 0:
                        S.dve.memset(xgc[:, :, 0:16], 0.0)
                    if grp > 0:
                        S.dve.tensor_copy(out=xgc[:, :, 0:16], in_=xg[(grp - 1) % 2][:, :, 2048:2064])
                    for pl in range(16):
                        p = grp * 16 + pl
                        pgt = pg[p % NPB]
                        S.pool.indirect_dma_start(out=pgt, out_offset=None, in_=rows_v,
                                                  in_offset=bass.IndirectOffsetOnAxis(ap=idxc[:, p:p + 1], axis=0),
                                                  R=[idxc.name], W=[pgt.name])
                        pt_ = pT_[tcnt[0] % 2]
                        tcnt[0] += 1
                        for k in range(4):
                            S.pe.transpose(pt_[:, k * 128:(k + 1) * 128], pgt[:, k * 128:(k + 1) * 128], identb)
                        S.act.copy(out=xgc[:, :, 16 + pl * 128:16 + (pl + 1) * 128],
                                   in_=pt_[:, 0:512].rearrange("p (k n) -> p k n", k=4))
                    for kh in range(4):
                        kv, h = kh // 2, kh % 2
                        xsb = xs[kh % 2]
                        cp_eng = S.pool if kh % 2 == 0 else S.act
                        if kh % 2 == 0:
                            S.dve.tensor_copy(out=xsb, in_=xgc[:, kh, :].rearrange("p (n s) -> p s n", s=16))
                        else:
                            S.act.copy(out=xsb, in_=xgc[:, kh, :].rearrange("p (n s) -> p s n", s=16))
                        ps_ = pH[kh % 2]
                        i = 0
                        for r in range(2):
                            for s_ in range(16):
                                S.pe.matmul(ps_[:, :256], lhsT=xsb[:, s_, r:r + 128], rhs=w1[kv][:, r * 16 + s_, :],
                                            start=(i == 0), stop=(i == 31))
                                i += 1
                        x_, u_ = hx2[kh % 2], hu2[kh % 2]
                        S.dve.tensor_tensor(out=x_, in0=ps_[:, :256], in1=hposB[:, kv, :], op=ALU.add)
                        S.dve.tensor_tensor(out=u_, in0=x_, in1=x_, op=ALU.mult)
                        S.dve.tensor_scalar(out=u_, in0=u_, scalar1=0.044715, scalar2=1.0, op0=ALU.mult, op1=ALU.add)
                        S.dve.tensor_tensor(out=u_, in0=u_, in1=x_, op=ALU.mult)
                        S.act.activation(out=u_, in_=u_, func=AF.Tanh, scale=0.7978845608028654)
                        S.dve.tensor_scalar(out=u_, in0=u_, scalar1=1.0, scalar2=0.5, op0=ALU.add, op1=ALU.mult)
                        S.dve.tensor_tensor(out=hb2[kh % 2], in0=u_, in1=x_, op=ALU.mult)
                        pt_ = pT_[tcnt[0] % 2]
                        tcnt[0] += 1
                        for fcn in range(2):
                            S.pe.transpose(pt_[:, fcn * 128:(fcn + 1) * 128], hb2[kh % 2][:, fcn * 128:(fcn + 1) * 128], identb)
                        hT2 = hTs[kh % 2]
                        S.act.copy(out=hT2, in_=pt_[:, 0:256].rearrange("p (k n) -> p k n", k=2))
                        if kv == 0:
                            for fcn in range(2):
                                S.pe.matmul(pX[:, :128], lhsT=w2[0][:, fcn, :], rhs=hT2[:, fcn, :], start=(fcn == 0), stop=(fcn == 1))
                            S.act.copy(out=kcT[:, h, grp * 128:(grp + 1) * 128], in_=pX[:, :128])
                        else:
                            for fcn in range(2):
                                S.pe.matmul(pX[:, :128], lhsT=hT2[:, fcn, :], rhs=w2[1][:, fcn, :], start=(fcn == 0), stop=(fcn == 1))
                            S.act.copy(out=vc[:, grp, h, :], in_=pX[:, :128])

                first = True
                for g in range(NG):
                    for h in range(NKV):
                        c0 = (g * 2 + h) * 32
                        S.pe.matmul(pX[:, c0:c0 + 32], lhsT=kcT[:, h, g * 128:(g + 1) * 128], rhs=qs[:, h * 32:(h + 1) * 32],
                                    start=first, stop=True)
                        first = False
                S.act.activation(out=es_c, in_=pX, func=AF.Exp, scale=SCALE)
                S.dve.memset(es_c[0:1, 0:64], 0.0)
                for g in range(NG):
                    for h in range(NKV):
                        l_ = es_c[:, (g * 2 + h) * 32:(g * 2 + h) * 32 + 32]
                        S.pe.matmul(pOa[:32, h * 128:(h + 1) * 128], lhsT=l_, rhs=vc[:, g, h, :], start=st_flag("pOa"), stop=(g == NG - 1))
                        S.pe.matmul(pOb[:32, 256 + h:257 + h], lhsT=l_, rhs=ones_b[:, 0:1], start=st_flag("pOb"), stop=(g == NG - 1))
                        S.pe.matmul(pH[h][:32, :257], lhsT=l_, rhs=cov[:, g, :], start=(g == 0), stop=(g == NG - 1))
                S.dve.tensor_scalar(out=dens[:, 0:2], in0=pOb[:32, 256:258], scalar1=1e-30, scalar2=None, op0=ALU.max)
                S.dve.reciprocal(out=dens[:, 0:2], in_=dens[:, 0:2])
                for h in range(NKV):
                    S.act.copy(out=impu[h], in_=pH[h][:32, :257])
                    S.dve.tensor_scalar(out=wsum[h], in0=s16[:, h, :], scalar1=dens[:, h:h + 1], scalar2=None, op0=ALU.mult)
                for h in range(NKV):
                    S.pe.matmul(pM[:16, :257], lhsT=wsum[h], rhs=impu[h], start=(h == 0), stop=(h == 1))
                S.dve.tensor_tensor(out=prio, in0=pM[:16, :257], in1=selcs[:, 0, :], op=ALU.mult)
                S.dve.tensor_tensor(out=prio, in0=prio, in1=selcs[:, 1, :], op=ALU.add)
                for n0 in range(0, 257, 16):
                    nn = min(16, 257 - n0)
                    S.dve.tensor_tensor(out=cmpm[:, :nn, :], in0=prio.unsqueeze(1).to_broadcast([16, nn, 257]),
                                        in1=prio[:, n0:n0 + nn].unsqueeze(2).to_broadcast([16, nn, 257]), op=ALU.is_gt)
                    S.dve.reduce_sum(out=rank[:, n0:n0 + nn], in_=cmpm[:, :nn, :], axis=AX.X)
                S.dve.tensor_scalar(out=rank, in0=rank, scalar1=15.5, scalar2=None, op0=ALU.is_lt)
                S.dve.tensor_tensor(out=rank, in0=rank, in1=selcs[:, 2, :], op=ALU.mult)
                for t_ in range(3):
                    nb = 128 if t_ < 2 else 1
                    S.pe.transpose(pM[:nb, 272 + t_ * 16:272 + (t_ + 1) * 16], rank[:, t_ * 128:t_ * 128 + nb], ident[:16, :16])
                S.dve.memset(selN, 0.0)
                S.dve.tensor_copy(out=selN[:, 0:2, :], in_=pM[:, 272:304].rearrange("p (t n) -> p t n", t=2))
                S.dve.tensor_copy(out=selN[0:1, 2, :], in_=pM[0:1, 304:320])

                for p4 in range(NPG // 4):
                    q2 = p4 % 2
                    for pl in range(4):
                        p = p4 * 4 + pl
                        pgt, kst = pg[p % NPB], ksT[p % 8]
                        S.pool.indirect_dma_start(out=pgt, out_offset=None, in_=rows_v,
                                                  in_offset=bass.IndirectOffsetOnAxis(ap=idxs[:, p:p + 1], axis=0),
                                                  R=[idxs.name], W=[pgt.name])
                        pt_ = pT_[tcnt[0] % 2]
                        tcnt[0] += 1
                        for k in range(2):
                            S.pe.transpose(pt_[:, k * 128:(k + 1) * 128], pgt[:, k * 128:(k + 1) * 128], identb)
                        S.act.copy(out=kst, in_=pt_[:, 0:256].rearrange("p (k n) -> p k n", k=2))
                    for pl in range(4):
                        p = p4 * 4 + pl
                        for h in range(NKV):
                            c0 = (pl * 2 + h) * 32
                            S.pe.matmul(pX[:, c0:c0 + 32], lhsT=ksT[p % 8][:, h, :], rhs=qs[:, h * 32:(h + 1) * 32],
                                        start=(pl == 0 and h == 0), stop=True)
                        S.pe.matmul(pM[:, pl * 16:(pl + 1) * 16], lhsT=E_all[:, p % 64, :], rhs=selN[:, p // 64, :],
                                    start=(pl == 0), stop=True)
                    S.act.activation(out=es_s[q2], in_=pX[:, 0:256], func=AF.Exp, scale=SCALE)
                    S.dve.tensor_copy(out=mk[q2], in_=pM[:, 0:64])
                    S.dve.tensor_tensor(out=pt_s[q2].rearrange("p (a g i) -> p a g i", a=8, g=4),
                                        in0=es_s[q2].rearrange("p (a g i) -> p a g i", a=8, g=4),
                                        in1=mk[q2].rearrange("p (a i) -> p a i", a=8).unsqueeze(2).to_broadcast([128, 8, 4, 8]),
                                        op=ALU.mult)
                    for pl in range(4):
                        p = p4 * 4 + pl
                        for h in range(NKV):
                            l_ = pt_s[q2][:, (pl * 2 + h) * 32:(pl * 2 + h) * 32 + 32]
                            S.pe.matmul(pOa[:32, 256 + h * 128:256 + (h + 1) * 128], lhsT=l_, rhs=pg[p % NPB][:, 256 + h * 128:256 + (h + 1) * 128],
                                        start=st_flag("pOa"), stop=False)
                            S.pe.matmul(pOb[:32, 258 + h:259 + h], lhsT=l_, rhs=ones_b[:, 0:1], start=st_flag("pOb"), stop=False)

                def new_rows(kind, po_ap_fn, den_col):
                    for h in range(NKV):
                        S.pe.matmul(pX[:DEC, h * 32:(h + 1) * 32], lhsT=knew[kind][:, h, :], rhs=qs[:, h * 32:(h + 1) * 32],
                                    start=(h == 0), stop=True)
                    S.act.activation(out=es_n, in_=pX[:DEC, 0:64], func=AF.Exp, scale=SCALE)
                    S.dve.tensor_tensor(out=pt_n.rearrange("p (a i) -> p a i", a=8), in0=es_n.rearrange("p (a i) -> p a i", a=8),
                                        in1=msmb[:DEC, 8:16].unsqueeze(1).to_broadcast([DEC, 8, 8]), op=ALU.mult)
                    for h in range(NKV):
                        l_ = pt_n[:, h * 32:(h + 1) * 32]
                        S.pe.matmul(po_ap_fn(h), lhsT=l_, rhs=vnew[kind][:, h * 128:(h + 1) * 128], start=False, stop=True)
                        S.pe.matmul(pOb[:32, den_col + h:den_col + h + 1], lhsT=l_, rhs=ones_b[:DEC, 0:1], start=False, stop=True)

                new_rows("slc", lambda h: pOa[:32, 256 + h * 128:256 + (h + 1) * 128], 258)

                for t_ in range(4):
                    pt_ = pT_[tcnt[0] % 2]
                    tcnt[0] += 1
                    for k in range(2):
                        S.pe.transpose(pt_[:, k * 128:(k + 1) * 128], cw_b[:, t_, k * 128:(k + 1) * 128], identb)
                    S.act.copy(out=ksT[t_], in_=pt_[:, 0:256].rearrange("p (k n) -> p k n", k=2))
                for t_ in range(4):
                    for h in range(NKV):
                        c0 = (t_ * 2 + h) * 32
                        S.pe.matmul(pX[:, c0:c0 + 32], lhsT=ksT[t_][:, h, :], rhs=qs[:, h * 32:(h + 1) * 32],
                                    start=(t_ == 0 and h == 0), stop=True)
                S.act.activation(out=es_s[0], in_=pX[:, 0:256], func=AF.Exp, scale=SCALE)
                S.dve.tensor_tensor(out=es_s[0][:, 0:64].rearrange("p (a i) -> p a i", a=8),
                                    in0=es_s[0][:, 0:64].rearrange("p (a i) -> p a i", a=8),
                                    in1=msmb[:, 0:8].unsqueeze(1).to_broadcast([128, 8, 8]), op=ALU.mult)
                for t_ in range(4):
                    for h in range(NKV):
                        l_ = es_s[0][:, (t_ * 2 + h) * 32:(t_ * 2 + h) * 32 + 32]
                        S.pe.matmul(pOb[:32, h * 128:(h + 1) * 128], lhsT=l_, rhs=cw_b[:, t_, 256 + h * 128:256 + (h + 1) * 128],
                                    start=st_flag("pOb"), stop=False)
                        S.pe.matmul(pOb[:32, 260 + h:261 + h], lhsT=l_, rhs=ones_b[:, 0:1], start=False, stop=False)
                new_rows("win", lambda h: pOb[:32, h * 128:(h + 1) * 128], 260)

                S.dve.tensor_scalar(out=dens, in0=pOb[:32, 256:262], scalar1=1e-30, scalar2=None, op0=ALU.max)
                S.dve.reciprocal(out=dens, in_=dens)
                S.pe.matmul(pM[:32, 0:24], lhsT=r8, rhs=g8, start=True, stop=True)
                for h in range(NKV):
                    S.dve.tensor_tensor(out=gm, in0=pM[:32, 0:24], in1=gsel[:, h, :], op=ALU.mult)
                    S.dve.reduce_sum(out=gate[:, h, :], in_=gm.rearrange("p (b e) -> p b e", b=3), axis=AX.X)
                    for br in range(3):
                        S.dve.tensor_tensor(out=coef[:, h, br:br + 1], in0=gate[:, h, br:br + 1], in1=dens[:, br * 2 + h:br * 2 + h + 1], op=ALU.mult)
                    S.dve.tensor_scalar(out=ob, in0=pOa[:32, h * 128:(h + 1) * 128], scalar1=coef[:, h, 0:1], scalar2=None, op0=ALU.mult)
                    S.dve.scalar_tensor_tensor(out=ob, in0=pOa[:32, 256 + h * 128:256 + (h + 1) * 128], scalar=coef[:, h, 1:2], in1=ob,
                                               op0=ALU.mult, op1=ALU.add)
                    S.dve.scalar_tensor_tensor(out=mixs[:, h, :], in0=pOb[:32, h * 128:(h + 1) * 128], scalar=coef[:, h, 2:3], in1=ob,
                                               op0=ALU.mult, op1=ALU.add)
                mv = mixed[tb:tb + DEC, RH * RDV:D].rearrange("i (h g d) -> i h g d", h=NKV, g=4)
                for g in range(4):
                    S.dma('sp', out=mv[:, :, g, :], in_=mixs[g * DEC:(g + 1) * DEC, :, :])
            S.flush()

    if cfg.do_mixer:
        retention_phase()
        nsa_prompt_phase()
        if cfg.do_sample and cfg.n_sseq:
            nsa_sample_phase()

    with ExitStack() as stD:
        b = dense_bufs(stD)
        mtok = [sbx(stD, "mtok", [128, 1024], BF16) for _ in range(2)]
        for (t0, TW) in cfg.tiles:
            nsub = (TW + 127) // 128
            for s in range(nsub):
                rows = min(128, TW - s * 128)
                for hf in range(2):
                    mt = mtok[hf]
                    S.dma('sp', out=mt[:rows, :], in_=mixed[t0 + s * 128:t0 + s * 128 + rows, hf * 1024:(hf + 1) * 1024])
                    pt = b.psT[hf]
                    for k in range(8):
                        S.pe.transpose(pt[:, k * 128:k * 128 + rows], mt[:rows, k * 128:(k + 1) * 128], identb[:rows, :rows])
                    S.act.copy(out=b.gT[:, hf * 8:(hf + 1) * 8, s * 128:s * 128 + rows],
                               in_=pt.rearrange("p (k n) -> p k n", k=8)[:, :, :rows])
            S.dma('sp', out=b.xf[:, :, :TW], in_=x1T[:, :, t0:t0 + TW])
            for g8 in range(8):
                wbk = b.wa[g8 % 2]
                S.dma('sp', out=wbk, in_=wout_b[g8], nosplit=True)
                for k in range(2):
                    dc = g8 * 2 + k
                    pa = b.psA[dc % 2]
                    for kc in range(KC):
                        S.pe.matmul(pa[:, :TW], lhsT=wbk[:, kc, k * 128:(k + 1) * 128], rhs=b.gT[:, kc, :TW],
                                    start=(kc == 0), stop=(kc == KC - 1))
                    S.dve.tensor_tensor(out=b.xf[:, dc, :TW], in0=b.xf[:, dc, :TW], in1=pa[:, :TW], op=ALU.add)
            layer_norm(b, 1, TW, True)
            ffn(b, 1, TW)
            layer_norm(b, 2, TW, False)
            for s in range(nsub):
                rows = min(128, TW - s * 128)
                for hf in range(2):
                    yt = b.xtok[hf]
                    for c4 in range(2):
                        pa = b.psA[c4 % 2]
                        for k in range(4):
                            c = hf * 8 + c4 * 4 + k
                            S.pe.transpose(pa[:rows, k * 128:(k + 1) * 128], b.xf[:, c, s * 128:s * 128 + rows], ident)
                        S.act.copy(out=yt[:rows, c4 * 512:(c4 + 1) * 512], in_=pa[:rows, :])
                    S.dma('sp', out=y_out[t0 + s * 128:t0 + s * 128 + rows, hf * 1024:(hf + 1) * 1024], in_=yt[:rows, :])
        S.flush()
    n = S.finish()
    return nc, n


def _rope_table(positions):
    half = 64
    inv = (np.float32(10000.0) ** (-np.arange(half, dtype=np.float32) / np.float32(half))).astype(np.float32)
    ang = positions.astype(np.float32)[:, None] * inv[None, :]
    return np.concatenate([np.cos(ang), np.sin(ang)], axis=1).astype(np.float32)


def _lnp(ins):
    rows = [ins['ln1_g'][0], ins['ln1_b'][0], ins['ln2_g'][0], ins['ln2_b'][0], ins['ln3_g'][0], ins['ln3_b'][0]]
    a = np.stack([np.asarray(r).reshape(KC, 128).T for r in rows], axis=1)
    return np.ascontiguousarray(a.astype(np.float32))


def _ret_consts(C):
    out = np.zeros((128, 3, RH, 128), np.float32)
    i = np.arange(C, dtype=np.float32)
    for h in range(RH):
        lg = np.float32(LG[h])
        diff = i[None, :] - i[:, None]
        out[:C, 0, h, :C] = np.where(diff >= 0, np.exp(lg * np.maximum(diff, 0.0)), 0.0)
        out[:, 1, h, :C] = np.exp(lg * (i + 1.0))[None, :]
        out[:C, 2, h, 0] = np.exp(lg * (C - 1.0 - i))
    return out


def _nsa_consts(seq):
    nqt, ncmp, nslc = seq // 128, seq // 16 - 1, seq // 64
    t = np.arange(seq)
    c = np.arange(128)
    mcmp = ((c[:, None] * 16 + 31 <= t[None, :]) & (c[:, None] < ncmp)).astype(np.float32)
    ci = np.arange(128)[:, None]
    ni = np.arange(32)[None, :]
    cover = np.clip(np.minimum(ci * 16 + 32, (ni + 1) * 64) - np.maximum(ci * 16, ni * 64), 0, None).astype(np.float32) / 32.0
    cover[ncmp:, :] = 0
    cover[:, nslc:] = 0
    j = np.arange(128)[:, None]
    i = np.arange(128)[None, :]
    mtri = np.stack([(j <= i), (j > i)], axis=1).astype(np.float32)
    selc = np.zeros((128, nqt, 3, 32), np.float32)
    n = np.arange(32)
    for qt in range(nqt):
        for p in range(128):
            tt = qt * 128 + p
            cur = tt // 64
            valid = (n <= cur) & (n < nslc)
            forced = (n == 0) | (n == cur) | (n == cur - 1)
            selc[p, qt, 0] = (valid & ~forced).astype(np.float32)
            bonus = np.where(forced & valid, 3e30 - n * 1e28, np.where(valid, 0.0, -1e30 - n * 1e27))
            selc[p, qt, 1] = bonus.astype(np.float32)
            selc[p, qt, 2] = valid.astype(np.float32)
    E = np.zeros((32, nqt, 128), np.float32)
    for kt in range(nqt):
        for p in range(128):
            E[2 * kt + p // 64, kt, p] = 1.0
    return mcmp, cover, mtri, selc, E


def _nsa_sample_consts():
    nslc, ncmp = 257, 1023
    cover = np.zeros((128, 8, 257), np.float32)
    n = np.arange(257)
    for g in range(8):
        for j in range(128):
            c = 128 * g - 1 + j
            if c < 0 or c >= ncmp:
                continue
            cover[j, g] = np.clip(np.minimum(c * 16 + 32, (n + 1) * 64) - np.maximum(c * 16, n * 64), 0, None) / 32.0
    selc = np.zeros((16, 3, 257), np.float32)
    for h in range(2):
        for i in range(DEC):
            t = PAST + i
            cur = t // 64
            valid = n <= cur
            forced = (n == 0) | (n == cur) | (n == cur - 1)
            selc[h * 8 + i, 0] = (valid & ~forced)
            selc[h * 8 + i, 1] = np.where(forced & valid, 3e30 - n * 1e27, np.where(valid, 0.0, -1e30 - n * 1e26))
            selc[h * 8 + i, 2] = valid
    E = np.zeros((128, 64, 128), np.float32)
    for e in range(64):
        for key in range(128):
            E[2 * e + key // 64, e, key] = 1.0
    msm = np.zeros((128, 16), np.float32)
    j = np.arange(128)[:, None]
    i = np.arange(8)[None, :]
    msm[:, 0:8] = (j >= i + 1)
    msm[:8, 8:16] = (np.arange(8)[:, None] <= i)
    r8 = np.zeros((8, 32), np.float32)
    gsel = np.zeros((32, 2, 24), np.float32)
    s16 = np.zeros((32, 2, 16), np.float32)
    for g in range(4):
        for ii in range(8):
            r8[ii, g * 8 + ii] = 1.0
            for h in range(2):
                s16[g * 8 + ii, h, h * 8 + ii] = 1.0
                for br in range(3):
                    gsel[g * 8 + ii, h, br * 8 + 4 * h + g] = 1.0
    return cover, selc, E, msm, r8, gsel, s16


def _perm_gate_cols(w_in):
    w = np.array(w_in, copy=True)
    g0 = NIN - 24
    idx = np.array([h * 3 + br for br in range(3) for h in range(NH)])
    w[:, g0:] = w_in[:, g0 + idx]
    return w


def make_in_maps(ins, cfg, n_cores, pseq_per_core=2, sseq_per_core=4):
    seq = cfg.seq
    pos = np.concatenate([np.tile(np.arange(seq), cfg.n_pseq), np.tile(PAST + np.arange(DEC), cfg.n_sseq)])
    rope = _rope_table(pos)
    lnp = _lnp(ins)
    ident = np.eye(128, dtype=np.float32)
    gn = np.ascontiguousarray(np.broadcast_to(np.stack([ins['ret_gn_g'][0], ins['ret_gn_b'][0]])[None], (128, 2, RH * RDV))).astype(np.float32)
    rc128, rc8 = _ret_consts(128), _ret_consts(8)
    mcmp, cover, mtri, selc, E = _nsa_consts(seq)
    posT = np.ascontiguousarray(np.stack([np.asarray(ins['cmp_pos_k'][0]).T, np.asarray(ins['cmp_pos_v'][0]).T], axis=1)).astype(np.float32)
    w_in_p = _perm_gate_cols(np.asarray(ins['w_in'][0]))
    cov_s, selc_s, E_all, msm, r8, gsel, s16 = _nsa_sample_consts()
    cache = np.asarray(ins['cache_nsa_kv'][0])
    ptab = np.asarray(ins['page_table']).astype(np.int32)
    in_maps = []
    for c in range(n_cores):
        xp = np.asarray(ins['x_prompt'][cfg.n_pseq * c:cfg.n_pseq * (c + 1)]).reshape(-1, D)
        xs = np.asarray(ins['x_sample'][cfg.n_sseq * c:cfg.n_sseq * (c + 1)]).reshape(-1, D)
        m = {
            'xin': np.ascontiguousarray(np.concatenate([xp, xs], 0)),
            'ffn1_w_up': ins['ffn1_w_up'][0], 'ffn2_w_up': ins['ffn2_w_up'][0],
            'ffn1_w_down': ins['ffn1_w_down'][0], 'ffn2_w_down': ins['ffn2_w_down'][0],
            'w_in': w_in_p, 'w_out': ins['w_out'][0],
            'lnp': lnp, 'ident': ident, 'rope': rope, 'gn': gn, 'rc128': rc128, 'rc8': rc8,
            'cmp_w1_k': ins['cmp_w1_k'][0], 'cmp_w1_v': ins['cmp_w1_v'][0],
            'cmp_w2_k': ins['cmp_w2_k'][0], 'cmp_w2_v': ins['cmp_w2_v'][0],
            'cmp_posT': posT, 'mcmp': mcmp, 'cover': cover, 'mtri': mtri, 'selc': selc, 'Eexp': E,
            'state': np.ascontiguousarray(ins['state_ret'][0][cfg.n_sseq * c:cfg.n_sseq * (c + 1)]),
            'cache': cache, 'ptab': np.ascontiguousarray(ptab[cfg.n_sseq * c:cfg.n_sseq * (c + 1)]),
            'cover_s': cov_s, 'selc_s': selc_s, 'E_all': E_all, 'msmall': msm, 'r8': r8, 'gsel': gsel, 'sel16': s16,
            'cwin': np.ascontiguousarray(np.asarray(ins['cache_win'][0][cfg.n_sseq * c:cfg.n_sseq * (c + 1)]).reshape(cfg.n_sseq, WINDOW, 512)),
        }
        in_maps.append(m)
    return in_maps


def kernel(**ins):
    cfg = Cfg()
    nc, _ = build_program(cfg)
    in_maps = make_in_maps(ins, cfg, N_CORES)
    res = run_bass_kernel_spmd(nc, in_maps, core_ids=list(range(N_CORES)))
    R = res.results
    ntp = cfg.ntp
    f = np.float32
    y_p = np.stack([R[c]['y'][:ntp].reshape(2, SEQ, D) for c in range(N_CORES)]).reshape(NB_P, SEQ, D).astype(f)
    y_s = np.stack([R[c]['y'][ntp:].reshape(4, DEC, D) for c in range(N_CORES)]).reshape(NB_S, DEC, D).astype(f)
    kv_p = np.stack([R[c]['kvrows'][:ntp].reshape(2, SEQ, 4, NKV, HD) for c in range(N_CORES)]).reshape(1, NB_P, SEQ, 4, NKV, HD).astype(f)
    kv_s = np.stack([R[c]['kvrows'][ntp:].reshape(4, DEC, 4, NKV, HD) for c in range(N_CORES)]).reshape(1, NB_S, DEC, 4, NKV, HD).astype(f)
    win_p = np.stack([R[c]['winp'].reshape(2, WINDOW, 2, NKV, HD) for c in range(N_CORES)]).reshape(1, NB_P, WINDOW, 2, NKV, HD).astype(f)
    win_s = np.stack([R[c]['wins'].reshape(4, WINDOW, 2, NKV, HD) for c in range(N_CORES)]).reshape(1, NB_S, WINDOW, 2, NKV, HD).astype(f)
    rs_p = np.stack([R[c]['rs'][:2] for c in range(N_CORES)]).reshape(1, NB_P, RH, RDK, RDV).astype(f)
    rs_s = np.stack([R[c]['rs'][2:] for c in range(N_CORES)]).reshape(1, NB_S, RH, RDK, RDV).astype(f)
    return (y_p, y_s, rs_p, rs_s, kv_p, kv_s, win_p, win_s)
```

```python
import math
import numpy as np
import concourse.bass as bass
import concourse.mybir as mybir
from concourse.bass_utils import run_bass_kernel_spmd

F32 = mybir.dt.float32
BF16 = mybir.dt.bfloat16
I32 = mybir.dt.int32
AF = mybir.ActivationFunctionType
ALU = mybir.AluOpType
AX = mybir.AxisListType

D = 2048
KC = 16
DFF = 5632
FC = 44
SEQ = 2048
NB_P = 16
NB_S = 32
DEC = 8
PAST = 16384
PAGE = 128
RH, RDK, RDV = 4, 128, 256
NH, NKV, HD = 8, 2, 128
NIN = 5656
ALPHA = 2.0 ** 0.25
EPS = 1e-5
WINDOW = 512
N_CORES = 8

_ENG_ATTR = {'pe': 'tensor', 'dve': 'vector', 'act': 'scalar', 'pool': 'gpsimd', 'sp': 'sync'}
_DMA_METHODS = ('dma_start', 'indirect_dma_start', 'dma_start_transpose')
SEM_ROTATE = 30000


def _keys_of(x):
    if x is None or isinstance(x, (int, float, str, bool)):
        return []
    nm = getattr(x, 'name', None)
    if isinstance(nm, str) and hasattr(x, 'ap'):
        return [nm]
    return []


class _Op:
    __slots__ = ('eng', 'meth', 'args', 'kw', 'R', 'W', 'is_dma', 'deps', 'signal', 'sem', 'val', 'idx', 'fn')


class _Proxy:
    def __init__(self, sched, eng):
        self._s = sched
        self._e = eng

    def __getattr__(self, meth):
        def rec(*args, R=None, W=None, **kw):
            return self._s._record(self._e, meth, args, kw, R, W)
        return rec


class Sched:
    def __init__(self, nc, same_engine_sync=True, n_dma_sems=24):
        self.nc = nc
        self.ops = []
        self.same_engine_sync = same_engine_sync
        self.n_dma_sems = n_dma_sems
        self.dram_names = set()
        for e in _ENG_ATTR:
            setattr(self, e, _Proxy(self, e))
        self.eng_obj = {e: getattr(nc, a) for e, a in _ENG_ATTR.items()}
        self.eng_sem = {}
        self.eng_cnt = {}
        self.sem_id = 0
        self.dma_sems = {}
        self.dma_rr = {}
        self.known = {e: {} for e in _ENG_ATTR}
        self.sems = {}
        self.pending_barrier = {}
        self.total_ops = 0

    def _new_sem(self, tag):
        self.sem_id += 1
        s = self.nc.alloc_semaphore(f"s_{tag}_{self.sem_id}")
        self.sems[id(s)] = s
        return s

    def _record(self, eng, meth, args, kw, R, W):
        op = _Op()
        op.eng, op.meth, op.args, op.kw = eng, meth, args, kw
        op.is_dma = meth in _DMA_METHODS
        if W is None:
            W = []
            if 'out' in kw:
                W += _keys_of(kw['out'])
            elif args:
                W += _keys_of(args[0])
            if kw.get('accum_out') is not None:
                W += _keys_of(kw['accum_out'])
        if R is None:
            R = []
            first_is_out = 'out' not in kw
            for i, a in enumerate(args):
                if i == 0 and first_is_out:
                    continue
                R += _keys_of(a)
            for k, v in kw.items():
                if k in ('out', 'accum_out'):
                    continue
                R += _keys_of(v)
        op.R = [k for k in R if k not in self.dram_names]
        op.W = [k for k in W if k not in self.dram_names]
        op.idx = len(self.ops)
        self.ops.append(op)
        return op

    def dma(self, eng, out, in_, max_desc=512, nosplit=False):
        shp = tuple(out.shape)
        if (not nosplit) and len(shp) == 3 and shp[0] * shp[1] > max_desc and tuple(in_.shape) == shp:
            step = max(1, max_desc // shp[0])
            for a0 in range(0, shp[1], step):
                a1 = min(shp[1], a0 + step)
                getattr(self, eng).dma_start(out=out[:, a0:a1, :], in_=in_[:, a0:a1, :])
        else:
            getattr(self, eng).dma_start(out=out, in_=in_)

    def custom(self, eng, fn, R, W, is_dma=True):
        op = _Op()
        op.eng, op.meth, op.args, op.kw, op.fn = eng, None, (), {}, fn
        op.is_dma = is_dma
        op.R = [k for k in R if k not in self.dram_names]
        op.W = [k for k in W if k not in self.dram_names]
        op.idx = len(self.ops)
        self.ops.append(op)
        return op

    def _wait(self, eng, sem, val):
        kn = self.known[eng]
        key = id(sem)
        if kn.get(key, 0) >= val:
            return
        self.eng_obj[eng].wait_ge(sem, val)
        kn[key] = val

    def flush(self):
        ops = self.ops
        self.ops = []
        if not ops:
            return 0
        last_w = {}
        readers = {}
        for op in ops:
            deps = set()
            for k in op.R:
                if k in last_w:
                    deps.add(last_w[k])
            for k in op.W:
                if k in last_w:
                    deps.add(last_w[k])
                for r in readers.get(k, ()):
                    deps.add(r)
            deps.discard(op.idx)
            fd = []
            for d in deps:
                dop = ops[d]
                if dop.is_dma or op.is_dma or dop.eng != op.eng:
                    fd.append(d)
                elif self.same_engine_sync and op.eng != 'pe':
                    fd.append(d)
            op.deps = fd
            op.signal = op.is_dma
            for k in op.W:
                last_w[k] = op.idx
                readers[k] = []
            for k in op.R:
                if k not in op.W:
                    readers.setdefault(k, []).append(op.idx)
        for op in ops:
            for d in op.deps:
                ops[d].signal = True
        last_on = {}
        for op in ops:
            if not op.is_dma:
                last_on[op.eng] = op
        for op in last_on.values():
            op.signal = True
        first_done = set()
        for op in ops:
            e = op.eng
            if e not in first_done:
                first_done.add(e)
                for key, val in self.pending_barrier.items():
                    self._wait(e, self.sems[key], val)
            need = {}
            for d in op.deps:
                dop = ops[d]
                key = id(dop.sem)
                if need.get(key, (None, 0))[1] < dop.val:
                    need[key] = (dop.sem, dop.val)
            for sem, val in need.values():
                self._wait(e, sem, val)
            if op.is_dma:
                lst = self.dma_sems.setdefault(e, [])
                if len(lst) < self.n_dma_sems:
                    lst.append([self._new_sem('d' + e), 0])
                    slot = lst[-1]
                else:
                    i = self.dma_rr.get(e, 0)
                    slot = lst[i]
                    self.dma_rr[e] = (i + 1) % len(lst)
                    self._wait(e, slot[0], slot[1])
                    if slot[1] >= SEM_ROTATE:
                        slot[0] = self._new_sem('d' + e)
                        slot[1] = 0
                inst = (op.fn(self.eng_obj[e]) if op.meth is None else getattr(self.eng_obj[e], op.meth)(*op.args, **op.kw))
                slot[1] += 16
                inst.then_inc(slot[0], 16)
                op.sem, op.val = slot[0], slot[1]
            else:
                inst = (op.fn(self.eng_obj[e]) if op.meth is None else getattr(self.eng_obj[e], op.meth)(*op.args, **op.kw))
                if op.signal:
                    if e not in self.eng_sem or self.eng_cnt[e] >= SEM_ROTATE:
                        self.eng_sem[e] = self._new_sem(e)
                        self.eng_cnt[e] = 0
                    self.eng_cnt[e] += 1
                    inst.then_inc(self.eng_sem[e], 1)
                    op.sem, op.val = self.eng_sem[e], self.eng_cnt[e]
        pb = {}
        for e, s in self.eng_sem.items():
            pb[id(s)] = self.eng_cnt[e]
        for e, lst in self.dma_sems.items():
            for slot in lst:
                if slot[1] > 0:
                    pb[id(slot[0])] = slot[1]
        self.pending_barrier = pb
        self.total_ops += len(ops)
        return len(ops)

    def finish(self):
        self.flush()
        for key, val in self.pending_barrier.items():
            self._wait('sp', self.sems[key], val)
        return self.total_ops


from contextlib import ExitStack

SCALE = HD ** -0.5
LG = [math.log1p(-2.0 ** (-5.0 - h)) for h in range(RH)]


class Cfg:
    def __init__(self, n_pseq=2, seq=SEQ, n_sseq=4, do_mixer=True, do_sample=True, npool=5120):
        self.n_pseq = n_pseq
        self.seq = seq
        self.n_sseq = n_sseq
        self.ntp = n_pseq * seq
        self.nts = n_sseq * DEC
        self.nt = self.ntp + self.nts
        self.do_mixer = do_mixer
        self.do_sample = do_sample
        self.tiles = [(i * 512, 512) for i in range(self.ntp // 512)]
        if self.nts:
            self.tiles.append((self.ntp, self.nts))
        self.nqt = seq // 128
        self.ncmp = seq // 16 - 1
        self.nslc = seq // 64
        self.npool = npool


def build_program(cfg):
    nc = bass.Bass("TRN2", target_bir_lowering=False)
    NT = cfg.nt
    import os as _osx
    S = Sched(nc, same_engine_sync=bool(int(_osx.environ.get("MK_SES", "1"))))
    BGCAST = bool(int(_osx.environ.get("MK_BGCAST", "1")))
    seq, nqt, NCMP, NSLC = cfg.seq, cfg.nqt, cfg.ncmp, cfg.nslc

    def din(name, shape, dt=F32):
        t = nc.dram_tensor(name, list(shape), dt, kind="ExternalInput")
        S.dram_names.add(name)
        return t.ap()

    def dout(name, shape, dt=F32):
        t = nc.dram_tensor(name, list(shape), dt, kind="ExternalOutput")
        S.dram_names.add(name)
        return t.ap()

    def dscr(name, shape, dt):
        t = nc.dram_tensor(name, list(shape), dt)
        S.dram_names.add(name)
        return t.ap()

    xin = din("xin", [NT, D])
    w_up = [din("ffn1_w_up", [D, 2 * DFF]), din("ffn2_w_up", [D, 2 * DFF])]
    w_dn = [din("ffn1_w_down", [DFF, D]), din("ffn2_w_down", [DFF, D])]
    w_in = din("w_in", [D, NIN])
    w_out = din("w_out", [D, D])
    lnp_d = din("lnp", [128, 6, KC])
    ident_d = din("ident", [128, 128])
    rope_d = din("rope", [NT, 128])
    gn_d = din("gn", [128, 2, RH * RDV])
    rc128_d = din("rc128", [128, 3, RH, 128])
    rc8_d = din("rc8", [128, 3, RH, 128])
    w1_d = [din("cmp_w1_k", [32, HD, 256]), din("cmp_w1_v", [32, HD, 256])]
    w2_d = [din("cmp_w2_k", [256, HD]), din("cmp_w2_v", [256, HD])]
    posT_d = din("cmp_posT", [128, 2, 32])
    mcmp_d = din("mcmp", [128, seq])
    cover_d = din("cover", [128, 32])
    mtri_d = din("mtri", [128, 2, 128])
    sel_d = din("selc", [128, nqt, 3, 32])
    E_d = din("Eexp", [32, nqt, 128])
    state_d = din("state", [max(cfg.n_sseq, 1), RH, RDK, RDV])
    cache_d = din("cache", [cfg.npool, PAGE, 4, NKV, HD])
    ptab_d = din("ptab", [max(cfg.n_sseq, 1), PAST // PAGE], I32)
    covs_d = din("cover_s", [128, 8, 257])
    selcs_d = din("selc_s", [16, 3, 257])
    Eall_d = din("E_all", [128, 64, 128])
    msm_d = din("msmall", [128, 16])
    r8_d = din("r8", [8, 32])
    gsel_d = din("gsel", [32, 2, 24])
    s16_d = din("sel16", [32, 2, 16])
    cwin_d = din("cwin", [max(cfg.n_sseq, 1), WINDOW, 512])
    y_out = dout("y", [NT, D])
    kv_out = dout("kvrows", [NT, 1024])
    winp_out = dout("winp", [max(cfg.n_pseq, 1), WINDOW, 512])
    wins_out = dout("wins", [max(cfg.n_sseq, 1), WINDOW, 512])
    rs_out = dout("rs", [cfg.n_pseq + cfg.n_sseq, RH, RDK, RDV])
    wup_b = [dscr("wup1b", [2 * DFF // 256, 128, KC, 256], BF16), dscr("wup2b", [2 * DFF // 256, 128, KC, 256], BF16)]
    wdn_b = [dscr("wdn1b", [KC, 128, FC, 128], BF16), dscr("wdn2b", [KC, 128, FC, 128], BF16)]
    win_b = dscr("winb", [23, 128, KC, 256], BF16)
    wout_b = dscr("woutb", [D // 256, 128, KC, 256], BF16)
    x1T = dscr("x1T", [128, KC, NT], F32)
    import os as _os0
    mixed = (dout("mixed", [NT, D], BF16) if _os0.environ.get("MK_DBGOUT") else dscr("mixed", [NT, D], BF16))
    rqT = dscr("rqT", [128, RH, NT], BF16)
    rkT = dscr("rkT", [128, RH, NT], BF16)
    rk_tok = dscr("rk_tok", [NT, RH * RDK], BF16)
    rv_tok = dscr("rv_tok", [NT, RH * RDV], BF16)
    rg_tok = dscr("rg_tok", [NT, RH * RDV], BF16)
    nqT = dscr("nqT", [128, NH, NT], BF16)
    kcmpT = dscr("kcmpT", [128, NKV, NT], BF16)
    vcmpT = dscr("vcmpT", [128, NKV, NT], BF16)
    kslcT = dscr("kslcT", [128, NKV, NT], BF16)
    kwinT = dscr("kwinT", [128, NKV, NT], BF16)
    vslc_tok = dscr("vslc_tok", [NT, NKV * HD], BF16)
    vwin_tok = dscr("vwin_tok", [NT, NKV * HD], BF16)
    gates_d = dscr("gates", [NT, 24], F32)

    def sb(name, shape, dt=F32):
        return nc.alloc_sbuf_tensor(name, list(shape), dt).ap()

    uniq = [0]

    def sbx(stack, name, shape, dt=F32):
        uniq[0] += 1
        return stack.enter_context(nc.sbuf_tensor(f"{name}_{uniq[0]}", list(shape), dt)).ap()

    def psx(stack, name, shape, dt=F32):
        uniq[0] += 1
        return stack.enter_context(nc.psum_tensor(f"{name}_{uniq[0]}", list(shape), dt)).ap()

    ident = sb("ident_sb", [128, 128])
    identb = sb("identb_sb", [128, 128], BF16)
    ones_f = sb("ones_f", [128, 128])
    ones_b = sb("ones_b", [128, 128], BF16)
    lnp = sb("lnp_sb", [128, 6, KC])
    lnpa = sb("lnpa_sb", [128, 6, KC])

    S.dma('sp', out=ident, in_=ident_d)
    S.dma('sp', out=lnp, in_=lnp_d)
    S.dve.tensor_copy(out=identb, in_=ident)
    S.dve.memset(ones_f, 1.0)
    S.dve.memset(ones_b, 1.0)
    S.act.mul(out=lnpa, in_=lnp, mul=ALPHA)
    CW = 1024

    def cast_items():
        items = []
        for li in range(2):
            for kc in range(KC):
                for c0 in range(0, 2 * DFF, CW):
                    n = min(CW, 2 * DFF - c0)
                    nb = n // 256
                    dst = wup_b[li][c0 // 256:c0 // 256 + nb, :, kc, :].rearrange("j p c -> p j c")
                    items.append((w_up[li][kc * 128:(kc + 1) * 128, c0:c0 + n], n, [(dst, 0, n, 256)], li))
            for fc in range(FC):
                for c0 in range(0, D, CW):
                    dst = wdn_b[li][c0 // 128:(c0 + CW) // 128, :, fc, :].rearrange("d p c -> p d c")
                    items.append((w_dn[li][fc * 128:(fc + 1) * 128, c0:c0 + CW], CW, [(dst, 0, CW, 128)], li))
            if li == 0:
                for kc in range(KC):
                    for c0 in range(0, NIN, CW):
                        n = min(CW, NIN - c0)
                        nb = n // 256
                        outs = []
                        if nb:
                            outs.append((win_b[c0 // 256:c0 // 256 + nb, :, kc, :].rearrange("j p c -> p j c"), 0, nb * 256, 256))
                        if n % 256:
                            outs.append((win_b[c0 // 256 + nb, :, kc, 0:n % 256], nb * 256, n, 0))
                        items.append((w_in[kc * 128:(kc + 1) * 128, c0:c0 + n], n, outs, 0))
        for kc in range(KC):
            for c0 in range(0, D, CW):
                dst = wout_b[c0 // 256:(c0 + CW) // 256, :, kc, :].rearrange("j p c -> p j c")
                items.append((w_out[kc * 128:(kc + 1) * 128, c0:c0 + CW], CW, [(dst, 0, CW, 256)], 1))
        return items

    cast_ci = [0]

    def emit_cast(item, wld, wcv):
        src, n, outs, _ = item
        q = cast_ci[0] % len(wld)
        cast_ci[0] += 1
        S.dma('sp', out=wld[q][:, :n], in_=src)
        ce = (cast_ci[0] - 1) % 3
        if ce == 0:
            S.dve.tensor_copy(out=wcv[q][:, :n], in_=wld[q][:, :n])
        elif ce == 1:
            S.act.copy(out=wcv[q][:, :n], in_=wld[q][:, :n])
        else:
            S.pool.tensor_copy(out=wcv[q][:, :n], in_=wld[q][:, :n])
        for (dst, a0, a1, blk) in outs:
            if blk:
                S.dma('sp', out=dst, in_=wcv[q][:, a0:a1].rearrange("p (j c) -> p j c", c=blk))
            else:
                S.dma('sp', out=dst, in_=wcv[q][:, a0:a1])

    all_items = cast_items()
    early_items = [it for it in all_items if it[3] == 0]
    late_items = [it for it in all_items if it[3] == 1]
    with ExitStack() as stW:
        wld = [sbx(stW, "wld", [128, CW]) for _ in range(12)]
        wcv = [sbx(stW, "wcv", [128, CW], BF16) for _ in range(12)]
        for it in early_items:
            emit_cast(it, wld, wcv)
        if not BGCAST:
            for it in late_items:
                emit_cast(it, wld, wcv)
            late_items = []
        S.flush()
    if not cfg.do_mixer:
        with ExitStack() as st0:
            z = sbx(st0, "zt", [128, D], BF16)
            S.dve.memset(z, 0.0)
            for r0 in range(0, NT, 128):
                rows = min(128, NT - r0)
                S.dma('sp', out=mixed[r0:r0 + rows, :], in_=z[:rows, :])
            S.flush()
    S.flush()

    TWM = 512
    import os as _os
    STOP = int(_os.environ.get("MK_STOP", "9"))
    DBG = int(_os.environ.get("MK_DBG", "0"))
    if STOP == 0:
        return nc, S.finish()

    class NS:
        pass

    NWB = 3

    def bg_cast(b, n):
        if b.wld is None:
            return
        for _ in range(n):
            if late_items:
                emit_cast(late_items.pop(0), b.wld, b.wcv)

    def dense_bufs(stack):
        b = NS()
        b.xf = sbx(stack, "xf", [128, KC, TWM])
        b.xb = sbx(stack, "xb", [128, KC, TWM], BF16)
        b.gT = sbx(stack, "gT", [128, FC, TWM], BF16)
        b.wa = [sbx(stack, "wa", [128, KC, 256], BF16) for _ in range(NWB)]
        b.wb = [sbx(stack, "wb", [128, KC, 256], BF16) for _ in range(NWB)]
        b.wld = b.wcv = None
        b.wd = [sbx(stack, "wd", [128, FC // 2, 128], BF16) for _ in range(2)]
        b.scr = [sbx(stack, "scr", [128, TWM]) for _ in range(2)]
        b.mean = sbx(stack, "mean", [128, TWM])
        b.rstd = sbx(stack, "rstd", [128, TWM])
        b.xtok = [sbx(stack, "xtok", [128, 1024]) for _ in range(2)]
        b.psA = [psx(stack, "psA", [128, 512]) for _ in range(2)]
        b.psB = [psx(stack, "psB", [128, 512]) for _ in range(2)]
        b.psS = psx(stack, "psS", [128, 512])
        b.psQ = psx(stack, "psQ", [128, 512])
        b.psT = [psx(stack, "psT", [128, 1024], BF16) for _ in range(2)]
        return b

    def ffn(b, li, TW):
        for jc in range(DFF // 256):
            bi = jc % NWB
            S.dma('sp', out=b.wa[bi], in_=wup_b[li][jc], nosplit=True)
            S.dma('sp', out=b.wb[bi], in_=wup_b[li][DFF // 256 + jc], nosplit=True)
            bg_cast(b, 2)
            for jj in range(2):
                j = jc * 2 + jj
                pa, pb_ = b.psA[j % 2], b.psB[j % 2]
                for kc in range(KC):
                    S.pe.matmul(pa[:, :TW], lhsT=b.wa[bi][:, kc, jj * 128:(jj + 1) * 128], rhs=b.xb[:, kc, :TW],
                                start=(kc == 0), stop=(kc == KC - 1))
                for kc in range(KC):
                    S.pe.matmul(pb_[:, :TW], lhsT=b.wb[bi][:, kc, jj * 128:(jj + 1) * 128], rhs=b.xb[:, kc, :TW],
                                start=(kc == 0), stop=(kc == KC - 1))
                S.act.activation(out=b.scr[j % 2][:, :TW], in_=pa[:, :TW], func=AF.Silu)
                S.dve.tensor_tensor(out=b.gT[:, j, :TW], in0=b.scr[j % 2][:, :TW], in1=pb_[:, :TW], op=ALU.mult)
        HF = FC // 2
        for dc in range(KC):
            pa = b.psA[dc % 2]
            for hf in range(2):
                S.dma('sp', out=b.wd[hf], in_=wdn_b[li][dc, :, hf * HF:(hf + 1) * HF, :], nosplit=True)
                for jl in range(HF):
                    j = hf * HF + jl
                    S.pe.matmul(pa[:, :TW], lhsT=b.wd[hf][:, jl, :], rhs=b.gT[:, j, :TW], start=(j == 0), stop=(j == FC - 1))
            S.dve.scalar_tensor_tensor(out=b.xf[:, dc, :TW], in0=pa[:, :TW], scalar=0.5, in1=b.xf[:, dc, :TW],
                                       op0=ALU.mult, op1=ALU.add)

    def layer_norm(b, gi, TW, scale_out):
        xf, xb = b.xf, b.xb
        for c in range(KC):
            S.pe.matmul(b.psS[:, :TW], lhsT=ones_f, rhs=xf[:, c, :TW], start=(c == 0), stop=(c == KC - 1))
        for c in range(KC):
            S.act.activation(out=b.scr[c % 2][:, :TW], in_=xf[:, c, :TW], func=AF.Square)
            S.pe.matmul(b.psQ[:, :TW], lhsT=ones_f, rhs=b.scr[c % 2][:, :TW], start=(c == 0), stop=(c == KC - 1))
        mean, rstd = b.mean, b.rstd
        S.act.mul(out=mean[:, :TW], in_=b.psS[:, :TW], mul=1.0 / D)
        S.dve.tensor_tensor(out=rstd[:, :TW], in0=mean[:, :TW], in1=mean[:, :TW], op=ALU.mult)
        S.dve.scalar_tensor_tensor(out=rstd[:, :TW], in0=b.psQ[:, :TW], scalar=1.0 / D, in1=rstd[:, :TW],
                                   op0=ALU.mult, op1=ALU.subtract)
        S.dve.tensor_scalar(out=rstd[:, :TW], in0=rstd[:, :TW], scalar1=EPS, scalar2=None, op0=ALU.add)
        S.act.activation(out=rstd[:, :TW], in_=rstd[:, :TW], func=AF.Sqrt)
        S.dve.reciprocal(out=rstd[:, :TW], in_=rstd[:, :TW])
        gsrc = lnpa if scale_out else lnp
        for c in range(KC):
            t = b.scr[c % 2]
            S.dve.tensor_tensor(out=t[:, :TW], in0=xf[:, c, :TW], in1=mean[:, :TW], op=ALU.subtract)
            S.dve.tensor_tensor(out=t[:, :TW], in0=t[:, :TW], in1=rstd[:, :TW], op=ALU.mult)
            S.act.activation(out=xb[:, c, :TW], in_=t[:, :TW], func=AF.Identity,
                             scale=lnp[:, 2 * gi, c:c + 1], bias=lnp[:, 2 * gi + 1, c:c + 1])
            S.act.activation(out=xf[:, c, :TW], in_=t[:, :TW], func=AF.Identity,
                             scale=gsrc[:, 2 * gi, c:c + 1], bias=gsrc[:, 2 * gi + 1, c:c + 1])

    def load_x(b, t0, TW):
        nsub = (TW + 127) // 128
        for s in range(nsub):
            rows = min(128, TW - s * 128)
            for hf in range(2):
                xt = b.xtok[hf]
                S.dma('sp', out=xt[:rows, :], in_=xin[t0 + s * 128:t0 + s * 128 + rows, hf * 1024:(hf + 1) * 1024])
                if DBG == 1:
                    continue
                for c4 in range(2):
                    pa = b.psA[c4 % 2]
                    for k in range(4):
                        S.pe.transpose(pa[:, k * 128:k * 128 + rows], xt[:rows, (c4 * 4 + k) * 128:(c4 * 4 + k + 1) * 128],
                                       ident[:rows, :rows])
                    if DBG == 2:
                        continue
                    src = pa.rearrange("p (k n) -> p k n", k=4)[:, :, :rows]
                    cb = hf * 8 + c4 * 4
                    S.act.mul(out=b.xf[:, cb:cb + 4, s * 128:s * 128 + rows], in_=src, mul=ALPHA)
                    if DBG != 5:
                        S.act.copy(out=b.xb[:, cb:cb + 4, s * 128:s * 128 + rows], in_=src)
                    elif DBG == 4:
                        S.dve.tensor_copy(out=b.xb[:, cb:cb + 4, s * 128:s * 128 + rows], in_=b.xf[:, cb:cb + 4, s * 128:s * 128 + rows])
                    else:
                        S.dve.tensor_copy(out=b.xb[:, cb:cb + 4, s * 128:s * 128 + rows], in_=src)


    def w_in_proj(b, a, t0, TW):
        nsub = (TW + 127) // 128
        rts = a.ropet[(t0 // 512) % 2]
        for s in range(nsub):
            rows = min(128, TW - s * 128)
            S.dma('sp', out=rts[:rows, s, :], in_=rope_d[t0 + s * 128:t0 + s * 128 + rows, :])
        tcount = [0]

        def rope_tok(dst, src, nh, rt, rows):
            s3 = src[:rows, :nh * 128].rearrange("p (h d) -> p h d", h=nh)
            d3 = dst[:rows, :nh * 128].rearrange("p (h d) -> p h d", h=nh)
            cosb = rt[:rows, 0:64].unsqueeze(1).to_broadcast([rows, nh, 64])
            sinb = rt[:rows, 64:128].unsqueeze(1).to_broadcast([rows, nh, 64])
            ta = a.t1[:rows, :nh * 64].rearrange("p (h d) -> p h d", h=nh)
            tb_ = a.t2[:rows, :nh * 64].rearrange("p (h d) -> p h d", h=nh)
            x1, x2 = s3[:, :, 0:64], s3[:, :, 64:128]
            S.dve.tensor_tensor(out=ta, in0=x1, in1=cosb, op=ALU.mult)
            S.dve.tensor_tensor(out=tb_, in0=x2, in1=sinb, op=ALU.mult)
            S.dve.tensor_tensor(out=d3[:, :, 0:64], in0=ta, in1=tb_, op=ALU.subtract)
            S.dve.tensor_tensor(out=ta, in0=x1, in1=sinb, op=ALU.mult)
            S.dve.tensor_tensor(out=tb_, in0=x2, in1=cosb, op=ALU.mult)
            S.dve.tensor_tensor(out=d3[:, :, 64:128], in0=ta, in1=tb_, op=ALU.add)

        def transposes_to(src_b, s, rows, nh, blk_tr):
            pt = b.psT[tcount[0] % 2]
            tcount[0] += 1
            for h in range(nh):
                S.pe.transpose(pt[:, h * 128:h * 128 + rows], src_b[:rows, h * 128:(h + 1) * 128], identb[:rows, :rows])
            S.act.copy(out=blk_tr[:, :nh, s * 128:s * 128 + rows],
                       in_=pt[:, :nh * 128].rearrange("p (h n) -> p h n", h=nh)[:, :, :rows])

        for blk in range(12):
            c0 = blk * 512
            ncol = min(512, NIN - c0)
            bi = blk % 2
            halves = [(b.wa[bi], 0, min(256, ncol))]
            if ncol > 256:
                halves.append((b.wb[bi], 256, ncol - 256))
            for hi_, (wt, o, n) in enumerate(halves):
                S.dma('sp', out=wt[:, :, :n], in_=win_b[2 * blk + hi_][:, :, :n], nosplit=(n == 256))
            tr = a.trT[blk % 2]
            for s in range(nsub):
                rows = min(128, TW - s * 128)
                tok0 = t0 + s * 128
                q = (blk * 4 + s) % 2
                pa = b.psA[q]
                for (wt, o, n) in halves:
                    for kc in range(KC):
                        S.pe.matmul(pa[:rows, o:o + n], lhsT=b.xb[:, kc, s * 128:s * 128 + rows], rhs=wt[:, kc, :n],
                                    start=(kc == 0), stop=(kc == KC - 1))
                e, r, rb = a.ev[q], a.rp[q], a.rpb[q]
                rt = rts[:, s, :]
                if blk in (0, 1, 6, 7):
                    S.act.copy(out=e[:rows, :], in_=pa[:rows, :])
                    rope_tok(r, e, 4, rt, rows)
                    if blk == 1:
                        S.act.mul(out=rb[:rows, :], in_=r[:rows, :], mul=RDK ** -0.5)
                        S.dma('sp', out=rk_tok[tok0:tok0 + rows, :], in_=rb[:rows, :])
                    else:
                        S.act.copy(out=rb[:rows, :], in_=r[:rows, :])
                    transposes_to(rb, s, rows, 4, tr)
                elif blk in (2, 3):
                    S.act.copy(out=rb[:rows, :], in_=pa[:rows, :])
                    S.dma('sp', out=rv_tok[tok0:tok0 + rows, (blk - 2) * 512:(blk - 1) * 512], in_=rb[:rows, :])
                elif blk in (4, 5):
                    S.act.activation(out=rb[:rows, :], in_=pa[:rows, :], func=AF.Silu)
                    S.dma('sp', out=rg_tok[tok0:tok0 + rows, (blk - 4) * 512:(blk - 3) * 512], in_=rb[:rows, :])
                elif blk in (8, 9, 10):
                    S.act.copy(out=e[:rows, :], in_=pa[:rows, :])
                    S.act.copy(out=r[:rows, 256:512], in_=pa[:rows, 256:512])
                    rope_tok(r, e, 2, rt, rows)
                    S.dve.tensor_copy(out=rb[:rows, :], in_=r[:rows, :])
                    if blk in (8, 9):
                        S.dma('sp', out=kv_out[tok0:tok0 + rows, (blk - 8) * 512:(blk - 7) * 512], in_=r[:rows, :])
                    elif tok0 < cfg.ntp:
                        b_loc, pos0 = tok0 // seq, tok0 % seq
                        w0 = pos0 - (seq - WINDOW)
                        if w0 >= 0:
                            S.dma('sp', out=winp_out[b_loc, w0:w0 + rows, :], in_=r[:rows, :])
                    else:
                        for sq_ in range(cfg.n_sseq):
                            S.dma('sp', out=wins_out[sq_, WINDOW - DEC:WINDOW, :], in_=r[sq_ * DEC:(sq_ + 1) * DEC, :])
                    if blk == 8:
                        transposes_to(rb, s, rows, 4, tr)
                    else:
                        transposes_to(rb, s, rows, 2, tr)
                        dstv = vslc_tok if blk == 9 else vwin_tok
                        S.dma('sp', out=dstv[tok0:tok0 + rows, :], in_=rb[:rows, 256:512])
                else:
                    S.act.activation(out=a.gsb[:rows, s, :], in_=pa[:rows, :24], func=AF.Sigmoid)
                    S.dma('sp', out=gates_d[tok0:tok0 + rows, :], in_=a.gsb[:rows, s, :])
            if blk == 0:
                S.dma('sp', out=rqT[:, :, t0:t0 + TW], in_=tr[:, :, :TW])
            elif blk == 1:
                S.dma('sp', out=rkT[:, :, t0:t0 + TW], in_=tr[:, :, :TW])
            elif blk in (6, 7):
                S.dma('sp', out=nqT[:, (blk - 6) * 4:(blk - 5) * 4, t0:t0 + TW], in_=tr[:, :, :TW])
            elif blk == 8:
                S.dma('sp', out=kcmpT[:, :, t0:t0 + TW], in_=tr[:, 0:2, :TW])
                S.dma('sp', out=vcmpT[:, :, t0:t0 + TW], in_=tr[:, 2:4, :TW])
            elif blk == 9:
                S.dma('sp', out=kslcT[:, :, t0:t0 + TW], in_=tr[:, 0:2, :TW])
            elif blk == 10:
                S.dma('sp', out=kwinT[:, :, t0:t0 + TW], in_=tr[:, 0:2, :TW])

    with ExitStack() as stA:
        b = dense_bufs(stA)
        a = NS()
        a.ev = [sbx(stA, "ev", [128, 512]) for _ in range(2)]
        a.rp = [sbx(stA, "rp", [128, 512]) for _ in range(2)]
        a.rpb = [sbx(stA, "rpb", [128, 512], BF16) for _ in range(2)]
        a.t1 = sbx(stA, "t1", [128, 256])
        a.t2 = sbx(stA, "t2", [128, 256])
        a.ropet = [sbx(stA, "ropet", [128, 4, 128])] * 2
        a.trT = [sbx(stA, "trT", [128, 4, TWM], BF16) for _ in range(2)]
        a.gsb = sbx(stA, "gsb", [128, 4, 24])
        if late_items:
            b.wld = [sbx(stA, "wldA", [128, CW]) for _ in range(2)]
            b.wcv = [sbx(stA, "wcvA", [128, CW], BF16) for _ in range(2)]
        for (t0, TW) in cfg.tiles:
            load_x(b, t0, TW)
            if STOP == 10:
                if not _os.environ.get("MK_NOST"):
                    S.dma('sp', out=x1T[:, :, t0:t0 + TW], in_=b.xf[:, :, :TW])
                return nc, S.finish()
            ffn(b, 0, TW)
            if STOP == 11:
                S.dma('sp', out=x1T[:, :, t0:t0 + TW], in_=b.xf[:, :, :TW])
                return nc, S.finish()
            layer_norm(b, 0, TW, True)
            S.dma('sp', out=x1T[:, :, t0:t0 + TW], in_=b.xf[:, :, :TW])
            if STOP == 12:
                return nc, S.finish()
            w_in_proj(b, a, t0, TW)
            if STOP == 13:
                return nc, S.finish()
        while late_items:
            bg_cast(b, 1)
        for sq_ in range(cfg.n_sseq):
            for r0 in range(0, WINDOW - DEC, 126):
                q = (r0 // 126) % 2
                S.dma('sp', out=a.ev[q][:126, :], in_=cwin_d[sq_, DEC + r0:DEC + r0 + 126, :])
                S.dma('sp', out=wins_out[sq_, r0:r0 + 126, :], in_=a.ev[q][:126, :])
        S.flush()

    if STOP == 1:
        return nc, S.finish()

    def retention_phase():
        with ExitStack() as stB:
            rcs = {128: sbx(stB, "rc128", [128, 3, RH, 128]), 8: sbx(stB, "rc8", [128, 3, RH, 128])}
            S.dma('sp', out=rcs[128], in_=rc128_d)
            S.dma('sp', out=rcs[8], in_=rc8_d)
            zt = sbx(stB, "ztile", [128, NH * HD], BF16)
            S.dve.memset(zt, 0.0)
            if cfg.nts:
                S.dma('sp', out=mixed[cfg.ntp:cfg.ntp + cfg.nts, RH * RDV:D], in_=zt[:cfg.nts, :])
            gn = sbx(stB, "gn", [128, 2, RH * RDV])
            S.dma('sp', out=gn, in_=gn_d)
            LM = seq
            rq_sb = sbx(stB, "rq_sb", [128, RH, LM], BF16)
            rk_sb = sbx(stB, "rk_sb", [128, RH, LM], BF16)
            rkt_sb = sbx(stB, "rkt_sb", [128, LM // 128, RH * RDK], BF16)
            rv_sb = sbx(stB, "rv_sb", [128, LM // 128, RH * RDV], BF16)
            rg_sb = sbx(stB, "rg_sb", [128, LM // 128, RH * RDV], BF16)
            S_f = sbx(stB, "S_f", [128, RH, RDV])
            S_b = sbx(stB, "S_b", [128, RH, RDV], BF16)
            aTb = [sbx(stB, "aTb", [128, 128], BF16) for _ in range(2)]
            qd = [sbx(stB, "qd", [128, 128], BF16) for _ in range(2)]
            ksc = [sbx(stB, "ksc", [128, 128], BF16) for _ in range(2)]
            cen = [sbx(stB, "cen", [128, RDV]) for _ in range(2)]
            sqt = [sbx(stB, "sqt", [128, RDV]) for _ in range(2)]
            stt = [sbx(stB, "stt", [128, 4]) for _ in range(2)]
            mix = [sbx(stB, "mixr", [128, RH * RDV], BF16) for _ in range(2)]
            pa_ = [psx(stB, "rpa", [128, 512]) for _ in range(2)]
            po_ = [psx(stB, "rpo", [128, 512]) for _ in range(2)]
            pst_ = [psx(stB, "rps", [128, 512]) for _ in range(2)]

            def run_seq(tb, L, CS, s0_ap, rs_idx):
                nch = L // CS
                rc = rcs[CS]
                rcbb = rc
                S.dma('sp', out=rq_sb[:, :, :L], in_=rqT[:, :, tb:tb + L])
                S.dma('sp', out=rk_sb[:, :, :L], in_=rkT[:, :, tb:tb + L])
                if CS == 128:
                    S.dma('sp', out=rkt_sb[:, :nch, :], in_=rk_tok[tb:tb + L, :].rearrange("(c p) n -> p c n", p=128))
                    S.dma('sp', out=rv_sb[:, :nch, :], in_=rv_tok[tb:tb + L, :].rearrange("(c p) n -> p c n", p=128))
                    S.dma('sp', out=rg_sb[:, :nch, :], in_=rg_tok[tb:tb + L, :].rearrange("(c p) n -> p c n", p=128))
                else:
                    S.dma('sp', out=rkt_sb[:CS, 0, :], in_=rk_tok[tb:tb + L, :])
                    S.dma('sp', out=rv_sb[:CS, 0, :], in_=rv_tok[tb:tb + L, :])
                    S.dma('sp', out=rg_sb[:CS, 0, :], in_=rg_tok[tb:tb + L, :])
                have_state = s0_ap is not None
                if have_state:
                    S.dma('sp', out=S_f, in_=s0_ap.rearrange("h d e -> d h e"))
                    S.act.copy(out=S_b, in_=S_f)
                u = 0
                for c in range(nch):
                    mx = mix[c % 2]
                    for h in range(RH):
                        q = u % 2
                        u += 1
                        cs = slice(c * CS, (c + 1) * CS)
                        pa, po, pst = pa_[q], po_[q], pst_[q]
                        S.pe.matmul(pa[:CS, :CS], lhsT=rk_sb[:, h, cs], rhs=rq_sb[:, h, cs], start=True, stop=True)
                        S.dve.tensor_tensor(out=aTb[q][:CS, :CS], in0=pa[:CS, :CS], in1=rcbb[:CS, 0, h, :CS], op=ALU.mult)
                        S.pe.matmul(po[:CS, :RDV], lhsT=aTb[q][:CS, :CS], rhs=rv_sb[:CS, c, h * RDV:(h + 1) * RDV],
                                    start=True, stop=not have_state)
                        if have_state:
                            S.pool.tensor_tensor(out=qd[q][:, :CS], in0=rq_sb[:, h, cs], in1=rcbb[:, 1, h, :CS], op=ALU.mult)
                            S.pe.matmul(po[:CS, :RDV], lhsT=qd[q][:, :CS], rhs=S_b[:, h, :], start=False, stop=True)
                        st_ = stt[q]
                        S.dve.reduce_sum(out=st_[:CS, 0:1], in_=po[:CS, :RDV], axis=AX.X)
                        S.dve.tensor_scalar(out=st_[:CS, 1:2], in0=st_[:CS, 0:1], scalar1=-1.0 / RDV, scalar2=None, op0=ALU.mult)
                        S.act.activation(out=cen[q][:CS, :], in_=po[:CS, :RDV], func=AF.Identity, bias=st_[:CS, 1:2], scale=1.0)
                        S.dve.tensor_tensor(out=sqt[q][:CS, :], in0=cen[q][:CS, :], in1=cen[q][:CS, :], op=ALU.mult)
                        S.dve.reduce_sum(out=st_[:CS, 2:3], in_=sqt[q][:CS, :], axis=AX.X)
                        S.dve.tensor_scalar(out=st_[:CS, 3:4], in0=st_[:CS, 2:3], scalar1=1.0 / RDV, scalar2=EPS,
                                            op0=ALU.mult, op1=ALU.add)
                        S.act.activation(out=st_[:CS, 3:4], in_=st_[:CS, 3:4], func=AF.Sqrt)
                        S.dve.reciprocal(out=st_[:CS, 3:4], in_=st_[:CS, 3:4])
                        S.dve.scalar_tensor_tensor(out=cen[q][:CS, :], in0=cen[q][:CS, :], scalar=st_[:CS, 3:4],
                                                   in1=gn[:CS, 0, h * RDV:(h + 1) * RDV], op0=ALU.mult, op1=ALU.mult)
                        S.dve.tensor_tensor(out=cen[q][:CS, :], in0=cen[q][:CS, :], in1=gn[:CS, 1, h * RDV:(h + 1) * RDV], op=ALU.add)
                        S.dve.tensor_tensor(out=mx[:CS, h * RDV:(h + 1) * RDV], in0=cen[q][:CS, :],
                                            in1=rg_sb[:CS, c, h * RDV:(h + 1) * RDV], op=ALU.mult)
                        S.act.activation(out=ksc[q][:CS, :], in_=rkt_sb[:CS, c, h * RDK:(h + 1) * RDK], func=AF.Identity,
                                         scale=rc[:CS, 2, h, 0:1])
                        S.pe.matmul(pst[:, :RDV], lhsT=ksc[q][:CS, :], rhs=rv_sb[:CS, c, h * RDV:(h + 1) * RDV], start=True, stop=True)
                        cdk = math.exp(LG[h] * CS)
                        if have_state:
                            S.dve.scalar_tensor_tensor(out=S_f[:, h, :], in0=S_f[:, h, :], scalar=float(np.float32(cdk)),
                                                       in1=pst[:, :RDV], op0=ALU.mult, op1=ALU.add)
                        else:
                            S.dve.tensor_copy(out=S_f[:, h, :], in_=pst[:, :RDV])
                        S.act.copy(out=S_b[:, h, :], in_=S_f[:, h, :])
                    have_state = True
                    S.dma('sp', out=mixed[tb + c * CS:tb + (c + 1) * CS, 0:RH * RDV], in_=mx[:CS, :])
                S.dma('sp', out=rs_out[rs_idx].rearrange("h d e -> d h e"), in_=S_f)

            for bq in range(cfg.n_pseq):
                run_seq(bq * seq, seq, 128, None, bq)
            for sq_ in range(cfg.n_sseq):
                run_seq(cfg.ntp + sq_ * DEC, DEC, DEC, state_d[sq_], cfg.n_pseq + sq_)
            S.flush()

    def nsa_prompt_phase():
        with ExitStack() as stC:
            w1 = [sbx(stC, "w1", [128, 32, 256], BF16) for _ in range(2)]
            w2 = [sbx(stC, "w2", [128, 2, 128], BF16) for _ in range(2)]
            posf = sbx(stC, "posf", [128, 2, 32])
            posb = sbx(stC, "posb", [128, 2, 32], BF16)
            hpos = sbx(stC, "hpos", [128, 2, 2])
            ld32 = sbx(stC, "ld32", [128, seq])
            mcmp = sbx(stC, "mcmp", [128, seq], BF16)
            cover = sbx(stC, "cover", [128, 32], BF16)
            mtri = sbx(stC, "mtri", [128, 2, 128], BF16)
            selc = sbx(stC, "selc", [128, nqt, 3, 32])
            Eb = sbx(stC, "Eb", [32, nqt, 128], BF16)
            for kv in range(2):
                for p0 in range(0, 32, 8):
                    S.dma('pool', out=w1[kv][:, p0:p0 + 8, :], in_=w1_d[kv][p0:p0 + 8].rearrange("p d f -> d p f"))
                S.dma('pool', out=w2[kv], in_=w2_d[kv].rearrange("(c p) d -> p c d", p=128))
            S.dma('sp', out=posf, in_=posT_d)
            S.dve.tensor_copy(out=posb, in_=posf)
            S.dma('sp', out=ld32, in_=mcmp_d)
            S.dve.tensor_copy(out=mcmp, in_=ld32)
            S.dma('sp', out=selc, in_=sel_d)
            S.dma('pool', out=cover, in_=cover_d)
            S.dma('pool', out=mtri, in_=mtri_d)
            S.dma('pool', out=Eb, in_=E_d)
            nq_sb = sbx(stC, "nq_sb", [128, NH, seq], BF16)
            kT_sb = {k: sbx(stC, k, [128, NKV, seq], BF16) for k in ("kslc", "kwin", "kcmp", "vcmp")}
            v_sb = {k: sbx(stC, k, [128, nqt, NKV * HD], BF16) for k in ("vslc", "vwin")}
            g_sb = sbx(stC, "g_sb", [128, nqt, 24])
            kcT = sbx(stC, "kcT", [128, NKV, 128], BF16)
            vc = sbx(stC, "vc", [128, NKV, 128], BF16)
            hx = [sbx(stC, "hx", [128, 128]) for _ in range(2)]
            hu = [sbx(stC, "hu", [128, 128]) for _ in range(2)]
            hT = [sbx(stC, "hT", [128, 128], BF16) for _ in range(2)]
            qc = [sbx(stC, "qc", [128, 512], BF16) for _ in range(2)]
            es = [sbx(stC, "es", [128, 512], BF16) for _ in range(3)]
            pT = [sbx(stC, "pT", [128, 512], BF16) for _ in range(3)]
            msk = [sbx(stC, "msk", [128, 128], BF16) for _ in range(2)]
            dens = sbx(stC, "dens", [128, 12])
            coef = sbx(stC, "coef", [128, 12])
            imp = sbx(stC, "imp", [128, 32])
            prio = sbx(stC, "prio", [128, 32])
            cmpm = sbx(stC, "cmpm", [128, 32, 32])
            rank = sbx(stC, "rank", [128, 32])
            selN = sbx(stC, "selN", [32, 128], BF16)
            ob = [sbx(stC, "ob", [128, 128]) for _ in range(2)]
            mixn = [sbx(stC, "mixn", [128, NH * HD], BF16) for _ in range(2)]
            pS = [psx(stC, "pS", [128, 512]) for _ in range(2)]
            pO = [psx(stC, "pO", [128, 512]) for _ in range(3)]
            pDen = psx(stC, "pDen", [128, 512])
            pX = psx(stC, "pX", [128, 512])
            pM = psx(stC, "pM", [128, 512])
            for kv in range(2):
                for fcn in range(2):
                    for p_ in range(32):
                        S.pe.matmul(pX[:, 0:1], lhsT=w1[kv][:, p_, fcn * 128:(fcn + 1) * 128], rhs=posb[:, kv, p_:p_ + 1],
                                    start=(p_ == 0), stop=(p_ == 31))
                    S.act.copy(out=hpos[:, kv, fcn:fcn + 1], in_=pX[:, 0:1])

            for bq in range(cfg.n_pseq):
                tb = bq * seq
                S.dma('sp', out=nq_sb, in_=nqT[:, :, tb:tb + seq])
                for k, src in (("kslc", kslcT), ("kwin", kwinT), ("kcmp", kcmpT), ("vcmp", vcmpT)):
                    S.dma('sp', out=kT_sb[k], in_=src[:, :, tb:tb + seq])
                S.dma('sp', out=v_sb["vslc"], in_=vslc_tok[tb:tb + seq, :].rearrange("(c p) n -> p c n", p=128))
                S.dma('sp', out=v_sb["vwin"], in_=vwin_tok[tb:tb + seq, :].rearrange("(c p) n -> p c n", p=128))
                S.dma('sp', out=g_sb, in_=gates_d[tb:tb + seq, :].rearrange("(c p) n -> p c n", p=128))
                for kv in range(2):
                    srcT = kT_sb["kcmp" if kv == 0 else "vcmp"]
                    for h in range(NKV):
                        xv = srcT[:, h, :].rearrange("p (n s) -> p n s", s=16)
                        for fcn in range(2):
                            ps_ = pS[fcn]
                            i = 0
                            for r in range(2):
                                for s_ in range(16):
                                    S.pe.matmul(ps_[:, :NCMP], lhsT=w1[kv][:, r * 16 + s_, fcn * 128:(fcn + 1) * 128],
                                                rhs=xv[:, r:r + NCMP, s_], start=(i == 0), stop=(i == 31))
                                    i += 1
                            x_, u_ = hx[fcn], hu[fcn]
                            S.act.activation(out=x_[:, :NCMP], in_=ps_[:, :NCMP], func=AF.Identity, bias=hpos[:, kv, fcn:fcn + 1], scale=1.0)
                            S.dve.tensor_tensor(out=u_[:, :NCMP], in0=x_[:, :NCMP], in1=x_[:, :NCMP], op=ALU.mult)
                            S.dve.tensor_scalar(out=u_[:, :NCMP], in0=u_[:, :NCMP], scalar1=0.044715, scalar2=1.0, op0=ALU.mult, op1=ALU.add)
                            S.dve.tensor_tensor(out=u_[:, :NCMP], in0=u_[:, :NCMP], in1=x_[:, :NCMP], op=ALU.mult)
                            S.act.activation(out=u_[:, :NCMP], in_=u_[:, :NCMP], func=AF.Tanh, scale=0.7978845608028654)
                            S.dve.tensor_scalar(out=u_[:, :NCMP], in0=u_[:, :NCMP], scalar1=1.0, scalar2=0.5, op0=ALU.add, op1=ALU.mult)
                            S.dve.tensor_tensor(out=hT[fcn][:, :NCMP], in0=u_[:, :NCMP], in1=x_[:, :NCMP], op=ALU.mult)
                        if kv == 0:
                            for fcn in range(2):
                                S.pe.matmul(pX[:, :NCMP], lhsT=w2[0][:, fcn, :], rhs=hT[fcn][:, :NCMP], start=(fcn == 0), stop=(fcn == 1))
                            S.act.copy(out=kcT[:, h, :NCMP], in_=pX[:, :NCMP])
                        else:
                            for fcn in range(2):
                                S.pe.matmul(pX[:NCMP, :128], lhsT=hT[fcn][:, :NCMP], rhs=w2[1][:, fcn, :], start=(fcn == 0), stop=(fcn == 1))
                            S.act.copy(out=vc[:NCMP, h, :], in_=pX[:NCMP, :128])
                u = 0
                ei = 0
                for qt in range(nqt):
                    mx = mixn[qt % 2]
                    for h in range(NKV):
                        qcu = qc[u % 2]
                        u += 1
                        S.pool.tensor_copy(out=qcu.rearrange("p (g n) -> p g n", g=4), in_=nq_sb[:, 4 * h:4 * h + 4, qt * 128:(qt + 1) * 128])

                        def pv(e_t, rows, vrhs, po, br, first, last):
                            for g in range(4):
                                S.pe.matmul(po[:, g * 128:(g + 1) * 128], lhsT=e_t[:rows, g * 128:(g + 1) * 128], rhs=vrhs,
                                            start=(first and g == 0), stop=last)
                            for g in range(4):
                                S.pe.matmul(pDen[:, br * 4 + g:br * 4 + g + 1], lhsT=e_t[:rows, g * 128:(g + 1) * 128],
                                            rhs=ones_b[:rows, 0:1], start=(first and br == 0 and g == 0), stop=last)

                        def bc4(m):
                            return m.unsqueeze(1).to_broadcast([m.shape[0], 4, 128])

                        ps_ = pS[ei % 2]
                        e_, p_ = es[ei % 3], pT[ei % 3]
                        ei += 1
                        S.pe.matmul(ps_[:NCMP, :], lhsT=kcT[:, h, :NCMP], rhs=qcu, start=True, stop=True)
                        S.act.activation(out=e_[:NCMP, :], in_=ps_[:NCMP, :], func=AF.Exp, scale=SCALE)
                        S.dve.tensor_tensor(out=p_[:NCMP, :].rearrange("p (g n) -> p g n", g=4),
                                            in0=e_[:NCMP, :].rearrange("p (g n) -> p g n", g=4),
                                            in1=bc4(mcmp[:NCMP, qt * 128:(qt + 1) * 128]), op=ALU.mult)
                        pv(p_, NCMP, vc[:NCMP, h, :], pO[0], 0, True, True)
                        for g in range(4):
                            S.pe.matmul(pX[:, g * 32:(g + 1) * 32], lhsT=p_[:NCMP, g * 128:(g + 1) * 128], rhs=cover[:NCMP, :],
                                        start=(g == 0), stop=True)
                        S.dve.tensor_scalar(out=dens[:, 0:4], in0=pDen[:, 0:4], scalar1=1e-30, scalar2=None, op0=ALU.max)
                        S.dve.reciprocal(out=dens[:, 0:4], in_=dens[:, 0:4])
                        S.dve.tensor_scalar(out=imp, in0=pX[:, 0:32], scalar1=dens[:, 0:1], scalar2=None, op0=ALU.mult)
                        for g in range(1, 4):
                            S.dve.scalar_tensor_tensor(out=imp, in0=pX[:, g * 32:(g + 1) * 32], scalar=dens[:, g:g + 1], in1=imp,
                                                       op0=ALU.mult, op1=ALU.add)
                        S.dve.tensor_tensor(out=prio, in0=imp, in1=selc[:, qt, 0, :], op=ALU.mult)
                        S.dve.tensor_tensor(out=prio, in0=prio, in1=selc[:, qt, 1, :], op=ALU.add)
                        S.dve.tensor_tensor(out=cmpm, in0=prio.unsqueeze(1).to_broadcast([128, 32, 32]),
                                            in1=prio.unsqueeze(2).to_broadcast([128, 32, 32]), op=ALU.is_gt)
                        S.dve.reduce_sum(out=rank, in_=cmpm, axis=AX.X)
                        S.dve.tensor_scalar(out=rank, in0=rank, scalar1=15.5, scalar2=None, op0=ALU.is_lt)
                        S.dve.tensor_tensor(out=rank, in0=rank, in1=selc[:, qt, 2, :], op=ALU.mult)
                        k0 = max(0, qt - WINDOW // 128)
                        tasks = [("win", kt) for kt in range(k0, qt + 1)] + [("slc", kt) for kt in range(qt + 1)]
                        slots = []

                        def do_score(i):
                            kind, kt = tasks[i]
                            ps_ = pS[i % 2]
                            if kind == "win":
                                S.pe.matmul(ps_, lhsT=kT_sb["kwin"][:, h, kt * 128:(kt + 1) * 128], rhs=qcu, start=True, stop=True)
                            else:
                                if kt == 0:
                                    S.pe.transpose(pM[:32, 128:256], rank, ident)
                                    S.dve.tensor_copy(out=selN, in_=pM[:32, 128:256])
                                S.pe.matmul(ps_, lhsT=kT_sb["kslc"][:, h, kt * 128:(kt + 1) * 128], rhs=qcu, start=True, stop=True)
                                mreg = pM[:, 0:128] if i % 2 == 0 else pM[:, 384:512]
                                S.pe.matmul(mreg, lhsT=Eb[:, kt, :], rhs=selN, start=True, stop=True)

                        def do_post(i):
                            kind, kt = tasks[i]
                            ps_ = pS[i % 2]
                            e_, p_ = es[i % 3], pT[i % 3]
                            S.act.activation(out=e_, in_=ps_, func=AF.Exp, scale=SCALE)
                            if kind == "win":
                                if kt == qt or kt == qt - WINDOW // 128:
                                    mi = 0 if kt == qt else 1
                                    S.dve.tensor_tensor(out=p_.rearrange("p (g n) -> p g n", g=4), in0=e_.rearrange("p (g n) -> p g n", g=4),
                                                        in1=bc4(mtri[:, mi, :]), op=ALU.mult)
                                    return p_
                                return e_
                            mreg = pM[:, 0:128] if i % 2 == 0 else pM[:, 384:512]
                            mk = msk[i % 2]
                            if kt == qt:
                                S.dve.tensor_tensor(out=mk, in0=mreg, in1=mtri[:, 0, :], op=ALU.mult)
                            else:
                                S.dve.tensor_copy(out=mk, in_=mreg)
                            S.dve.tensor_tensor(out=p_.rearrange("p (g n) -> p g n", g=4), in0=e_.rearrange("p (g n) -> p g n", g=4),
                                                in1=bc4(mk), op=ALU.mult)
                            return p_

                        def do_pv(i, src_e):
                            kind, kt = tasks[i]
                            if kind == "win":
                                pv(src_e, 128, v_sb["vwin"][:, kt, h * 128:(h + 1) * 128], pO[2], 2, kt == k0, kt == qt)
                            else:
                                pv(src_e, 128, v_sb["vslc"][:, kt, h * 128:(h + 1) * 128], pO[1], 1, kt == 0, kt == qt)

                        do_score(0)
                        for i in range(len(tasks)):
                            if i + 1 < len(tasks):
                                do_score(i + 1)
                            src_e = do_post(i)
                            do_pv(i, src_e)
                        S.dve.tensor_scalar(out=dens, in0=pDen[:, 0:12], scalar1=1e-30, scalar2=None, op0=ALU.max)
                        S.dve.reciprocal(out=dens, in_=dens)
                        for br in range(3):
                            S.dve.tensor_tensor(out=coef[:, br * 4:(br + 1) * 4], in0=dens[:, br * 4:(br + 1) * 4],
                                                in1=g_sb[:, qt, br * 8 + 4 * h:br * 8 + 4 * h + 4], op=ALU.mult)
                        for g in range(4):
                            o_ = ob[g % 2]
                            S.dve.tensor_scalar(out=o_, in0=pO[0][:, g * 128:(g + 1) * 128], scalar1=coef[:, g:g + 1], scalar2=None, op0=ALU.mult)
                            S.dve.scalar_tensor_tensor(out=o_, in0=pO[1][:, g * 128:(g + 1) * 128], scalar=coef[:, 4 + g:5 + g], in1=o_,
                                                       op0=ALU.mult, op1=ALU.add)
                            S.dve.scalar_tensor_tensor(out=mx[:, (4 * h + g) * 128:(4 * h + g + 1) * 128], in0=pO[2][:, g * 128:(g + 1) * 128],
                                                       scalar=coef[:, 8 + g:9 + g], in1=o_, op0=ALU.mult, op1=ALU.add)
                    S.dma('sp', out=mixed[tb + qt * 128:tb + (qt + 1) * 128, RH * RDV:D], in_=mx)
            S.flush()


    def nsa_sample_phase():
        NPG = PAST // PAGE
        NG = 8
        with ExitStack() as stS:
            w1 = [sbx(stS, "w1s", [128, 32, 256], BF16) for _ in range(2)]
            w2 = [sbx(stS, "w2s", [128, 2, 128], BF16) for _ in range(2)]
            posf = sbx(stS, "posfs", [128, 2, 32])
            posb = sbx(stS, "posbs", [128, 2, 32], BF16)
            hpos = sbx(stS, "hposs", [128, 2, 2])
            for kv in range(2):
                for p0 in range(0, 32, 8):
                    S.dma('pool', out=w1[kv][:, p0:p0 + 8, :], in_=w1_d[kv][p0:p0 + 8].rearrange("p d f -> d p f"))
                S.dma('pool', out=w2[kv], in_=w2_d[kv].rearrange("(c p) d -> p c d", p=128))
            S.dma('sp', out=posf, in_=posT_d)
            S.dve.tensor_copy(out=posb, in_=posf)
            E_all = sbx(stS, "E_all", [128, 64, 128], BF16)
            for e0 in range(0, 64, 4):
                S.dma('pool', out=E_all[:, e0:e0 + 4, :], in_=Eall_d[:, e0:e0 + 4, :])
            cov = sbx(stS, "cov_s", [128, 8, 257], BF16)
            for g0 in range(0, 8, 4):
                S.dma('pool', out=cov[:, g0:g0 + 4, :], in_=covs_d[:, g0:g0 + 4, :])
            selcs = sbx(stS, "selcs", [16, 3, 257])
            S.dma('sp', out=selcs, in_=selcs_d)
            msm = sbx(stS, "msm", [128, 16])
            msmb = sbx(stS, "msmb", [128, 16], BF16)
            S.dma('sp', out=msm, in_=msm_d)
            S.dve.tensor_copy(out=msmb, in_=msm)
            r8 = sbx(stS, "r8", [8, 32])
            gsel = sbx(stS, "gsel", [32, 2, 24])
            s16 = sbx(stS, "s16", [32, 2, 16])
            S.dma('sp', out=r8, in_=r8_d)
            S.dma('sp', out=gsel, in_=gsel_d)
            S.dma('sp', out=s16, in_=s16_d)
            iof = sbx(stS, "iof", [128, 1])
            S.pool.iota(iof, pattern=[[0, 1]], base=0, channel_multiplier=2, allow_small_or_imprecise_dtypes=True)
            pt_i = sbx(stS, "pt_i", [128, NPG], I32)
            pt_f = sbx(stS, "pt_f", [128, NPG])
            idxc = sbx(stS, "idxc", [128, NPG], I32)
            idxs = sbx(stS, "idxs", [128, NPG], I32)
            xg = [sbx(stS, "xg", [128, 4, 16 + 2048], BF16) for _ in range(2)]
            NPB = 24
            pg = [sbx(stS, "pg", [128, 512], BF16) for _ in range(NPB)]
            ksT = [sbx(stS, "ksT", [128, 2, 128], BF16) for _ in range(8)]
            xs = [sbx(stS, "xs", [128, 16, 129], BF16) for _ in range(2)]
            hx2 = [sbx(stS, "hx2", [128, 256]) for _ in range(2)]
            hu2 = [sbx(stS, "hu2", [128, 256]) for _ in range(2)]
            hb2 = [sbx(stS, "hb2", [128, 256], BF16) for _ in range(2)]
            hTs = [sbx(stS, "hTs", [128, 2, 128], BF16) for _ in range(2)]
            posrep = sbx(stS, "posrep", [128, 32, 128], BF16)
            hposB = sbx(stS, "hposB", [128, 2, 256])
            kcT = sbx(stS, "kcTs", [128, NKV, NG * 128], BF16)
            vc = sbx(stS, "vcs", [128, NG, NKV, 128], BF16)
            qs = sbx(stS, "qs", [128, NH * DEC], BF16)
            knew = {k: sbx(stS, "knew" + k, [128, NKV, DEC], BF16) for k in ("slc", "win")}
            vnew = {k: sbx(stS, "vnew" + k, [DEC, NKV * HD], BF16) for k in ("slc", "win")}
            g8 = sbx(stS, "g8", [DEC, 24])
            es_c = sbx(stS, "es_c", [128, 512], BF16)
            es_s = [sbx(stS, "es_s", [128, 256], BF16) for _ in range(2)]
            pt_s = [sbx(stS, "pt_s", [128, 256], BF16) for _ in range(2)]
            mk = [sbx(stS, "mks", [128, 64], BF16) for _ in range(2)]
            es_n = sbx(stS, "es_n", [DEC, 64], BF16)
            pt_n = sbx(stS, "pt_n", [DEC, 64], BF16)
            impu = [sbx(stS, "impu", [32, 257]) for _ in range(2)]
            wsum = [sbx(stS, "wsum", [32, 16]) for _ in range(2)]
            prio = sbx(stS, "prios", [16, 257])
            rank = sbx(stS, "ranks", [16, 257])
            cmpm = sbx(stS, "cmpms", [16, 16, 257])
            selN = sbx(stS, "selNs", [128, 3, 16], BF16)
            cw_f = sbx(stS, "cw_f", [128, 4, 512])
            cw_b = sbx(stS, "cw_b", [128, 4, 512], BF16)
            dens = sbx(stS, "denss", [32, 6])
            gm = sbx(stS, "gm", [32, 24])
            gate = sbx(stS, "gate", [32, 2, 3])
            coef = sbx(stS, "coefs", [32, 2, 3])
            ob = sbx(stS, "obs", [32, 128])
            mixs = sbx(stS, "mixs", [32, NKV, 128], BF16)
            pT_ = [psx(stS, "sT", [128, 1024], BF16) for _ in range(2)]
            pH = [psx(stS, "sH", [128, 512]) for _ in range(2)]
            pX = psx(stS, "sX", [128, 512])
            pM = psx(stS, "sM", [128, 512])
            pOa = psx(stS, "sOa", [128, 512])
            pOb = psx(stS, "sOb", [128, 512])
            for kv in range(2):
                S.dve.tensor_copy(out=posrep, in_=posb[:, kv, :].unsqueeze(2).to_broadcast([128, 32, 128]))
                for p_ in range(32):
                    S.pe.matmul(pX[:, 0:256], lhsT=posrep[:, p_, :], rhs=w1[kv][:, p_, :], start=(p_ == 0), stop=(p_ == 31))
                S.act.copy(out=hposB[:, kv, :], in_=pX[:, 0:256])
            rows_v = cache_d.rearrange("n r (k2 k1) h d -> (n r k2) (k1 h d)", k2=2)
            tcnt = [0]

            for sq_ in range(cfg.n_sseq):
                tb = cfg.ntp + sq_ * DEC
                started = set()

                def st_flag(bank):
                    if bank in started:
                        return False
                    started.add(bank)
                    return True

                S.dma('sp', out=pt_i, in_=ptab_d[sq_:sq_ + 1, :].to_broadcast([128, NPG]))
                S.dve.tensor_copy(out=pt_f, in_=pt_i)
                S.dve.tensor_scalar(out=pt_f, in0=pt_f, scalar1=256.0, scalar2=iof[:, 0:1], op0=ALU.mult, op1=ALU.add)
                S.dve.tensor_copy(out=idxc, in_=pt_f)
                S.dve.tensor_scalar(out=pt_f, in0=pt_f, scalar1=1.0, scalar2=None, op0=ALU.add)
                S.dve.tensor_copy(out=idxs, in_=pt_f)
                S.dma('sp', out=qs.rearrange("p (h i) -> p h i", h=NH), in_=nqT[:, :, tb:tb + DEC])
                S.dma('sp', out=knew["slc"], in_=kslcT[:, :, tb:tb + DEC])
                S.dma('sp', out=knew["win"], in_=kwinT[:, :, tb:tb + DEC])
                S.dma('sp', out=vnew["slc"], in_=vslc_tok[tb:tb + DEC, :])
                S.dma('sp', out=vnew["win"], in_=vwin_tok[tb:tb + DEC, :])
                S.dma('sp', out=g8, in_=gates_d[tb:tb + DEC, :])
                S.dma('sp', out=cw_f, in_=cwin_d[sq_].rearrange("(t p) n -> p t n", p=128))
                S.dve.tensor_copy(out=cw_b, in_=cw_f)

                for grp in range(NG):
                    xgc = xg[grp % 2]
                    if grp == 0:
                        S.dve.memset(xgc[:, :, 0:16], 0.0)
                    if grp > 0:
                        S.dve.tensor_copy(out=xgc[:, :, 0:16], in_=xg[(grp - 1) % 2][:, :, 2048:2064])
                    for pl in range(16):
                        p = grp * 16 + pl
                        pgt = pg[p % NPB]
                        S.pool.indirect_dma_start(out=pgt, out_offset=None, in_=rows_v,
                                                  in_offset=bass.IndirectOffsetOnAxis(ap=idxc[:, p:p + 1], axis=0),
                                                  R=[idxc.name], W=[pgt.name])
                        pt_ = pT_[tcnt[0] % 2]
                        tcnt[0] += 1
                        for k in range(4):
                            S.pe.transpose(pt_[:, k * 128:(k + 1) * 128], pgt[:, k * 128:(k + 1) * 128], identb)
                        S.act.copy(out=xgc[:, :, 16 + pl * 128:16 + (pl + 1) * 128],
                                   in_=pt_[:, 0:512].rearrange("p (k n) -> p k n", k=4))
                    for kh in range(4):
                        kv, h = kh // 2, kh % 2
                        xsb = xs[kh % 2]
                        cp_eng = S.pool if kh % 2 == 0 else S.act
                        if kh % 2 == 0:
                            S.dve.tensor_copy(out=xsb, in_=xgc[:, kh, :].rearrange("p (n s) -> p s n", s=16))
                        else:
                            S.act.copy(out=xsb, in_=xgc[:, kh, :].rearrange("p (n s) -> p s n", s=16))
                        ps_ = pH[kh % 2]
                        i = 0
                        for r in range(2):
                            for s_ in range(16):
                                S.pe.matmul(ps_[:, :256], lhsT=xsb[:, s_, r:r + 128], rhs=w1[kv][:, r * 16 + s_, :],
                                            start=(i == 0), stop=(i == 31))
                                i += 1
                        x_, u_ = hx2[kh % 2], hu2[kh % 2]
                        S.dve.tensor_tensor(out=x_, in0=ps_[:, :256], in1=hposB[:, kv, :], op=ALU.add)
                        S.dve.tensor_tensor(out=u_, in0=x_, in1=x_, op=ALU.mult)
                        S.dve.tensor_scalar(out=u_, in0=u_, scalar1=0.044715, scalar2=1.0, op0=ALU.mult, op1=ALU.add)
                        S.dve.tensor_tensor(out=u_, in0=u_, in1=x_, op=ALU.mult)
                        S.act.activation(out=u_, in_=u_, func=AF.Tanh, scale=0.7978845608028654)
                        S.dve.tensor_scalar(out=u_, in0=u_, scalar1=1.0, scalar2=0.5, op0=ALU.add, op1=ALU.mult)
                        S.dve.tensor_tensor(out=hb2[kh % 2], in0=u_, in1=x_, op=ALU.mult)
                        pt_ = pT_[tcnt[0] % 2]
                        tcnt[0] += 1
                        for fcn in range(2):
                            S.pe.transpose(pt_[:, fcn * 128:(fcn + 1) * 128], hb2[kh % 2][:, fcn * 128:(fcn + 1) * 128], identb)
                        hT2 = hTs[kh % 2]
                        S.act.copy(out=hT2, in_=pt_[:, 0:256].rearrange("p (k n) -> p k n", k=2))
                        if kv == 0:
                            for fcn in range(2):
                                S.pe.matmul(pX[:, :128], lhsT=w2[0][:, fcn, :], rhs=hT2[:, fcn, :], start=(fcn == 0), stop=(fcn == 1))
                            S.act.copy(out=kcT[:, h, grp * 128:(grp + 1) * 128], in_=pX[:, :128])
                        else:
                            for fcn in range(2):
                                S.pe.matmul(pX[:, :128], lhsT=hT2[:, fcn, :], rhs=w2[1][:, fcn, :], start=(fcn == 0), stop=(fcn == 1))
                            S.act.copy(out=vc[:, grp, h, :], in_=pX[:, :128])

                first = True
                for g in range(NG):
                    for h in range(NKV):
                        c0 = (g * 2 + h) * 32
                        S.pe.matmul(pX[:, c0:c0 + 32], lhsT=kcT[:, h, g * 128:(g + 1) * 128], rhs=qs[:, h * 32:(h + 1) * 32],
                                    start=first, stop=True)
                        first = False
                S.act.activation(out=es_c, in_=pX, func=AF.Exp, scale=SCALE)
                S.dve.memset(es_c[0:1, 0:64], 0.0)
                for g in range(NG):
                    for h in range(NKV):
                        l_ = es_c[:, (g * 2 + h) * 32:(g * 2 + h) * 32 + 32]
                        S.pe.matmul(pOa[:32, h * 128:(h + 1) * 128], lhsT=l_, rhs=vc[:, g, h, :], start=st_flag("pOa"), stop=(g == NG - 1))
                        S.pe.matmul(pOb[:32, 256 + h:257 + h], lhsT=l_, rhs=ones_b[:, 0:1], start=st_flag("pOb"), stop=(g == NG - 1))
                        S.pe.matmul(pH[h][:32, :257], lhsT=l_, rhs=cov[:, g, :], start=(g == 0), stop=(g == NG - 1))
                S.dve.tensor_scalar(out=dens[:, 0:2], in0=pOb[:32, 256:258], scalar1=1e-30, scalar2=None, op0=ALU.max)
                S.dve.reciprocal(out=dens[:, 0:2], in_=dens[:, 0:2])
                for h in range(NKV):
                    S.act.copy(out=impu[h], in_=pH[h][:32, :257])
                    S.dve.tensor_scalar(out=wsum[h], in0=s16[:, h, :], scalar1=dens[:, h:h + 1], scalar2=None, op0=ALU.mult)
                for h in range(NKV):
                    S.pe.matmul(pM[:16, :257], lhsT=wsum[h], rhs=impu[h], start=(h == 0), stop=(h == 1))
                S.dve.tensor_tensor(out=prio, in0=pM[:16, :257], in1=selcs[:, 0, :], op=ALU.mult)
                S.dve.tensor_tensor(out=prio, in0=prio, in1=selcs[:, 1, :], op=ALU.add)
                for n0 in range(0, 257, 16):
                    nn = min(16, 257 - n0)
                    S.dve.tensor_tensor(out=cmpm[:, :nn, :], in0=prio.unsqueeze(1).to_broadcast([16, nn, 257]),
                                        in1=prio[:, n0:n0 + nn].unsqueeze(2).to_broadcast([16, nn, 257]), op=ALU.is_gt)
                    S.dve.reduce_sum(out=rank[:, n0:n0 + nn], in_=cmpm[:, :nn, :], axis=AX.X)
                S.dve.tensor_scalar(out=rank, in0=rank, scalar1=15.5, scalar2=None, op0=ALU.is_lt)
                S.dve.tensor_tensor(out=rank, in0=rank, in1=selcs[:, 2, :], op=ALU.mult)
                for t_ in range(3):
                    nb = 128 if t_ < 2 else 1
                    S.pe.transpose(pM[:nb, 272 + t_ * 16:272 + (t_ + 1) * 16], rank[:, t_ * 128:t_ * 128 + nb], ident[:16, :16])
                S.dve.memset(selN, 0.0)
                S.dve.tensor_copy(out=selN[:, 0:2, :], in_=pM[:, 272:304].rearrange("p (t n) -> p t n", t=2))
                S.dve.tensor_copy(out=selN[0:1, 2, :], in_=pM[0:1, 304:320])

                for p4 in range(NPG // 4):
                    q2 = p4 % 2
                    for pl in range(4):
                        p = p4 * 4 + pl
                        pgt, kst = pg[p % NPB], ksT[p % 8]
                        S.pool.indirect_dma_start(out=pgt, out_offset=None, in_=rows_v,
                                                  in_offset=bass.IndirectOffsetOnAxis(ap=idxs[:, p:p + 1], axis=0),
                                                  R=[idxs.name], W=[pgt.name])
                        pt_ = pT_[tcnt[0] % 2]
                        tcnt[0] += 1
                        for k in range(2):
                            S.pe.transpose(pt_[:, k * 128:(k + 1) * 128], pgt[:, k * 128:(k + 1) * 128], identb)
                        S.act.copy(out=kst, in_=pt_[:, 0:256].rearrange("p (k n) -> p k n", k=2))
                    for pl in range(4):
                        p = p4 * 4 + pl
                        for h in range(NKV):
                            c0 = (pl * 2 + h) * 32
                            S.pe.matmul(pX[:, c0:c0 + 32], lhsT=ksT[p % 8][:, h, :], rhs=qs[:, h * 32:(h + 1) * 32],
                                        start=(pl == 0 and h == 0), stop=True)
                        S.pe.matmul(pM[:, pl * 16:(pl + 1) * 16], lhsT=E_all[:, p % 64, :], rhs=selN[:, p // 64, :],
                                    start=(pl == 0), stop=True)
                    S.act.activation(out=es_s[q2], in_=pX[:, 0:256], func=AF.Exp, scale=SCALE)
                    S.dve.tensor_copy(out=mk[q2], in_=pM[:, 0:64])
                    S.dve.tensor_tensor(out=pt_s[q2].rearrange("p (a g i) -> p a g i", a=8, g=4),
                                        in0=es_s[q2].rearrange("p (a g i) -> p a g i", a=8, g=4),
                                        in1=mk[q2].rearrange("p (a i) -> p a i", a=8).unsqueeze(2).to_broadcast([128, 8, 4, 8]),
                                        op=ALU.mult)
                    for pl in range(4):
                        p = p4 * 4 + pl
                        for h in range(NKV):
                            l_ = pt_s[q2][:, (pl * 2 + h) * 32:(pl * 2 + h) * 32 + 32]
                            S.pe.matmul(pOa[:32, 256 + h * 128:256 + (h + 1) * 128], lhsT=l_, rhs=pg[p % NPB][:, 256 + h * 128:256 + (h + 1) * 128],
                                        start=st_flag("pOa"), stop=False)
                            S.pe.matmul(pOb[:32, 258 + h:259 + h], lhsT=l_, rhs=ones_b[:, 0:1], start=st_flag("pOb"), stop=False)

                def new_rows(kind, po_ap_fn, den_col):
                    for h in range(NKV):
                        S.pe.matmul(pX[:DEC, h * 32:(h + 1) * 32], lhsT=knew[kind][:, h, :], rhs=qs[:, h * 32:(h + 1) * 32],
                                    start=(h == 0), stop=True)
                    S.act.activation(out=es_n, in_=pX[:DEC, 0:64], func=AF.Exp, scale=SCALE)
                    S.dve.tensor_tensor(out=pt_n.rearrange("p (a i) -> p a i", a=8), in0=es_n.rearrange("p (a i) -> p a i", a=8),
                                        in1=msmb[:DEC, 8:16].unsqueeze(1).to_broadcast([DEC, 8, 8]), op=ALU.mult)
                    for h in range(NKV):
                        l_ = pt_n[:, h * 32:(h + 1) * 32]
                        S.pe.matmul(po_ap_fn(h), lhsT=l_, rhs=vnew[kind][:, h * 128:(h + 1) * 128], start=False, stop=True)
                        S.pe.matmul(pOb[:32, den_col + h:den_col + h + 1], lhsT=l_, rhs=ones_b[:DEC, 0:1], start=False, stop=True)

                new_rows("slc", lambda h: pOa[:32, 256 + h * 128:256 + (h + 1) * 128], 258)

                for t_ in range(4):
                    pt_ = pT_[tcnt[0] % 2]
                    tcnt[0] += 1
                    for k in range(2):
                        S.pe.transpose(pt_[:, k * 128:(k + 1) * 128], cw_b[:, t_, k * 128:(k + 1) * 128], identb)
                    S.act.copy(out=ksT[t_], in_=pt_[:, 0:256].rearrange("p (k n) -> p k n", k=2))
                for t_ in range(4):
                    for h in range(NKV):
                        c0 = (t_ * 2 + h) * 32
                        S.pe.matmul(pX[:, c0:c0 + 32], lhsT=ksT[t_][:, h, :], rhs=qs[:, h * 32:(h + 1) * 32],
                                    start=(t_ == 0 and h == 0), stop=True)
                S.act.activation(out=es_s[0], in_=pX[:, 0:256], func=AF.Exp, scale=SCALE)
                S.dve.tensor_tensor(out=es_s[0][:, 0:64].rearrange("p (a i) -> p a i", a=8),
                                    in0=es_s[0][:, 0:64].rearrange("p (a i) -> p a i", a=8),
                                    in1=msmb[:, 0:8].unsqueeze(1).to_broadcast([128, 8, 8]), op=ALU.mult)
                for t_ in range(4):
                    for h in range(NKV):
                        l_ = es_s[0][:, (t_ * 2 + h) * 32:(t_ * 2 + h) * 32 + 32]
                        S.pe.matmul(pOb[:32, h * 128:(h + 1) * 128], lhsT=l_, rhs=cw_b[:, t_, 256 + h * 128:256 + (h + 1) * 128],
                                    start=st_flag("pOb"), stop=False)
                        S.pe.matmul(pOb[:32, 260 + h:261 + h], lhsT=l_, rhs=ones_b[:, 0:1], start=False, stop=False)
                new_rows("win", lambda h: pOb[:32, h * 128:(h + 1) * 128], 260)

                S.dve.tensor_scalar(out=dens, in0=pOb[:32, 256:262], scalar1=1e-30, scalar2=None, op0=ALU.max)
                S.dve.reciprocal(out=dens, in_=dens)
                S.pe.matmul(pM[:32, 0:24], lhsT=r8, rhs=g8, start=True, stop=True)
                for h in range(NKV):
                    S.dve.tensor_tensor(out=gm, in0=pM[:32, 0:24], in1=gsel[:, h, :], op=ALU.mult)
                    S.dve.reduce_sum(out=gate[:, h, :], in_=gm.rearrange("p (b e) -> p b e", b=3), axis=AX.X)
                    for br in range(3):
                        S.dve.tensor_tensor(out=coef[:, h, br:br + 1], in0=gate[:, h, br:br + 1], in1=dens[:, br * 2 + h:br * 2 + h + 1], op=ALU.mult)
                    S.dve.tensor_scalar(out=ob, in0=pOa[:32, h * 128:(h + 1) * 128], scalar1=coef[:, h, 0:1], scalar2=None, op0=ALU.mult)
                    S.dve.scalar_tensor_tensor(out=ob, in0=pOa[:32, 256 + h * 128:256 + (h + 1) * 128], scalar=coef[:, h, 1:2], in1=ob,
                                               op0=ALU.mult, op1=ALU.add)
                    S.dve.scalar_tensor_tensor(out=mixs[:, h, :], in0=pOb[:32, h * 128:(h + 1) * 128], scalar=coef[:, h, 2:3], in1=ob,
                                               op0=ALU.mult, op1=ALU.add)
                mv = mixed[tb:tb + DEC, RH * RDV:D].rearrange("i (h g d) -> i h g d", h=NKV, g=4)
                for g in range(4):
                    S.dma('sp', out=mv[:, :, g, :], in_=mixs[g * DEC:(g + 1) * DEC, :, :])
            S.flush()

    if cfg.do_mixer:
        retention_phase()
        nsa_prompt_phase()
        if cfg.do_sample and cfg.n_sseq:
            nsa_sample_phase()

    with ExitStack() as stD:
        b = dense_bufs(stD)
        mtok = [sbx(stD, "mtok", [128, 1024], BF16) for _ in range(2)]
        for (t0, TW) in cfg.tiles:
            nsub = (TW + 127) // 128
            for s in range(nsub):
                rows = min(128, TW - s * 128)
                for hf in range(2):
                    mt = mtok[hf]
                    S.dma('sp', out=mt[:rows, :], in_=mixed[t0 + s * 128:t0 + s * 128 + rows, hf * 1024:(hf + 1) * 1024])
                    pt = b.psT[hf]
                    for k in range(8):
                        S.pe.transpose(pt[:, k * 128:k * 128 + rows], mt[:rows, k * 128:(k + 1) * 128], identb[:rows, :rows])
                    S.act.copy(out=b.gT[:, hf * 8:(hf + 1) * 8, s * 128:s * 128 + rows],
                               in_=pt.rearrange("p (k n) -> p k n", k=8)[:, :, :rows])
            S.dma('sp', out=b.xf[:, :, :TW], in_=x1T[:, :, t0:t0 + TW])
            for g8 in range(8):
                wbk = b.wa[g8 % 2]
                S.dma('sp', out=wbk, in_=wout_b[g8], nosplit=True)
                for k in range(2):
                    dc = g8 * 2 + k
                    pa = b.psA[dc % 2]
                    for kc in range(KC):
                        S.pe.matmul(pa[:, :TW], lhsT=wbk[:, kc, k * 128:(k + 1) * 128], rhs=b.gT[:, kc, :TW],
                                    start=(kc == 0), stop=(kc == KC - 1))
                    S.dve.tensor_tensor(out=b.xf[:, dc, :TW], in0=b.xf[:, dc, :TW], in1=pa[:, :TW], op=ALU.add)
            layer_norm(b, 1, TW, True)
            ffn(b, 1, TW)
            layer_norm(b, 2, TW, False)
            for s in range(nsub):
                rows = min(128, TW - s * 128)
                for hf in range(2):
                    yt = b.xtok[hf]
                    for c4 in range(2):
                        pa = b.psA[c4 % 2]
                        for k in range(4):
                            c = hf * 8 + c4 * 4 + k
                            S.pe.transpose(pa[:rows, k * 128:(k + 1) * 128], b.xf[:, c, s * 128:s * 128 + rows], ident)
                        S.act.copy(out=yt[:rows, c4 * 512:(c4 + 1) * 512], in_=pa[:rows, :])
                    S.dma('sp', out=y_out[t0 + s * 128:t0 + s * 128 + rows, hf * 1024:(hf + 1) * 1024], in_=yt[:rows, :])
        S.flush()
    n = S.finish()
    return nc, n


def _rope_table(positions):
    half = 64
    inv = (np.float32(10000.0) ** (-np.arange(half, dtype=np.float32) / np.float32(half))).astype(np.float32)
    ang = positions.astype(np.float32)[:, None] * inv[None, :]
    return np.concatenate([np.cos(ang), np.sin(ang)], axis=1).astype(np.float32)


def _lnp(ins):
    rows = [ins['ln1_g'][0], ins['ln1_b'][0], ins['ln2_g'][0], ins['ln2_b'][0], ins['ln3_g'][0], ins['ln3_b'][0]]
    a = np.stack([np.asarray(r).reshape(KC, 128).T for r in rows], axis=1)
    return np.ascontiguousarray(a.astype(np.float32))


def _ret_consts(C):
    out = np.zeros((128, 3, RH, 128), np.float32)
    i = np.arange(C, dtype=np.float32)
    for h in range(RH):
        lg = np.float32(LG[h])
        diff = i[None, :] - i[:, None]
        out[:C, 0, h, :C] = np.where(diff >= 0, np.exp(lg * np.maximum(diff, 0.0)), 0.0)
        out[:, 1, h, :C] = np.exp(lg * (i + 1.0))[None, :]
        out[:C, 2, h, 0] = np.exp(lg * (C - 1.0 - i))
    return out


def _nsa_consts(seq):
    nqt, ncmp, nslc = seq // 128, seq // 16 - 1, seq // 64
    t = np.arange(seq)
    c = np.arange(128)
    mcmp = ((c[:, None] * 16 + 31 <= t[None, :]) & (c[:, None] < ncmp)).astype(np.float32)
    ci = np.arange(128)[:, None]
    ni = np.arange(32)[None, :]
    cover = np.clip(np.minimum(ci * 16 + 32, (ni + 1) * 64) - np.maximum(ci * 16, ni * 64), 0, None).astype(np.float32) / 32.0
    cover[ncmp:, :] = 0
    cover[:, nslc:] = 0
    j = np.arange(128)[:, None]
    i = np.arange(128)[None, :]
    mtri = np.stack([(j <= i), (j > i)], axis=1).astype(np.float32)
    selc = np.zeros((128, nqt, 3, 32), np.float32)
    n = np.arange(32)
    for qt in range(nqt):
        for p in range(128):
            tt = qt * 128 + p
            cur = tt // 64
            valid = (n <= cur) & (n < nslc)
            forced = (n == 0) | (n == cur) | (n == cur - 1)
            selc[p, qt, 0] = (valid & ~forced).astype(np.float32)
            bonus = np.where(forced & valid, 3e30 - n * 1e28, np.where(valid, 0.0, -1e30 - n * 1e27))
            selc[p, qt, 1] = bonus.astype(np.float32)
            selc[p, qt, 2] = valid.astype(np.float32)
    E = np.zeros((32, nqt, 128), np.float32)
    for kt in range(nqt):
        for p in range(128):
            E[2 * kt + p // 64, kt, p] = 1.0
    return mcmp, cover, mtri, selc, E


def _nsa_sample_consts():
    nslc, ncmp = 257, 1023
    cover = np.zeros((128, 8, 257), np.float32)
    n = np.arange(257)
    for g in range(8):
        for j in range(128):
            c = 128 * g - 1 + j
            if c < 0 or c >= ncmp:
                continue
            cover[j, g] = np.clip(np.minimum(c * 16 + 32, (n + 1) * 64) - np.maximum(c * 16, n * 64), 0, None) / 32.0
    selc = np.zeros((16, 3, 257), np.float32)
    for h in range(2):
        for i in range(DEC):
            t = PAST + i
            cur = t // 64
            valid = n <= cur
            forced = (n == 0) | (n == cur) | (n == cur - 1)
            selc[h * 8 + i, 0] = (valid & ~forced)
            selc[h * 8 + i, 1] = np.where(forced & valid, 3e30 - n * 1e27, np.where(valid, 0.0, -1e30 - n * 1e26))
            selc[h * 8 + i, 2] = valid
    E = np.zeros((128, 64, 128), np.float32)
    for e in range(64):
        for key in range(128):
            E[2 * e + key // 64, e, key] = 1.0
    msm = np.zeros((128, 16), np.float32)
    j = np.arange(128)[:, None]
    i = np.arange(8)[None, :]
    msm[:, 0:8] = (j >= i + 1)
    msm[:8, 8:16] = (np.arange(8)[:, None] <= i)
    r8 = np.zeros((8, 32), np.float32)
    gsel = np.zeros((32, 2, 24), np.float32)
    s16 = np.zeros((32, 2, 16), np.float32)
    for g in range(4):
        for ii in range(8):
            r8[ii, g * 8 + ii] = 1.0
            for h in range(2):
                s16[g * 8 + ii, h, h * 8 + ii] = 1.0
                for br in range(3):
                    gsel[g * 8 + ii, h, br * 8 + 4 * h + g] = 1.0
    return cover, selc, E, msm, r8, gsel, s16


def _perm_gate_cols(w_in):
    w = np.array(w_in, copy=True)
    g0 = NIN - 24
    idx = np.array([h * 3 + br for br in range(3) for h in range(NH)])
    w[:, g0:] = w_in[:, g0 + idx]
    return w


def make_in_maps(ins, cfg, n_cores, pseq_per_core=2, sseq_per_core=4):
    seq = cfg.seq
    pos = np.concatenate([np.tile(np.arange(seq), cfg.n_pseq), np.tile(PAST + np.arange(DEC), cfg.n_sseq)])
    rope = _rope_table(pos)
    lnp = _lnp(ins)
    ident = np.eye(128, dtype=np.float32)
    gn = np.ascontiguousarray(np.broadcast_to(np.stack([ins['ret_gn_g'][0], ins['ret_gn_b'][0]])[None], (128, 2, RH * RDV))).astype(np.float32)
    rc128, rc8 = _ret_consts(128), _ret_consts(8)
    mcmp, cover, mtri, selc, E = _nsa_consts(seq)
    posT = np.ascontiguousarray(np.stack([np.asarray(ins['cmp_pos_k'][0]).T, np.asarray(ins['cmp_pos_v'][0]).T], axis=1)).astype(np.float32)
    w_in_p = _perm_gate_cols(np.asarray(ins['w_in'][0]))
    cov_s, selc_s, E_all, msm, r8, gsel, s16 = _nsa_sample_consts()
    cache = np.asarray(ins['cache_nsa_kv'][0])
    ptab = np.asarray(ins['page_table']).astype(np.int32)
    in_maps = []
    for c in range(n_cores):
        xp = np.asarray(ins['x_prompt'][cfg.n_pseq * c:cfg.n_pseq * (c + 1)]).reshape(-1, D)
        xs = np.asarray(ins['x_sample'][cfg.n_sseq * c:cfg.n_sseq * (c + 1)]).reshape(-1, D)
        m = {
            'xin': np.ascontiguousarray(np.concatenate([xp, xs], 0)),
            'ffn1_w_up': ins['ffn1_w_up'][0], 'ffn2_w_up': ins['ffn2_w_up'][0],
            'ffn1_w_down': ins['ffn1_w_down'][0], 'ffn2_w_down': ins['ffn2_w_down'][0],
            'w_in': w_in_p, 'w_out': ins['w_out'][0],
            'lnp': lnp, 'ident': ident, 'rope': rope, 'gn': gn, 'rc128': rc128, 'rc8': rc8,
            'cmp_w1_k': ins['cmp_w1_k'][0], 'cmp_w1_v': ins['cmp_w1_v'][0],
            'cmp_w2_k': ins['cmp_w2_k'][0], 'cmp_w2_v': ins['cmp_w2_v'][0],
            'cmp_posT': posT, 'mcmp': mcmp, 'cover': cover, 'mtri': mtri, 'selc': selc, 'Eexp': E,
            'state': np.ascontiguousarray(ins['state_ret'][0][cfg.n_sseq * c:cfg.n_sseq * (c + 1)]),
            'cache': cache, 'ptab': np.ascontiguousarray(ptab[cfg.n_sseq * c:cfg.n_sseq * (c + 1)]),
            'cover_s': cov_s, 'selc_s': selc_s, 'E_all': E_all, 'msmall': msm, 'r8': r8, 'gsel': gsel, 'sel16': s16,
            'cwin': np.ascontiguousarray(np.asarray(ins['cache_win'][0][cfg.n_sseq * c:cfg.n_sseq * (c + 1)]).reshape(cfg.n_sseq, WINDOW, 512)),
        }
        in_maps.append(m)
    return in_maps


def kernel(**ins):
    cfg = Cfg()
    nc, _ = build_program(cfg)
    in_maps = make_in_maps(ins, cfg, N_CORES)
    res = run_bass_kernel_spmd(nc, in_maps, core_ids=list(range(N_CORES)))
    R = res.results
    ntp = cfg.ntp
    f = np.float32
    y_p = np.stack([R[c]['y'][:ntp].reshape(2, SEQ, D) for c in range(N_CORES)]).reshape(NB_P, SEQ, D).astype(f)
    y_s = np.stack([R[c]['y'][ntp:].reshape(4, DEC, D) for c in range(N_CORES)]).reshape(NB_S, DEC, D).astype(f)
    kv_p = np.stack([R[c]['kvrows'][:ntp].reshape(2, SEQ, 4, NKV, HD) for c in range(N_CORES)]).reshape(1, NB_P, SEQ, 4, NKV, HD).astype(f)
    kv_s = np.stack([R[c]['kvrows'][ntp:].reshape(4, DEC, 4, NKV, HD) for c in range(N_CORES)]).reshape(1, NB_S, DEC, 4, NKV, HD).astype(f)
    win_p = np.stack([R[c]['winp'].reshape(2, WINDOW, 2, NKV, HD) for c in range(N_CORES)]).reshape(1, NB_P, WINDOW, 2, NKV, HD).astype(f)
    win_s = np.stack([R[c]['wins'].reshape(4, WINDOW, 2, NKV, HD) for c in range(N_CORES)]).reshape(1, NB_S, WINDOW, 2, NKV, HD).astype(f)
    rs_p = np.stack([R[c]['rs'][:2] for c in range(N_CORES)]).reshape(1, NB_P, RH, RDK, RDV).astype(f)
    rs_s = np.stack([R[c]['rs'][2:] for c in range(N_CORES)]).reshape(1, NB_S, RH, RDK, RDV).astype(f)
    return (y_p, y_s, rs_p, rs_s, kv_p, kv_s, win_p, win_s)
```
